# Optimizing a Trainium2 kernel written in Bass

```python
import jax
import jax.numpy as jnp
from jax import lax
import numpy as np


D_MODEL = 2048
BATCH = 8
SEQ = 4096
DEPTH = 4

GRID_W = 64
CTX_LEN = 256
N_MIXERS = 2
MLA_HEADS = 16
MLA_Q_RANK = 512
MLA_KV_RANK = 512
MLA_NOPE = 128
MLA_ROPE = 64
MLA_V = 128
RET_HEADS = 8
RET_DK = D_MODEL // RET_HEADS
RET_DV = 2 * D_MODEL // RET_HEADS
RET_CHUNK = 128
FFN_DIM = 256 * ((8 * D_MODEL // 3 + 255) // 256)
CONV_W = 3
N_MOD = 6
ROPE_BASE = 10000.0
NORM_EPS = 1e-6
GN_EPS = 1e-5
Q_BLOCK = 128

kernel_name = "hybrid_mla_retention_dit"


def rmsnorm(x, g):
    x32 = x.astype(jnp.float32)
    y = x32 * lax.rsqrt(jnp.mean(x32 * x32, axis=-1, keepdims=True) + NORM_EPS)
    return (y * g.astype(jnp.float32)).astype(x.dtype)


def rope(x, ang):
    m = x.shape[-1] // 2
    cos = jnp.cos(ang).astype(x.dtype)
    sin = jnp.sin(ang).astype(x.dtype)
    x1, x2 = x[..., :m], x[..., m:]
    return jnp.concatenate([x1 * cos - x2 * sin, x2 * cos + x1 * sin], axis=-1)


def axial_rope(x, ang_row, ang_col):
    half = x.shape[-1] // 2
    return jnp.concatenate([rope(x[..., :half], ang_row), rope(x[..., half:], ang_col)], axis=-1)


def merge_heads(y):
    B, H, L, d = y.shape
    return y.transpose(0, 2, 1, 3).reshape(B, L, H * d)


def attend(q, k, v):
    s = jnp.einsum('bhqd,bhkd->bhqk', q, k, preferred_element_type=jnp.float32) * (q.shape[-1] ** -0.5)
    p = jax.nn.softmax(s, axis=-1)
    return jnp.einsum('bhqk,bhkd->bhqd', p.astype(v.dtype), v)


def blocked_attend(q, k, v):
    B, H, L, dq = q.shape
    nb = L // Q_BLOCK
    qb = q.reshape(B, H, nb, Q_BLOCK, dq).transpose(2, 0, 1, 3, 4)
    ob = lax.map(lambda qi: attend(qi, k, v), qb)
    return ob.transpose(1, 2, 0, 3, 4).reshape(B, H, L, v.shape[-1])


def mla_q(h, w_dq, q_g, w_uq, ang_row, ang_col):
    B, L, _ = h.shape
    q = (rmsnorm(h @ w_dq, q_g) @ w_uq).reshape(B, L, MLA_HEADS, MLA_NOPE + MLA_ROPE).transpose(0, 2, 1, 3)
    q_nope, q_rope = q[..., :MLA_NOPE], q[..., MLA_NOPE:]
    if ang_row is not None:
        q_rope = axial_rope(q_rope, ang_row, ang_col)
    return jnp.concatenate([q_nope, q_rope], axis=-1)


def mla_kv(h, w_dkv, kv_g, w_ukv, ang_row, ang_col):
    B, L, _ = h.shape
    kv_in = h @ w_dkv
    c_kv = rmsnorm(kv_in[..., :MLA_KV_RANK], kv_g)
    k_rope = kv_in[..., MLA_KV_RANK:][:, None]
    if ang_row is not None:
        k_rope = axial_rope(k_rope, ang_row, ang_col)
    kv = (c_kv @ w_ukv).reshape(B, L, MLA_HEADS, MLA_NOPE + MLA_V).transpose(0, 2, 1, 3)
    k = jnp.concatenate([kv[..., :MLA_NOPE], jnp.broadcast_to(k_rope, (B, MLA_HEADS, L, MLA_ROPE))], axis=-1)
    return k, kv[..., MLA_NOPE:]


def mla_mixer(hx, hc, last, ang_row, ang_col, w_dq, q_g, w_uq, w_dkv, kv_g, w_ukv, w_o):
    qx = mla_q(hx, w_dq, q_g, w_uq, ang_row, ang_col)
    kx, vx = mla_kv(hx, w_dkv, kv_g, w_ukv, ang_row, ang_col)
    kc, vc = mla_kv(hc, w_dkv, kv_g, w_ukv, None, None)
    k_all = jnp.concatenate([kc, kx], axis=2)
    v_all = jnp.concatenate([vc, vx], axis=2)
    ox = merge_heads(blocked_attend(qx, k_all, v_all)) @ w_o
    if last:
        return ox, None
    qc = mla_q(hc, w_dq, q_g, w_uq, None, None)
    oc = merge_heads(attend(qc, kc, vc)) @ w_o
    return ox, oc


def ret_proj(h, w):
    B, L, _ = h.shape
    return (h @ w).reshape(B, L, RET_HEADS, -1).transpose(0, 2, 1, 3)


def retention_chunkwise(q, k, v, log_gamma, state0):
    B, H, L, _ = q.shape
    dv = v.shape[-1]
    n = L // RET_CHUNK
    pos = jnp.arange(RET_CHUNK, dtype=jnp.float32)
    lg = log_gamma[:, None]
    diff = pos[:, None] - pos[None, :]
    lower = diff >= 0
    intra = jnp.where(lower, jnp.exp(lg[:, :, None] * jnp.where(lower, diff, 0.0)), 0.0)
    xi = jnp.exp(lg * (pos + 1.0))[None, :, :, None]
    zeta = jnp.exp(lg * (RET_CHUNK - 1.0 - pos))[None, :, :, None]
    chunk_decay = jnp.exp(log_gamma * RET_CHUNK)[None, :, None, None]

    def to_chunks(t):
        return t.reshape(B, H, n, RET_CHUNK, t.shape[-1]).transpose(2, 0, 1, 3, 4).astype(jnp.float32)

    def step(state, inp):
        qi, ki, vi = inp
        s = jnp.einsum('bhid,bhjd->bhij', qi, ki) * intra
        y = jnp.einsum('bhij,bhjv->bhiv', s, vi) + jnp.einsum('bhid,bhdv->bhiv', qi, state) * xi
        state = state * chunk_decay + jnp.einsum('bhjd,bhjv->bhdv', ki * zeta, vi)
        return state, y

    state, ys = lax.scan(step, state0, (to_chunks(q), to_chunks(k), to_chunks(v)))
    return ys.transpose(1, 2, 0, 3, 4).reshape(B, H, L, dv), state


def retention_state(k, v, log_gamma):
    L = k.shape[2]
    w = jnp.exp(log_gamma[:, None] * (L - 1.0 - jnp.arange(L, dtype=jnp.float32)))
    return jnp.einsum('bhld,bhlv->bhdv', k.astype(jnp.float32) * w[None, :, :, None], v.astype(jnp.float32))


def head_norm(y):
    mu = jnp.mean(y, axis=-1, keepdims=True)
    var = jnp.mean(jnp.square(y - mu), axis=-1, keepdims=True)
    return (y - mu) * lax.rsqrt(var + GN_EPS)


def ret_output(h, yf, yb, w_gf, w_gb, w_o):
    nf = merge_heads(head_norm(yf)).astype(h.dtype)
    nb = merge_heads(head_norm(yb)).astype(h.dtype)
    return (jax.nn.silu(h @ w_gf) * nf + jax.nn.silu(h @ w_gb) * nb) @ w_o


def retention_mixer(hx, hc, last, ang_seq, w_q, w_k, w_v, w_gf, w_gb, w_o, decay_f, decay_b):
    B = hx.shape[0]
    lgf = -jnp.exp(decay_f.astype(jnp.float32))
    lgb = -jnp.exp(decay_b.astype(jnp.float32))
    k_scale = RET_DK ** -0.5
    flip = lambda t: jnp.flip(t, axis=2)
    qx = rope(ret_proj(hx, w_q), ang_seq)
    kx = rope(ret_proj(hx, w_k), ang_seq) * k_scale
    vx = ret_proj(hx, w_v)
    kc = ret_proj(hc, w_k) * k_scale
    vc = ret_proj(hc, w_v)
    if last:
        sf = retention_state(kc, vc, lgf)
        sb = retention_state(flip(kc), flip(vc), lgb)
        oc = None
    else:
        qc = ret_proj(hc, w_q)
        zero = jnp.zeros((B, RET_HEADS, RET_DK, RET_DV), jnp.float32)
        ycf, sf = retention_chunkwise(qc, kc, vc, lgf, zero)
        ycb, sb = retention_chunkwise(flip(qc), flip(kc), flip(vc), lgb, zero)
        oc = ret_output(hc, ycf, flip(ycb), w_gf, w_gb, w_o)
    yxf, _ = retention_chunkwise(qx, kx, vx, lgf, sf)
    yxb, _ = retention_chunkwise(flip(qx), flip(kx), flip(vx), lgb, sb)
    ox = ret_output(hx, yxf, flip(yxb), w_gf, w_gb, w_o)
    return ox, oc


def conv_ffn(h, w_gate, w_up, conv_w, conv_b, w_down):
    L = h.shape[1]
    g = h @ w_gate
    pad = CONV_W // 2
    gp = jnp.pad(g, ((0, 0), (pad, pad), (0, 0)))
    g = sum(gp[:, j:j + L] * conv_w[j] for j in range(CONV_W)) + conv_b
    return (jax.nn.silu(g) * (h @ w_up)) @ w_down


def setup_inputs(seed: int = 0) -> dict:
    key = jax.random.key(seed)
    keys = iter(jax.random.split(key, 64))
    f32 = jnp.float32
    D, F = D_MODEL, FFN_DIM
    n_a = len(range(0, DEPTH, N_MIXERS))
    n_b = len(range(1, DEPTH, N_MIXERS))

    def w(shape, fan_in, gain=1.0):
        return jax.random.normal(next(keys), shape, f32) * (gain * fan_in ** -0.5)

    def g(shape):
        return 1.0 + 0.02 * jax.random.normal(next(keys), shape, f32)

    def small(shape):
        return 0.01 * jax.random.normal(next(keys), shape, f32)

    heads = jnp.arange(RET_HEADS, dtype=f32)
    decay0 = jnp.log(-jnp.log(1.0 - 2.0 ** (-5.0 - heads)))
    return {
        "x": jax.random.normal(next(keys), (BATCH, SEQ, D), f32),
        "c": jax.random.normal(next(keys), (BATCH, D), f32),
        "ctx": jax.random.normal(next(keys), (BATCH, CTX_LEN, D), f32),
        "c_ctx": jax.random.normal(next(keys), (D,), f32),
        "mod_w": w((DEPTH, D, N_MOD * D), D, 0.5),
        "mod_b": small((DEPTH, N_MOD * D)),
        "norm_mix_g": g((DEPTH, D)),
        "norm_ffn_g": g((DEPTH, D)),
        "mla_w_dq": w((n_a, D, MLA_Q_RANK), D),
        "mla_q_norm_g": g((n_a, MLA_Q_RANK)),
        "mla_w_uq": w((n_a, MLA_Q_RANK, MLA_HEADS * (MLA_NOPE + MLA_ROPE)), MLA_Q_RANK),
        "mla_w_dkv": w((n_a, D, MLA_KV_RANK + MLA_ROPE), D),
        "mla_kv_norm_g": g((n_a, MLA_KV_RANK)),
        "mla_w_ukv": w((n_a, MLA_KV_RANK, MLA_HEADS * (MLA_NOPE + MLA_V)), MLA_KV_RANK),
        "mla_w_o": w((n_a, MLA_HEADS * MLA_V, D), MLA_HEADS * MLA_V),
        "ret_w_q": w((n_b, D, RET_HEADS * RET_DK), D),
        "ret_w_k": w((n_b, D, RET_HEADS * RET_DK), D),
        "ret_w_v": w((n_b, D, RET_HEADS * RET_DV), D),
        "ret_w_gf": w((n_b, D, RET_HEADS * RET_DV), D),
        "ret_w_gb": w((n_b, D, RET_HEADS * RET_DV), D),
        "ret_w_o": w((n_b, RET_HEADS * RET_DV, D), RET_HEADS * RET_DV),
        "ret_decay_f": decay0[None] + 0.1 * jax.random.normal(next(keys), (n_b, RET_HEADS), f32),
        "ret_decay_b": decay0[None] + 0.1 * jax.random.normal(next(keys), (n_b, RET_HEADS), f32),
        "ffn_w_gate": w((DEPTH, D, F), D),
        "ffn_w_up": w((DEPTH, D, F), D),
        "ffn_conv_w": w((DEPTH, CONV_W, F), CONV_W),
        "ffn_conv_b": small((DEPTH, F)),
        "ffn_w_down": w((DEPTH, F, D), F),
        "final_norm_g": g((D,)),
    }


def reference(x, c, ctx, c_ctx, mod_w, mod_b, norm_mix_g, norm_ffn_g,
              mla_w_dq, mla_q_norm_g, mla_w_uq, mla_w_dkv, mla_kv_norm_g, mla_w_ukv, mla_w_o,
              ret_w_q, ret_w_k, ret_w_v, ret_w_gf, ret_w_gb, ret_w_o, ret_decay_f, ret_decay_b,
              ffn_w_gate, ffn_w_up, ffn_conv_w, ffn_conv_b, ffn_w_down, final_norm_g):
    L = x.shape[1]
    rows = L // GRID_W
    f32 = jnp.float32
    row = jnp.repeat(jnp.arange(rows), GRID_W).astype(f32)
    col = jnp.tile(jnp.arange(GRID_W), rows).astype(f32)
    axis_dim = MLA_ROPE // 2
    inv_axis = ROPE_BASE ** (-jnp.arange(0, axis_dim, 2, dtype=f32) / axis_dim)
    ang_row = row[:, None] * inv_axis[None]
    ang_col = col[:, None] * inv_axis[None]
    inv_ret = ROPE_BASE ** (-jnp.arange(0, RET_DK, 2, dtype=f32) / RET_DK)
    ang_seq = jnp.arange(L, dtype=f32)[:, None] * inv_ret[None]

    cond_x = jax.nn.silu(c)
    cond_c = jax.nn.silu(c_ctx)[None]
    for l in range(DEPTH):
        last = l == DEPTH - 1
        i = l // N_MIXERS
        mod_x = (cond_x @ mod_w[l] + mod_b[l])[:, None, :]
        mod_c = (cond_c @ mod_w[l] + mod_b[l])[:, None, :]
        sh1x, sc1x, g1x, sh2x, sc2x, g2x = jnp.split(mod_x, N_MOD, axis=-1)
        sh1c, sc1c, g1c, sh2c, sc2c, g2c = jnp.split(mod_c, N_MOD, axis=-1)
        hx = rmsnorm(x, norm_mix_g[l]) * (1.0 + sc1x) + sh1x
        hc = rmsnorm(ctx, norm_mix_g[l]) * (1.0 + sc1c) + sh1c
        if l % N_MIXERS == 0:
            ox, oc = mla_mixer(hx, hc, last, ang_row, ang_col, mla_w_dq[i], mla_q_norm_g[i], mla_w_uq[i],
                               mla_w_dkv[i], mla_kv_norm_g[i], mla_w_ukv[i], mla_w_o[i])
        else:
            ox, oc = retention_mixer(hx, hc, last, ang_seq, ret_w_q[i], ret_w_k[i], ret_w_v[i],
                                     ret_w_gf[i], ret_w_gb[i], ret_w_o[i], ret_decay_f[i], ret_decay_b[i])
        x = x + g1x * ox
        hx2 = rmsnorm(x, norm_ffn_g[l]) * (1.0 + sc2x) + sh2x
        x = x + g2x * conv_ffn(hx2, ffn_w_gate[l], ffn_w_up[l], ffn_conv_w[l], ffn_conv_b[l], ffn_w_down[l])
        if not last:
            ctx = ctx + g1c * oc
            hc2 = rmsnorm(ctx, norm_ffn_g[l]) * (1.0 + sc2c) + sh2c
            ctx = ctx + g2c * conv_ffn(hc2, ffn_w_gate[l], ffn_w_up[l], ffn_conv_w[l], ffn_conv_b[l], ffn_w_down[l])
    return rmsnorm(x, final_norm_g)
```

```python
import contextlib
import math
import numpy as np
import concourse.bass as bass
import concourse.mybir as mybir
from concourse.bass_utils import run_bass_kernel_spmd

F32 = mybir.dt.float32
BF16 = mybir.dt.bfloat16
I32 = mybir.dt.int32
AF = mybir.ActivationFunctionType
ALU = mybir.AluOpType
AX = mybir.AxisListType

D = 2048
SEQ = 4096
CTX = 256
NTOK = SEQ + CTX
NT = NTOK // 128
DEPTH = 4
FF = 5632
NFC = FF // 128
KD = D // 128
NMOD = 6
EPS = 1e-6
GN_EPS = 1e-5
MLA_H = 16
QR = 512
NOPE = 128
ROPE = 64
RET_H = 8
RDK = 256
RDV = 512

ENGS = ("pe", "act", "dve", "pool", "sp")
DMA_K = 8
SB_BASE = 16640
SB_END = 229376


class Buf:
    __slots__ = ("name", "w", "r", "excl")

    def __init__(self, name="", excl=False):
        self.name = name
        self.w = None
        self.r = {}
        self.excl = excl


class Sched:
    def __init__(self):
        self.streams = {e: [] for e in ENGS}
        self.cnt = {e: 0 for e in ENGS}
        self.seen = {e: {} for e in ENGS}
        self.dma_i = {e: 0 for e in ENGS}
        self.dma_val = {}

    def _wait(self, eng, tok):
        if tok is None:
            return
        k, v = tok
        if k == eng and eng == "pe":
            return
        s = self.seen[eng]
        if s.get(k, 0) >= v:
            return
        s[k] = v
        self.streams[eng].append(("w", k, v))

    def _deps(self, eng, reads, writes):
        for b in reads:
            self._wait(eng, b.w)
        for b in writes:
            self._wait(eng, b.w)
            for k, v in b.r.items():
                self._wait(eng, (k, v))

    def _mark(self, tok, reads, writes):
        k, v = tok
        for b in reads:
            if b.r.get(k, 0) < v:
                b.r[k] = v
        for b in writes:
            b.w = tok
            b.r = {}

    def op(self, eng, fn, reads=(), writes=(), signal=True):
        if any(b.excl for b in reads):
            writes = list(writes) + [b for b in reads if b.excl and b not in writes]
            reads = [b for b in reads if not b.excl]
        self._deps(eng, reads, writes)
        if signal:
            self.cnt[eng] += 1
            tok = (eng, self.cnt[eng])
        else:
            tok = (eng, self.cnt[eng] + 1)
        self.streams[eng].append(("o", fn, signal))
        self._mark(tok, reads, writes)

    def dma(self, q, fn, reads=(), writes=()):
        i = self.dma_i[q]
        self.dma_i[q] += 1
        key = ("d", q, i % DMA_K)
        prev = self.dma_val.get(key, 0)
        if prev:
            self._wait(q, (key, prev))
        self._deps(q, reads, writes)
        val = prev + 16
        self.dma_val[key] = val
        self.streams[q].append(("d", fn, key))
        self._mark((key, val), reads, writes)

    def barrier(self):
        for e in ENGS:
            for key, v in self.dma_val.items():
                self._wait(e, (key, v))
            for e2 in ENGS:
                if e2 != e and self.cnt[e2]:
                    self._wait(e, (e2, self.cnt[e2]))
            if e != "pe" and self.cnt[e]:
                self._wait(e, (e, self.cnt[e]))

    def emit(self, nc):
        keys = [e for e in ENGS if self.cnt[e]] + list(self.dma_val.keys())
        with contextlib.ExitStack() as st:
            sems = {}
            for k in keys:
                nm = k if isinstance(k, str) else "d_%s_%d" % (k[1], k[2])
                sems[k] = st.enter_context(nc.semaphore("s_" + nm))
            block = st.enter_context(nc.Block())
            streams = self.streams

            def run(engname, eng):
                for it in streams[engname]:
                    t = it[0]
                    if t == "w":
                        eng.wait_ge(sems[it[1]], it[2])
                    elif t == "o":
                        ins = it[1](eng)
                        if it[2]:
                            ins.then_inc(sems[engname], 1)
                    else:
                        it[1](eng).then_inc(sems[it[2]], 16)

            block.tensor(lambda e: run("pe", e))
            block.scalar(lambda e: run("act", e))
            block.vector(lambda e: run("dve", e))
            block.gpsimd(lambda e: run("pool", e))
            block.sync(lambda e: run("sp", e))


class Mem:
    def __init__(self, nc):
        self.nc = nc
        self.base = SB_BASE
        self.off = SB_BASE
        self.n = 0

    def alloc(self, name, shape, dt):
        nbytes = int(np.prod(shape[1:])) * mybir.dt.size(dt)
        nbytes = (nbytes + 63) // 64 * 64
        assert self.off + nbytes <= SB_END, (name, self.off, nbytes)
        self.n += 1
        t = self.nc.alloc_sbuf_tensor_at("%s_%d" % (name, self.n), list(shape), dt, offset=self.off)
        self.off += nbytes
        return t

    def mark(self):
        return self.off

    def release(self, m):
        self.off = m


def bcast_ap(ap, nparts):
    inner = [list(x) for x in ap.ap]
    while len(inner) > 1 and inner[0][1] == 1:
        inner = inner[1:]
    return bass.AP(tensor=ap.tensor, offset=ap.offset, ap=[[0, nparts]] + inner)


class K:
    def __init__(self, cfg):
        self.cfg = cfg
        self.nc = nc = bass.Bass("TRN2", target_bir_lowering=False)
        self.S = Sched()
        self.mem = Mem(nc)
        self.din = {}

    def dram_in(self, name, shape, dt=F32):
        t = self.nc.dram_tensor(name, list(shape), dt, kind="ExternalInput").ap()
        self.din[name] = t
        return t

    SHAPES = {
        "x": [SEQ, D], "c": [1, D], "ctx": [CTX, D], "c_ctx": [1, D], "final_norm_g": [1, D],
        "mod_w": [D, NMOD * D], "mod_b": [1, NMOD * D], "norm_mix_g": [1, D], "norm_ffn_g": [1, D],
        "mla_w_dq": [D, QR], "mla_q_norm_g": [1, QR], "mla_w_uq": [QR, MLA_H * 192],
        "mla_w_dkv": [D, QR + ROPE], "mla_kv_norm_g": [1, QR], "mla_w_ukv": [QR, MLA_H * 256], "mla_w_o": [D, D],
        "ret_w_q": [D, D], "ret_w_k": [D, D], "ret_w_v": [D, 2 * D], "ret_w_gf": [D, 2 * D], "ret_w_gb": [D, 2 * D],
        "ret_w_o": [2 * D, D], "ret_decay_f": [1, RET_H], "ret_decay_b": [1, RET_H],
        "ffn_w_gate": [D, FF], "ffn_w_up": [D, FF], "ffn_conv_w": [3, FF], "ffn_conv_b": [1, FF], "ffn_w_down": [FF, D],
    }
    GLOBAL = ("x", "c", "ctx", "c_ctx", "final_norm_g")

    def W(self, name, idx=None):
        key = name if name in self.GLOBAL else "%s_%d" % (name, idx)
        if key not in self.din:
            self.din[key] = self.nc.dram_tensor(key, list(self.SHAPES[name]), F32, kind="ExternalInput").ap()
        return self.din[key]

    def declare(self):
        nc = self.nc
        self.out = nc.dram_tensor("out", [SEQ, D], F32, kind="ExternalOutput").ap()
        self.res = nc.dram_tensor("res", [NTOK, D], F32).ap()
        self.modrows = nc.dram_tensor("modrows", [DEPTH, 2, NMOD * D], F32).ap()
        self.h2T = nc.dram_tensor("h2T_scr", [128, KD, NTOK], BF16).ap()
        if self.cfg.get("dbg_ctx"):
            self.dbg_ctx = nc.dram_tensor("dbg_ctx", [CTX, D], F32, kind="ExternalOutput").ap()

    def sb(self, name, shape, dt):
        return self.mem.alloc(name, shape, dt)

    def setup_consts(self):
        S = self.S
        self.ident = self.sb("ident", [128, 128], BF16)
        self.identf = self.sb("identf", [128, 128], F32)
        idb = Buf("ident")
        for t in (self.ident, self.identf):
            S.op("pool", lambda e, t=t: e.memset(t[:], 1.0), writes=[idb])
            S.op("pool", lambda e, t=t: e.affine_select(out=t[:], in_=t[:], pattern=[[-1, 128]], compare_op=ALU.is_equal,
                                                        fill=0.0, base=0, channel_multiplier=1), reads=[idb], writes=[idb])
        self.persist = self.mem.mark()

    def psum_setup(self):
        nc = self.nc
        self.pf = [nc.alloc_psum_tensor("pf%d" % i, [128, 512], F32) for i in range(8)]
        self.pfb = [Buf("pf%d" % i, excl=True) for i in range(8)]
        self.pb = [self.pf[6][:].bitcast(BF16), self.pf[7][:].bitcast(BF16)]
        self.pbb = [self.pfb[6], self.pfb[7]]

    def phase_begin(self):
        self.S.barrier()
        self.mem.release(self.persist)

    def init_res(self):
        S = self.S
        b = Buf("res_init")
        S.dma("sp", lambda e: e.dma_start(out=self.res[0:CTX, :], in_=self.W("ctx")), writes=[b])
        for i in range(4):
            S.dma("sp", lambda e, i=i: e.dma_start(out=self.res[CTX + i * 1024:CTX + (i + 1) * 1024, :],
                                                   in_=self.W("x")[i * 1024:(i + 1) * 1024, :]), writes=[b])

    def mod_setup(self, layers):
        S, nc = self.S, self.nc
        self.phase_begin()
        craw = self.sb("craw", [128, KD, 2], F32)
        condT = self.sb("condT", [128, KD, 2], BF16)
        b_c = Buf("craw")
        with nc.allow_non_contiguous_dma(reason="tiny cond vectors"):
            S.dma("sp", lambda e: e.dma_start(out=craw[:, :, 0:1], in_=self.W("c").rearrange("o (k p) -> p k o", p=128), allow_slow_non_contiguous=True), writes=[b_c])
            S.dma("sp", lambda e: e.dma_start(out=craw[:, :, 1:2], in_=self.W("c_ctx").rearrange("o (k p) -> p k o", p=128), allow_slow_non_contiguous=True), writes=[b_c])
        b_ct = Buf("condT")
        S.op("act", lambda e: e.activation(out=condT[:], in_=craw[:], func=AF.Silu), reads=[b_c], writes=[b_ct])
        NB = NMOD * D // 512
        wts = [self.sb("modw%d" % i, [128, KD, 512], BF16) for i in range(3)]
        wbs = [Buf("modw%d" % i) for i in range(3)]
        rows = self.sb("modrow", [2, NMOD * D], F32)
        brow = self.sb("modbrow", [2, NMOD * D], F32)
        grow = self.sb("modgrow", [2, 2, D], F32)
        b_rows, b_brow, b_grow = Buf("rows"), Buf("brow"), Buf("grow")
        it = 0
        for l in layers:
            S.dma("sp", lambda e, l=l: e.dma_start(out=brow[:], in_=bcast_ap(self.W("mod_b", l), 2)), writes=[b_brow])
            S.dma("sp", lambda e, l=l: e.dma_start(out=grow[:, 0, :], in_=bcast_ap(self.W("norm_mix_g", l), 2)), writes=[b_grow])
            S.dma("sp", lambda e, l=l: e.dma_start(out=grow[:, 1, :], in_=bcast_ap(self.W("norm_ffn_g", l), 2)), writes=[b_grow])
            for nb in range(NB):
                w, wb = wts[it % 3], wbs[it % 3]
                pf, pfb = self.pf[it % 2], self.pfb[it % 2]
                it += 1
                S.dma("pool", lambda e, l=l, nb=nb, w=w: e.dma_start(
                    out=w[:], in_=self.W("mod_w", l)[:, nb * 512:(nb + 1) * 512].rearrange("(k p) n -> p k n", p=128)), writes=[wb])
                for k in range(KD):
                    S.op("pe", lambda e, k=k, w=w, pf=pf: e.matmul(pf[0:2, :], lhsT=condT[:, k, :], rhs=w[:, k, :], start=(k == 0), stop=(k == KD - 1)),
                         reads=[b_ct, wb], writes=[pfb], signal=(k == KD - 1))
                S.op("dve", lambda e, nb=nb, pf=pf: e.tensor_tensor(out=rows[:, nb * 512:(nb + 1) * 512], in0=pf[0:2, :],
                                                                      in1=brow[:, nb * 512:(nb + 1) * 512], op=ALU.add),
                     reads=[pfb, b_brow], writes=[b_rows])
            S.op("dve", lambda e: e.scalar_tensor_tensor(out=rows[:, D:2 * D], in0=rows[:, D:2 * D], scalar=1.0, in1=grow[:, 0, :],
                                                         op0=ALU.add, op1=ALU.mult), reads=[b_rows, b_grow], writes=[b_rows])
            S.op("dve", lambda e: e.scalar_tensor_tensor(out=rows[:, 4 * D:5 * D], in0=rows[:, 4 * D:5 * D], scalar=1.0, in1=grow[:, 1, :],
                                                         op0=ALU.add, op1=ALU.mult), reads=[b_rows, b_grow], writes=[b_rows])
            S.dma("sp", lambda e, l=l: e.dma_start(out=self.modrows[l], in_=rows[:]), reads=[b_rows], writes=[])

    def load_bc(self, name, l, who, seg, n=D, eng="sp"):
        t = self.sb(name, [128, n], F32)
        b = Buf(name)
        src = self.modrows[l, who:who + 1, seg * D:seg * D + n]
        self.S.dma(eng, lambda e: e.dma_start(out=t[:], in_=bcast_ap(src, 128)), writes=[b])
        return t, b

    def norm_mod_tile(self, xt, b_xt, gs, b_gs, sh, b_sh, hT_out, b_hT, tmp, pbi):
        S = self.S
        junk, b_junk = tmp["junk"]
        ss, b_ss = tmp["ss"]
        t1, b_t1 = tmp["t1"]
        h, b_h = tmp["h"]
        S.op("act", lambda e: e.activation(out=junk[:], in_=xt, func=AF.Square, accum_out=ss[:, 0:1]), reads=[b_xt], writes=[b_junk, b_ss])
        eps, b_eps = tmp["eps"]
        S.op("act", lambda e: e.activation(out=ss[:, 1:2], in_=ss[:, 0:1], func=AF.Sqrt, scale=1.0 / D, bias=eps[:, 0:1]), reads=[b_ss, b_eps], writes=[b_ss])
        S.op("dve", lambda e: e.reciprocal(out=ss[:, 2:3], in_=ss[:, 1:2]), reads=[b_ss], writes=[b_ss])
        S.op("dve", lambda e: e.scalar_tensor_tensor(out=t1[:], in0=xt, scalar=ss[:, 2:3], in1=gs[:], op0=ALU.mult, op1=ALU.mult),
             reads=[b_xt, b_ss, b_gs], writes=[b_t1])
        S.op("pool", lambda e: e.tensor_tensor(out=h[:], in0=t1[:], in1=sh[:], op=ALU.add), reads=[b_t1, b_sh], writes=[b_h])
        for half in range(2):
            pb, pbb = self.pb[(pbi + half) % 2], self.pbb[(pbi + half) % 2]
            for j in range(8):
                k = half * 8 + j
                S.op("pe", lambda e, k=k, j=j, pb=pb: e.transpose(out=pb[:, j * 128:(j + 1) * 128], in_=h[:, k * 128:(k + 1) * 128], identity=self.ident[:]),
                     reads=[b_h], writes=[pbb], signal=(j == 7))
            eng = "act" if half == 0 else "dve"
            if eng == "act":
                S.op("act", lambda e, half=half, pb=pb: e.copy(out=hT_out[:, half * 8:(half + 1) * 8, :], in_=pb[:].rearrange("p (k t) -> p k t", k=8)),
                     reads=[pbb], writes=[b_hT])
            else:
                S.op("dve", lambda e, half=half, pb=pb: e.tensor_copy(out=hT_out[:, half * 8:(half + 1) * 8, :], in_=pb[:].rearrange("p (k t) -> p k t", k=8)),
                     reads=[pbb], writes=[b_hT])

    def norm_tmp(self, nsets=2):
        eps = self.sb("eps", [128, 1], F32)
        b_eps = Buf("eps")
        self.S.op("pool", lambda e: e.memset(eps[:], EPS), writes=[b_eps])
        sets = []
        for i in range(nsets):
            sets.append({
                "junk": (self.sb("junk%d" % i, [128, D], BF16), Buf("junk")),
                "ss": (self.sb("ss%d" % i, [128, 4], F32), Buf("ss")),
                "t1": (self.sb("t1_%d" % i, [128, D], F32), Buf("t1")),
                "h": (self.sb("h%d" % i, [128, D], BF16), Buf("h")),
                "eps": (eps, b_eps),
            })
        if nsets == 1:
            sets.append(sets[0])
        return sets

    def ffn_f1(self, l, skip_ctx=False):
        self.make_hT(l, 4, 3, skip_ctx)

    def make_hT(self, l, seg_gs, seg_sh, skip_ctx=False):
        S = self.S
        self.phase_begin()
        bcs = {}
        for who in (0, 1):
            bcs[who] = (self.load_bc("gs2_%d" % who, l, who, seg_gs), self.load_bc("sh2_%d" % who, l, who, seg_sh))
        tmps = self.norm_tmp()
        xts = [(self.sb("xt%d" % i, [128, D], F32), Buf("xt")) for i in range(2)]
        stg = [(self.sb("stg%d" % i, [128, KD, 512], BF16), Buf("stg")) for i in range(2)]
        t0 = 2 if skip_ctx else 0
        groups = ([] if skip_ctx else [[0, 1]]) + [list(range(2 + 4 * g, 6 + 4 * g)) for g in range(8)]
        gi = 0
        for grp in groups:
            st, b_st = stg[gi % 2]
            for j, t in enumerate(grp):
                xt, b_xt = xts[t % 2]
                who = 1 if t < 2 else 0
                S.dma("sp", lambda e, t=t, xt=xt: e.dma_start(out=xt[:], in_=self.res[t * 128:(t + 1) * 128, :]), writes=[b_xt])
                (gs, b_gs), (sh, b_sh) = bcs[who]
                self.norm_mod_tile(xt[:], b_xt, gs, b_gs, sh, b_sh, st[:, :, j * 128:(j + 1) * 128], b_st, tmps[t % 2], 0)
            n = len(grp) * 128
            c0 = grp[0] * 128
            S.dma("sp", lambda e, st=st, n=n, c0=c0: e.dma_start(out=self.h2T[:, :, c0:c0 + n], in_=st[:, :, 0:n]), reads=[b_st], writes=[])
            gi += 1

    def ffn_f2(self, l, skip_ctx=False):
        S, nc = self.S, self.nc
        self.phase_begin()
        cw = self.sb("cw", [128, NFC, 3], F32)
        cb = self.sb("cb", [128, NFC], F32)
        b_cw = Buf("cw")
        crow = self.sb("crow", [NFC, 4, 128], F32)
        b_crow = Buf("crow")
        for j in range(3):
            S.dma("sp", lambda e, j=j: e.dma_start(out=crow[:, j, :], in_=self.W("ffn_conv_w", l)[j:j + 1, :].rearrange("o (c p) -> (o c) p", p=128)), writes=[b_crow])
        S.dma("sp", lambda e: e.dma_start(out=crow[:, 3, :], in_=self.W("ffn_conv_b", l).rearrange("o (c p) -> (o c) p", p=128)), writes=[b_crow])
        for j in range(4):
            S.op("pe", lambda e, j=j: e.transpose(out=self.pf[7][:, j * NFC:(j + 1) * NFC], in_=crow[:, j, :], identity=self.identf[0:NFC, 0:NFC]),
                 reads=[b_crow], writes=[self.pfb[7]])
        S.op("dve", lambda e: e.tensor_copy(out=cw[:].rearrange("p c j -> p j c"), in_=self.pf[7][:, 0:3 * NFC].rearrange("p (j c) -> p j c", j=3)), reads=[self.pfb[7]], writes=[b_cw])
        S.op("dve", lambda e: e.tensor_copy(out=cb[:], in_=self.pf[7][:, 3 * NFC:4 * NFC]), reads=[self.pfb[7]], writes=[b_cw])
        g2t = self.sb("g2t", [128, D], F32)
        b_g2 = Buf("g2t")
        g2_who = [None]
        hblk = self.sb("hblk", [128, KD, 514], BF16)
        b_hblk = Buf("hblk")
        aT = self.sb("aT", [128, NFC, 512], BF16)
        b_aT = [Buf("aT%d" % i) for i in range(NFC)]
        wg = [(self.sb("wg%d" % i, [128, KD, 512], BF16), Buf("wg")) for i in range(2)]
        wu = [(self.sb("wu%d" % i, [128, KD, 512], BF16), Buf("wu")) for i in range(2)]
        wd = [(self.sb("wd%d" % i, [128, NFC, 256], BF16), Buf("wd")) for i in range(2)]
        gsb = [(self.sb("gsb%d" % i, [128, 514], F32), Buf("gsb")) for i in range(2)]
        c1 = [(self.sb("c1_%d" % i, [128, 512], F32), Buf("c1")) for i in range(2)]
        c2 = [(self.sb("c2_%d" % i, [128, 512], F32), Buf("c2")) for i in range(2)]
        sg = [(self.sb("sg%d" % i, [128, 512], F32), Buf("sg")) for i in range(2)]
        xp = [(self.sb("xp%d" % i, [128, 256], F32), Buf("xp")) for i in range(3)]
        yp = [(self.sb("yp%d" % i, [128, 256], F32), Buf("yp")) for i in range(3)]
        tp = [(self.sb("tp%d" % i, [128, 256], F32), Buf("tp")) for i in range(2)]
        pg, pu, ph, pd = (self.pf[0], self.pf[1]), (self.pf[2], self.pf[3]), self.pf[4], (self.pf[5], self.pf[6])
        b_pg, b_pu, b_ph, b_pd = (self.pfb[0], self.pfb[1]), (self.pfb[2], self.pfb[3]), self.pfb[4], (self.pfb[5], self.pfb[6])
        blocks = [] if skip_ctx else [(0, 256, True, True, 1)]
        for i in range(8):
            blocks.append((CTX + i * 512, 512, i == 0, i == 7, 0))
        it = 0
        pit = 0
        for (s0, n, lpad, rpad, who) in blocks:
            lo = s0 - (0 if lpad else 1)
            hi = s0 + n + (0 if rpad else 1)
            dst0 = 1 - (s0 - lo)
            S.dma("sp", lambda e, lo=lo, hi=hi, dst0=dst0: e.dma_start(out=hblk[:, :, dst0:dst0 + hi - lo], in_=self.h2T[:, :, lo:hi]), writes=[b_hblk])
            if lpad:
                S.op("dve", lambda e: e.memset(hblk[:, :, 0:1], 0.0), writes=[b_hblk])
            if rpad:
                S.op("dve", lambda e, n=n: e.memset(hblk[:, :, n + 1:n + 2], 0.0), writes=[b_hblk])
            for fb in range(NFC // 4):
                (wgt, b_wg), (wut, b_wu) = wg[fb % 2], wu[fb % 2]
                S.dma("pool", lambda e, fb=fb, wgt=wgt: e.dma_start(out=wgt[:], in_=self.W("ffn_w_gate", l)[:, fb * 512:(fb + 1) * 512].rearrange("(k p) n -> p k n", p=128)), writes=[b_wg])
                S.dma("pool", lambda e, fb=fb, wut=wut: e.dma_start(out=wut[:], in_=self.W("ffn_w_up", l)[:, fb * 512:(fb + 1) * 512].rearrange("(k p) n -> p k n", p=128)), writes=[b_wu])
                for f4 in range(4):
                    fc = fb * 4 + f4
                    i2 = it % 2
                    it += 1
                    for k in range(KD):
                        S.op("pe", lambda e, k=k, f4=f4, wgt=wgt, i2=i2, n=n: e.matmul(pg[i2][:, 0:n], lhsT=wgt[:, k, f4 * 128:(f4 + 1) * 128], rhs=hblk[:, k, 1:n + 1],
                                                                                         start=(k == 0), stop=(k == KD - 1)),
                             reads=[b_wg, b_hblk], writes=[b_pg[i2]], signal=(k == KD - 1))
                    for k in range(KD):
                        S.op("pe", lambda e, k=k, f4=f4, wgt=wgt, n=n: e.matmul(ph[:, 0:2], lhsT=wgt[:, k, f4 * 128:(f4 + 1) * 128], rhs=hblk[:, k, 0:n + 2:n + 1],
                                                                                  start=(k == 0), stop=(k == KD - 1)),
                             reads=[b_wg, b_hblk], writes=[b_ph], signal=(k == KD - 1))
                    for k in range(KD):
                        S.op("pe", lambda e, k=k, f4=f4, wut=wut, i2=i2, n=n: e.matmul(pu[i2][:, 0:n], lhsT=wut[:, k, f4 * 128:(f4 + 1) * 128], rhs=hblk[:, k, 1:n + 1],
                                                                                         start=(k == 0), stop=(k == KD - 1)),
                             reads=[b_wu, b_hblk], writes=[b_pu[i2]], signal=(k == KD - 1))
                    g, b_g = gsb[i2]
                    S.op("act", lambda e, g=g, i2=i2, n=n: e.copy(out=g[:, 1:n + 1], in_=pg[i2][:, 0:n]), reads=[b_pg[i2]], writes=[b_g])
                    S.op("act", lambda e, g=g, n=n: e.copy(out=g[:, 0:n + 2:n + 1], in_=ph[:, 0:2]), reads=[b_ph], writes=[b_g])
                    (t1, b_t1), (t2, b_t2), (s, b_s) = c1[i2], c2[i2], sg[i2]
                    S.op("pool", lambda e, g=g, t1=t1, fc=fc, n=n: e.tensor_scalar(out=t1[:, 0:n], in0=g[:, 0:n], scalar1=cw[:, fc, 0:1], scalar2=None, op0=ALU.mult),
                         reads=[b_g, b_cw], writes=[b_t1])
                    S.op("dve", lambda e, g=g, t1=t1, t2=t2, fc=fc, n=n: e.scalar_tensor_tensor(out=t2[:, 0:n], in0=g[:, 1:n + 1], scalar=cw[:, fc, 1:2], in1=t1[:, 0:n],
                                                                                             op0=ALU.mult, op1=ALU.add), reads=[b_g, b_t1, b_cw], writes=[b_t2])
                    S.op("dve", lambda e, g=g, t1=t1, t2=t2, fc=fc, n=n: e.scalar_tensor_tensor(out=t1[:, 0:n], in0=g[:, 2:n + 2], scalar=cw[:, fc, 2:3], in1=t2[:, 0:n],
                                                                                             op0=ALU.mult, op1=ALU.add), reads=[b_g, b_t2, b_cw], writes=[b_t1])
                    S.op("act", lambda e, t1=t1, s=s, fc=fc, n=n: e.activation(out=s[:, 0:n], in_=t1[:, 0:n], func=AF.Silu, bias=cb[:, fc:fc + 1]), reads=[b_t1, b_cw], writes=[b_s])
                    S.op("dve", lambda e, s=s, fc=fc, i2=i2, n=n: e.tensor_tensor(out=aT[:, fc, 0:n], in0=pu[i2][:, 0:n], in1=s[:, 0:n], op=ALU.mult),
                         reads=[b_pu[i2], b_s], writes=[b_aT[fc]])
            if g2_who[0] != who:
                g2_who[0] = who
                S.dma("sp", lambda e, who=who: e.dma_start(out=g2t[:], in_=bcast_ap(self.modrows[l, who:who + 1, 5 * D:6 * D], 128)), writes=[b_g2])
            for nb in range(D // 256):
                wdt, b_wd = wd[nb % 2]
                S.dma("pool", lambda e, nb=nb, wdt=wdt: e.dma_start(out=wdt[:], in_=self.W("ffn_w_down", l)[:, nb * 256:(nb + 1) * 256].rearrange("(c p) n -> p c n", p=128)), writes=[b_wd])
                for tj in range(n // 128):
                    r0 = s0 + tj * 128
                    (xpt, b_xp), (ypt, b_yp), (tpt, b_tp) = xp[pit % 3], yp[pit % 3], tp[pit % 2]
                    pit += 1
                    S.dma("sp", lambda e, r0=r0, nb=nb, xpt=xpt: e.dma_start(out=xpt[:], in_=self.res[r0:r0 + 128, nb * 256:(nb + 1) * 256]), writes=[b_xp])
                    hp = pit % 2
                    for fc in range(NFC):
                        S.op("pe", lambda e, fc=fc, tj=tj, wdt=wdt, hp=hp: e.matmul(pd[hp][:, 0:256], lhsT=aT[:, fc, tj * 128:(tj + 1) * 128], rhs=wdt[:, fc, :],
                                                                                     start=(fc == 0), stop=(fc == NFC - 1)),
                             reads=[b_aT[fc], b_wd], writes=[b_pd[hp]], signal=(fc == NFC - 1))
                    S.op("dve", lambda e, tpt=tpt, nb=nb, hp=hp: e.tensor_tensor(out=tpt[:], in0=pd[hp][:, 0:256], in1=g2t[:, nb * 256:(nb + 1) * 256], op=ALU.mult),
                         reads=[b_pd[hp], b_g2], writes=[b_tp])
                    S.op("pool", lambda e, tpt=tpt, xpt=xpt, ypt=ypt: e.tensor_tensor(out=ypt[:], in0=tpt[:], in1=xpt[:], op=ALU.add), reads=[b_tp, b_xp], writes=[b_yp])
                    S.dma("sp", lambda e, r0=r0, nb=nb, ypt=ypt: e.dma_start(out=self.res[r0:r0 + 128, nb * 256:(nb + 1) * 256], in_=ypt[:]), reads=[b_yp], writes=[])

    def final_norm(self, plain_copy=False):
        S = self.S
        self.phase_begin()
        if self.cfg.get("dbg_ctx"):
            S.dma("sp", lambda e: e.dma_start(out=self.dbg_ctx, in_=self.res[0:CTX, :]), writes=[Buf("dbgc")])
        if plain_copy:
            b = Buf("o")
            for i in range(4):
                S.dma("sp", lambda e, i=i: e.dma_start(out=self.out[i * 1024:(i + 1) * 1024, :], in_=self.res[CTX + i * 1024:CTX + (i + 1) * 1024, :]), writes=[b])
            return
        g = self.sb("fng", [128, D], F32)
        b_g = Buf("fng")
        S.dma("sp", lambda e: e.dma_start(out=g[:], in_=bcast_ap(self.W("final_norm_g"), 128)), writes=[b_g])
        eps = self.sb("eps", [128, 1], F32)
        b_eps = Buf("eps")
        S.op("pool", lambda e: e.memset(eps[:], EPS), writes=[b_eps])
        xts = [(self.sb("xt%d" % i, [128, D], F32), Buf("xt")) for i in range(3)]
        ys = [(self.sb("y%d" % i, [128, D], F32), Buf("y")) for i in range(3)]
        junk = [(self.sb("junk%d" % i, [128, D], BF16), Buf("junk")) for i in range(2)]
        sss = [(self.sb("ss%d" % i, [128, 4], F32), Buf("ss")) for i in range(2)]
        for t in range(SEQ // 128):
            (xt, b_xt), (y, b_y), (jk, b_jk), (ss, b_ss) = xts[t % 3], ys[t % 3], junk[t % 2], sss[t % 2]
            S.dma("sp", lambda e, t=t, xt=xt: e.dma_start(out=xt[:], in_=self.res[CTX + t * 128:CTX + (t + 1) * 128, :]), writes=[b_xt])
            S.op("act", lambda e, xt=xt, jk=jk, ss=ss: e.activation(out=jk[:], in_=xt[:], func=AF.Square, accum_out=ss[:, 0:1]), reads=[b_xt], writes=[b_jk, b_ss])
            S.op("act", lambda e, ss=ss: e.activation(out=ss[:, 1:2], in_=ss[:, 0:1], func=AF.Sqrt, scale=1.0 / D, bias=eps[:, 0:1]), reads=[b_ss, b_eps], writes=[b_ss])
            S.op("dve", lambda e, ss=ss: e.reciprocal(out=ss[:, 2:3], in_=ss[:, 1:2]), reads=[b_ss], writes=[b_ss])
            S.op("dve", lambda e, xt=xt, y=y, ss=ss: e.scalar_tensor_tensor(out=y[:], in0=xt[:], scalar=ss[:, 2:3], in1=g[:], op0=ALU.mult, op1=ALU.mult),
                 reads=[b_xt, b_ss, b_g], writes=[b_y])
            S.dma("sp", lambda e, t=t, y=y: e.dma_start(out=self.out[t * 128:(t + 1) * 128, :], in_=y[:]), reads=[b_y], writes=[])

    def build(self):
        cfg = self.cfg
        self.declare()
        self.psum_setup()
        self.setup_consts()
        self.init_res()
        layers = cfg.get("layers", list(range(DEPTH)))
        if not cfg.get("no_mod"):
            self.mod_setup(sorted(set(l for l, _ in layers)) or cfg.get("mod_layers", []))
        for (l, what) in layers:
            last = (l == DEPTH - 1)
            if what in ("mix", "all"):
                if l % 2 == 0:
                    self.mla(l, last)
                else:
                    self.ret(l, last)
            if what in ("ffn", "all", "f1"):
                self.ffn_f1(l, skip_ctx=last)
            if what in ("ffn", "all", "f2"):
                self.ffn_f2(l, skip_ctx=last)
        self.final_norm(plain_copy=cfg.get("plain_out", False))
        self.S.barrier()
        self.S.emit(self.nc)
        return self.nc


_CACHE = {}


def get_program(cfg_key="full"):
    if cfg_key not in _CACHE:
        cfg = {"layers": [(l, "all") for l in range(DEPTH)]}
        k = K(cfg)
        nc = k.build()
        _CACHE[cfg_key] = (nc, list(k.din.keys()))
    return _CACHE[cfg_key]


def make_in_maps(inputs, names, ncores=8):
    f = lambda a: np.ascontiguousarray(np.asarray(a, dtype=np.float32))
    shared = {}
    for key in names:
        if key in ("x", "c", "ctx"):
            continue
        if key in ("c_ctx", "final_norm_g"):
            shared[key] = f(inputs[key]).reshape(1, D)
            continue
        name, idx = key.rsplit("_", 1)
        a = f(inputs[name][int(idx)])
        shared[key] = a.reshape(1, -1) if a.ndim == 1 else a
    maps = []
    for b in range(ncores):
        m = dict(shared)
        if "x" in names:
            m["x"] = f(inputs["x"][b])
        if "c" in names:
            m["c"] = f(inputs["c"][b]).reshape(1, D)
        if "ctx" in names:
            m["ctx"] = f(inputs["ctx"][b])
        maps.append(m)
    return maps


def kernel(**inputs):
    nc, names = get_program()
    maps = make_in_maps(inputs, names, 8)
    res = run_bass_kernel_spmd(nc, maps, core_ids=list(range(8)))
    return np.stack([np.asarray(r["out"], dtype=np.float32) for r in res.results], axis=0)


TWO_PI = 2.0 * math.pi
PI_LO = 3.1415925


def _gen_sincos(self, ang, b_ang, P, N, cos_out, sin_out, b_out, tA, tB, tI, b_t):
    S = self.S
    a, A, B, I = ang[0:P, 0:N], tA[0:P, 0:N], tB[0:P, 0:N], tI[0:P, 0:N]
    S.op("dve", lambda e: e.tensor_scalar(out=A, in0=a, scalar1=1.0 / TWO_PI, scalar2=None, op0=ALU.mult), reads=[b_ang], writes=[b_t])
    S.op("dve", lambda e: e.tensor_copy(out=I, in_=A), reads=[b_t], writes=[b_t])
    S.op("dve", lambda e: e.tensor_copy(out=A, in_=I), reads=[b_t], writes=[b_t])
    S.op("dve", lambda e: e.scalar_tensor_tensor(out=B, in0=A, scalar=-TWO_PI, in1=a, op0=ALU.mult, op1=ALU.add), reads=[b_t, b_ang], writes=[b_t])

    def wrap(X):
        S.op("dve", lambda e: e.tensor_scalar(out=A, in0=X, scalar1=math.pi, scalar2=None, op0=ALU.is_gt), reads=[b_t], writes=[b_t])
        S.op("dve", lambda e: e.scalar_tensor_tensor(out=X, in0=A, scalar=-TWO_PI, in1=X, op0=ALU.mult, op1=ALU.add), reads=[b_t], writes=[b_t])
        S.op("dve", lambda e: e.tensor_scalar(out=A, in0=X, scalar1=-math.pi, scalar2=None, op0=ALU.is_lt), reads=[b_t], writes=[b_t])
        S.op("dve", lambda e: e.scalar_tensor_tensor(out=X, in0=A, scalar=TWO_PI, in1=X, op0=ALU.mult, op1=ALU.add), reads=[b_t], writes=[b_t])
        S.op("dve", lambda e: e.tensor_scalar(out=X, in0=X, scalar1=PI_LO, scalar2=-PI_LO, op0=ALU.min, op1=ALU.max), reads=[b_t], writes=[b_t])

    wrap(B)
    S.op("act", lambda e: e.activation(out=sin_out, in_=B, func=AF.Sin), reads=[b_t], writes=[b_out])
    S.op("dve", lambda e: e.tensor_scalar(out=B, in0=B, scalar1=math.pi / 2, scalar2=None, op0=ALU.add), reads=[b_t, b_out], writes=[b_t])
    wrap(B)
    S.op("act", lambda e: e.activation(out=cos_out, in_=B, func=AF.Sin), reads=[b_t], writes=[b_out])


K.gen_sincos = _gen_sincos


def _mla_tables(self):
    S = self.S
    cosT = self.sb("cosT", [64, NTOK], BF16)
    sinT = self.sb("sinT", [64, NTOK], BF16)
    b_tab = Buf("tab")
    self.mla_keep = self.mem.mark()
    m = self.mem.mark()
    pidx = self.sb("pidx", [64, 1], I32)
    inv = self.sb("inv", [64, 2], F32)
    posi = self.sb("posi", [64, SEQ], I32)
    ang = self.sb("ang", [64, SEQ], F32)
    tA = self.sb("tA", [64, SEQ], F32)
    tB = self.sb("tB", [64, SEQ], F32)
    tI = self.sb("tI", [64, SEQ], I32)
    b_i, b_ang = Buf("inv"), Buf("ang")
    b_t = b_ang
    S.op("pool", lambda e: e.iota(pidx[:], pattern=[[0, 1]], base=0, channel_multiplier=1), writes=[b_i])
    pfl = self.sb("pfl", [64, 2], F32)
    S.op("dve", lambda e: e.tensor_copy(out=pfl[:, 0:1], in_=pidx[:]), reads=[b_i], writes=[b_i])
    S.op("dve", lambda e: e.tensor_copy(out=inv[:, 0:1], in_=pidx[:]), reads=[b_i], writes=[b_i])
    for thr in (16.0, 32.0, 48.0):
        S.op("dve", lambda e, thr=thr: e.tensor_scalar(out=pfl[:, 1:2], in0=pfl[:, 0:1], scalar1=thr, scalar2=None, op0=ALU.is_ge), reads=[b_i], writes=[b_i])
        S.op("dve", lambda e: e.scalar_tensor_tensor(out=inv[:, 0:1], in0=pfl[:, 1:2], scalar=-16.0, in1=inv[:, 0:1], op0=ALU.mult, op1=ALU.add), reads=[b_i], writes=[b_i])
    S.op("act", lambda e: e.activation(out=inv[:, 1:2], in_=inv[:, 0:1], func=AF.Exp, scale=-math.log(10000.0) / 16.0), reads=[b_i], writes=[b_i])
    S.op("pool", lambda e: e.iota(posi[:, :], pattern=[[1, SEQ]], base=0, channel_multiplier=0), writes=[b_ang])
    S.op("dve", lambda e: e.tensor_copy(out=tA[:], in_=posi[:]), reads=[b_ang], writes=[b_ang])
    S.op("dve", lambda e: e.tensor_scalar(out=tB[:], in0=tA[:], scalar1=-31.5, scalar2=1.0 / 64.0, op0=ALU.add, op1=ALU.mult), reads=[b_ang], writes=[b_ang])
    S.op("dve", lambda e: e.tensor_copy(out=tI[:], in_=tB[:]), reads=[b_ang], writes=[b_ang])
    S.op("dve", lambda e: e.tensor_copy(out=tB[:], in_=tI[:]), reads=[b_ang], writes=[b_ang])
    S.op("dve", lambda e: e.tensor_scalar(out=ang[0:32, :], in0=tB[0:32, :], scalar1=inv[0:32, 1:2], scalar2=None, op0=ALU.mult), reads=[b_ang, b_i], writes=[b_ang])
    S.op("dve", lambda e: e.scalar_tensor_tensor(out=tA[32:64, :], in0=tB[32:64, :], scalar=-64.0, in1=tA[32:64, :], op0=ALU.mult, op1=ALU.add), reads=[b_ang], writes=[b_ang])
    S.op("dve", lambda e: e.tensor_scalar(out=ang[32:64, :], in0=tA[32:64, :], scalar1=inv[32:64, 1:2], scalar2=None, op0=ALU.mult), reads=[b_ang, b_i], writes=[b_ang])
    self.gen_sincos(ang, b_ang, 64, SEQ, cosT[:, CTX:NTOK], sinT[:, CTX:NTOK], b_tab, tA, tB, tI, b_t)
    S.op("dve", lambda e: e.memset(cosT[:, 0:CTX], 1.0), writes=[b_tab])
    S.op("dve", lambda e: e.memset(sinT[:, 0:CTX], 0.0), writes=[b_tab])
    self.mem.release(m)
    return cosT, sinT


K.mla_tables = _mla_tables


def _rot_weights(self, eng_ops, wsrc4, wdst4, reads, writes):
    S = self.S
    for (dq, sq, sgn) in ((0, 1, -1.0), (1, 0, 1.0), (2, 3, -1.0), (3, 2, 1.0)):
        S.op("dve", lambda e, dq=dq, sq=sq, sgn=sgn: e.tensor_scalar(out=wdst4[:, :, dq, :], in0=wsrc4[:, :, sq, :], scalar1=sgn, scalar2=None, op0=ALU.mult),
             reads=reads, writes=writes)


K.rot_weights = _rot_weights


def _mla(self, l, last):
    S, nc = self.S, self.nc
    i = l // 2
    SCALE = 192.0 ** -0.5
    self.phase_begin()
    cosT, sinT = self.mla_tables()
    cqnT = self.sb("cqnT", [128, 4, NTOK], BF16)
    ckvT = self.sb("ckvT", [128, 4, NTOK], BF16)
    krT = self.sb("krT", [64, NTOK], BF16)
    keep = self.mem.mark()
    if not hasattr(self, "oT_scr"):
        self.oT_scr = nc.dram_tensor("oT_scr", [MLA_H, 128, NTOK], BF16).ap()

    if self.cfg.get("mla_stop") == "tab":
        return
    S.barrier()
    wdq = self.sb("wdq", [128, KD, QR], BF16)
    wdkv = self.sb("wdkv", [128, KD, QR + ROPE], BF16)
    wrot = self.sb("wrot", [128, KD, ROPE], BF16)
    b_wdq, b_wdkv, b_wrot = Buf("wdq"), Buf("wdkv"), Buf("wrot")
    for c4 in range(4):
        S.dma("pool", lambda e, c4=c4: e.dma_start(out=wdq[:, c4 * 4:(c4 + 1) * 4, :], in_=self.W("mla_w_dq", i)[c4 * 512:(c4 + 1) * 512, :].rearrange("(k p) n -> p k n", p=128)), writes=[b_wdq])
        S.dma("pool", lambda e, c4=c4: e.dma_start(out=wdkv[:, c4 * 4:(c4 + 1) * 4, :], in_=self.W("mla_w_dkv", i)[c4 * 512:(c4 + 1) * 512, :].rearrange("(k p) n -> p k n", p=128)), writes=[b_wdkv])
    SK = self.cfg.get("p1_skip", "")
    if "r" not in SK:
        self.rot_weights(None, wdkv[:, :, QR:QR + ROPE].rearrange("p k (q f) -> p k q f", q=4), wrot[:].rearrange("p k (q f) -> p k q f", q=4), [b_wdkv], [b_wrot])
    qg = self.sb("qg", [128, QR], F32)
    kvg = self.sb("kvg", [128, QR], F32)
    b_g = Buf("qkvg")
    S.dma("sp", lambda e: e.dma_start(out=qg[:], in_=bcast_ap(self.W("mla_q_norm_g", i), 128)), writes=[b_g])
    S.dma("sp", lambda e: e.dma_start(out=kvg[:], in_=bcast_ap(self.W("mla_kv_norm_g", i), 128)), writes=[b_g])
    gs = self.sb("gs1", [128, D], F32)
    sh = self.sb("sh1", [128, D], F32)
    b_gs, b_sh = Buf("gs1"), Buf("sh1")
    tmps = self.norm_tmp(1)
    eps, b_eps = tmps[0]["eps"]
    xts = [(self.sb("xt%d" % j, [128, D], F32), Buf("xt")) for j in range(2)]
    hTs = [(self.sb("hT%d" % j, [128, KD, 128], BF16), Buf("hT")) for j in range(2)]
    cqn = [(self.sb("cqn%d" % j, [128, 2, QR], BF16), Buf("cqn")) for j in range(2)]
    sq = [(self.sb("sq%d" % j, [128, 8], F32), Buf("sq")) for j in range(2)]
    jk2 = self.sb("jk2", [128, QR], BF16)
    b_jk2 = Buf("jk2")
    rt = [(self.sb("rt%d" % j, [64, 2, 128], F32), Buf("rt")) for j in range(2)]
    b_cq, b_kv, b_kr = Buf("cqnT"), Buf("ckvT"), Buf("krT")
    for t in range(NT):
        who = 1 if t < 2 else 0
        if t in (0, 2):
            S.dma("sp", lambda e, who=who: e.dma_start(out=gs[:], in_=bcast_ap(self.modrows[l, who:who + 1, 1 * D:2 * D], 128)), writes=[b_gs])
            S.dma("sp", lambda e, who=who: e.dma_start(out=sh[:], in_=bcast_ap(self.modrows[l, who:who + 1, 0:D], 128)), writes=[b_sh])
        xt, b_xt = xts[t % 2]
        hT, b_hT = hTs[t % 2]
        S.dma("sp", lambda e, t=t, xt=xt: e.dma_start(out=xt[:], in_=self.res[t * 128:(t + 1) * 128, :]), writes=[b_xt])
        self.norm_mod_tile(xt[:], b_xt, gs, b_gs, sh, b_sh, hT[:], b_hT, tmps[t % 2], 0)
        pq, b_pq = self.pf[t % 2], self.pfb[t % 2]
        pk, b_pk = self.pf[2 + t % 2], self.pfb[2 + t % 2]
        for k in range(KD):
            S.op("pe", lambda e, k=k, hT=hT, pq=pq: e.matmul(pq[:, :], lhsT=hT[:, k, :], rhs=wdq[:, k, :], start=(k == 0), stop=(k == KD - 1)),
                 reads=[b_hT, b_wdq], writes=[b_pq], signal=(k == KD - 1))
        for k in range(KD):
            S.op("pe", lambda e, k=k, hT=hT, pk=pk: e.matmul(pk[:, :], lhsT=hT[:, k, :], rhs=wdkv[:, k, 0:QR], start=(k == 0), stop=(k == KD - 1)),
                 reads=[b_hT, b_wdkv], writes=[b_pk], signal=(k == KD - 1))
        for k in range(KD if "k" not in SK else 0):
            S.op("pe", lambda e, k=k, hT=hT: e.matmul(self.pf[4][0:64, 0:128], lhsT=wdkv[:, k, QR:QR + ROPE], rhs=hT[:, k, :], start=(k == 0), stop=(k == KD - 1)),
                 reads=[b_hT, b_wdkv], writes=[self.pfb[4]], signal=(k == KD - 1))
        for k in range(KD if "k" not in SK else 0):
            S.op("pe", lambda e, k=k, hT=hT: e.matmul(self.pf[5][0:64, 0:128], lhsT=wrot[:, k, :], rhs=hT[:, k, :], start=(k == 0), stop=(k == KD - 1)),
                 reads=[b_hT, b_wrot], writes=[self.pfb[5]], signal=(k == KD - 1))
        (cn, b_cn), (s8, b_s8) = cqn[t % 2], sq[t % 2]
        if "n" in SK:
            continue
        for which, (pp, b_pp, gt) in enumerate(((pq, b_pq, qg), (pk, b_pk, kvg))):
            o = which * 4
            S.op("act", lambda e, pp=pp, s8=s8, o=o: e.activation(out=jk2[:], in_=pp[:, :], func=AF.Square, accum_out=s8[:, o:o + 1]), reads=[b_pp], writes=[b_jk2, b_s8])
            S.op("act", lambda e, s8=s8, o=o: e.activation(out=s8[:, o + 1:o + 2], in_=s8[:, o:o + 1], func=AF.Sqrt, scale=1.0 / QR, bias=eps[:, 0:1]), reads=[b_s8, b_eps], writes=[b_s8])
            S.op("dve", lambda e, s8=s8, o=o: e.reciprocal(out=s8[:, o + 2:o + 3], in_=s8[:, o + 1:o + 2]), reads=[b_s8], writes=[b_s8])
            S.op("dve", lambda e, pp=pp, s8=s8, o=o, gt=gt, cn=cn, which=which: e.scalar_tensor_tensor(out=cn[:, which, :], in0=pp[:, :], scalar=s8[:, o + 2:o + 3], in1=gt[:], op0=ALU.mult, op1=ALU.mult),
                 reads=[b_pp, b_s8, b_g], writes=[b_cn])
        pb, b_pb = self.pb[t % 2], self.pbb[t % 2]
        for j in range(8):
            S.op("pe", lambda e, j=j, cn=cn, pb=pb: e.transpose(out=pb[:, j * 128:(j + 1) * 128], in_=cn[:, j // 4, (j % 4) * 128:(j % 4 + 1) * 128], identity=self.ident[:]),
                 reads=[b_cn], writes=[b_pb], signal=(j == 7))
        S.op("act", lambda e, pb=pb, t=t: e.copy(out=cqnT[:, :, t * 128:(t + 1) * 128], in_=pb[:, 0:512].rearrange("p (k t) -> p k t", k=4)), reads=[b_pb], writes=[b_cq])
        S.op("dve", lambda e, pb=pb, t=t: e.tensor_copy(out=ckvT[:, :, t * 128:(t + 1) * 128], in_=pb[:, 512:1024].rearrange("p (k t) -> p k t", k=4)), reads=[b_pb], writes=[b_kv])
        if "k" in SK:
            continue
        r, b_r = rt[t % 2]
        S.op("dve", lambda e, r=r, t=t: e.tensor_tensor(out=r[:, 0, :], in0=self.pf[4][0:64, 0:128], in1=cosT[:, t * 128:(t + 1) * 128], op=ALU.mult), reads=[self.pfb[4]], writes=[b_r])
        S.op("dve", lambda e, r=r, t=t: e.tensor_tensor(out=r[:, 1, :], in0=self.pf[5][0:64, 0:128], in1=sinT[:, t * 128:(t + 1) * 128], op=ALU.mult), reads=[self.pfb[5]], writes=[b_r])
        S.op("dve", lambda e, r=r, t=t: e.tensor_tensor(out=krT[:, t * 128:(t + 1) * 128], in0=r[:, 0, :], in1=r[:, 1, :], op=ALU.add), reads=[b_r], writes=[b_kr])

    if self.cfg.get("mla_stop") == "p1":
        return
    S.barrier()
    self.mem.release(keep)
    NB = [(0, CTX)] + [(CTX + 512 * b, 512) for b in range(8)]
    hb = []
    for j in range(2):
        hb.append({
            "wuq": self.sb("wuq%d" % j, [128, 4, 192], BF16), "wukv": self.sb("wukv%d" % j, [128, 4, 256], BF16),
            "wqr": self.sb("wqr%d" % j, [128, 4, 64], BF16),
            "qn": self.sb("qn%d" % j, [128, NTOK], BF16), "qr": self.sb("qr%d" % j, [64, NTOK], BF16),
            "kn": self.sb("kn%d" % j, [128, NTOK], BF16), "v": self.sb("v%d" % j, [128, NT, 129], BF16),
            "oT": self.sb("oTh%d" % j, [128, NTOK], BF16),
            "b_w": Buf("w"), "b_wqr": Buf("wqr"), "b_qn": Buf("qn"), "b_qr": Buf("qr"), "b_kn": Buf("kn"), "b_v": Buf("v"), "b_oT": Buf("oT"),
        })
        S.op("dve", lambda e, v=hb[j]["v"]: e.memset(v[:, :, 128:129], 1.0), writes=[hb[j]["b_v"]])
    pts = [(self.sb("pt%d" % j, [128, 512], BF16), Buf("pt")) for j in range(3)]
    ons = [(self.sb("on%d" % j, [128, 128], BF16), Buf("on")) for j in range(4)]
    rss = [(self.sb("rs%d" % j, [128, 1], F32), Buf("rs")) for j in range(4)]
    rq = [(self.sb("rq%d" % j, [64, 2, 512], F32), Buf("rq")) for j in range(2)]
    PREP = 7
    zl = self.sb("zl", [128, 128], BF16)
    zr = self.sb("zr", [128, 258], BF16)
    b_z = Buf("z")
    S.op("dve", lambda e: e.memset(zl[:], 0.0), writes=[b_z])
    S.op("dve", lambda e: e.memset(zr[:], 0.0), writes=[b_z])

    def load_head_w(h):
        d = hb[h % 2]
        S.dma("pool", lambda e: e.dma_start(out=d["wuq"][:], in_=self.W("mla_w_uq", i)[:, h * 192:(h + 1) * 192].rearrange("(k p) n -> p k n", p=128)), writes=[d["b_w"]])
        S.dma("pool", lambda e: e.dma_start(out=d["wukv"][:], in_=self.W("mla_w_ukv", i)[:, h * 256:(h + 1) * 256].rearrange("(k p) n -> p k n", p=128)), writes=[d["b_w"]])
        self.rot_weights(None, d["wuq"][:, :, 128:192].rearrange("p k (q f) -> p k q f", q=4), d["wqr"][:].rearrange("p k (q f) -> p k q f", q=4), [d["b_w"]], [d["b_wqr"]])

    def prep_groups(h):
        d = hb[h % 2]
        gl = []
        pp, b_pp = self.pf[PREP], self.pfb[PREP]
        rqi = [0]
        for (c0, n) in NB:
            def g_qn(c0=c0, n=n):
                for k in range(4):
                    S.op("pe", lambda e, k=k: e.matmul(pp[:, 0:n], lhsT=d["wuq"][:, k, 0:128], rhs=cqnT[:, k, c0:c0 + n], start=(k == 0), stop=(k == 3)),
                         reads=[d["b_w"]], writes=[b_pp], signal=(k == 3))
                S.op("dve", lambda e: e.tensor_copy(out=d["qn"][:, c0:c0 + n], in_=pp[:, 0:n]), reads=[b_pp], writes=[d["b_qn"]])
            def g_kn(c0=c0, n=n):
                for k in range(4):
                    S.op("pe", lambda e, k=k: e.matmul(pp[:, 0:n], lhsT=d["wukv"][:, k, 0:128], rhs=ckvT[:, k, c0:c0 + n], start=(k == 0), stop=(k == 3)),
                         reads=[d["b_w"]], writes=[b_pp], signal=(k == 3))
                S.op("dve", lambda e: e.tensor_copy(out=d["kn"][:, c0:c0 + n], in_=pp[:, 0:n]), reads=[b_pp], writes=[d["b_kn"]])
            def g_qr(c0=c0, n=n):
                r, b_r = rq[rqi[0] % 2]
                rqi[0] += 1
                for half, wt, bw, tab in ((0, d["wuq"], d["b_w"], cosT), (1, d["wqr"], d["b_wqr"], sinT)):
                    for k in range(4):
                        lhs = wt[:, k, 128:192] if half == 0 else wt[:, k, :]
                        S.op("pe", lambda e, k=k, lhs=lhs: e.matmul(pp[0:64, 0:n], lhsT=lhs, rhs=cqnT[:, k, c0:c0 + n], start=(k == 0), stop=(k == 3)),
                             reads=[bw], writes=[b_pp], signal=(k == 3))
                    S.op("dve", lambda e, half=half, tab=tab: e.tensor_tensor(out=r[:, half, 0:n], in0=pp[0:64, 0:n], in1=tab[:, c0:c0 + n], op=ALU.mult), reads=[b_pp], writes=[b_r])
                S.op("dve", lambda e: e.tensor_tensor(out=d["qr"][:, c0:c0 + n], in0=r[:, 0, 0:n], in1=r[:, 1, 0:n], op=ALU.add), reads=[b_r], writes=[d["b_qr"]])
            gl += [g_qn, g_kn, g_qr]
        for t0 in range(0, NT, 4):
            nt = min(4, NT - t0)
            def g_v(t0=t0, nt=nt):
                for tt in range(nt):
                    t = t0 + tt
                    for k in range(4):
                        S.op("pe", lambda e, k=k, t=t, tt=tt: e.matmul(pp[:, tt * 128:(tt + 1) * 128], lhsT=ckvT[:, k, t * 128:(t + 1) * 128], rhs=d["wukv"][:, k, 128:256],
                                                                    start=(k == 0), stop=(k == 3)),
                             reads=[d["b_w"]], writes=[b_pp], signal=(k == 3 and tt == nt - 1))
                S.op("dve", lambda e: e.tensor_copy(out=d["v"][:, t0:t0 + nt, 0:128], in_=pp[:, 0:nt * 128].rearrange("p (t c) -> p t c", c=128)), reads=[b_pp], writes=[d["b_v"]])
            gl.append(g_v)
        return gl

    load_head_w(0)
    for g in prep_groups(0):
        g()
    qblocks = ([] if last else [(0, CTX, [0, 1])]) + [(CTX + 512 * b, 512, list(range(NT))) for b in range(8)]
    sit = 0
    for h in range(MLA_H):
        d = hb[h % 2]
        pend = []
        if h + 1 < MLA_H:
            load_head_w(h + 1)
            pend = prep_groups(h + 1)
        iters = [(qi, q0, nq, kt, kts) for qi, (q0, nq, kts) in enumerate(qblocks) for kt in kts]
        n_it = len(iters)
        every = max(1, n_it // (len(pend) + 1)) if pend else 0

        def emit_S(it, d=d, iters=iters):
            qi, q0, nq, kt, kts = iters[it]
            ps, b_ps = self.pf[it % 2], self.pfb[it % 2]
            S.op("pe", lambda e: e.matmul(ps[:, 0:nq], lhsT=d["kn"][:, kt * 128:(kt + 1) * 128], rhs=d["qn"][:, q0:q0 + nq], start=True, stop=False),
                 reads=[d["b_kn"], d["b_qn"]], writes=[b_ps], signal=False)
            S.op("pe", lambda e: e.matmul(ps[:, 0:nq], lhsT=krT[:, kt * 128:(kt + 1) * 128], rhs=d["qr"][:, q0:q0 + nq], start=False, stop=True),
                 reads=[d["b_qr"]], writes=[b_ps], signal=True)
            pt, b_pt = pts[it % 3]
            S.op("act", lambda e: e.activation(out=pt[:, 0:nq], in_=ps[:, 0:nq], func=AF.Exp, scale=SCALE), reads=[b_ps], writes=[b_pt])

        def emit_PV(it, d=d, iters=iters):
            qi, q0, nq, kt, kts = iters[it]
            pt, b_pt = pts[it % 3]
            base = 2 + 2 * (qi % 2)
            nj = nq // 128
            if kt == kts[0]:
                for bk in range((nj + 1) // 2):
                    po, b_po = self.pf[base + bk], self.pfb[base + bk]
                    S.op("pe", lambda e, po=po: e.matmul(po[:, 0:258], lhsT=zl[:], rhs=zr[:], start=True, stop=False), reads=[b_z], writes=[b_po], signal=False)
            for j in range(nj):
                po, b_po = self.pf[base + j // 2], self.pfb[base + j // 2]
                lastmm = (kt == kts[-1]) and (j % 2 == 1 or j == nj - 1)
                S.op("pe", lambda e, j=j, po=po, lastmm=lastmm: e.matmul(po[:, (j % 2) * 129:(j % 2) * 129 + 129], lhsT=pt[:, j * 128:(j + 1) * 128], rhs=d["v"][:, kt, :],
                                                                        start=False, stop=lastmm),
                     reads=[b_pt, d["b_v"]], writes=[b_po], signal=(j == nj - 1))
            if kt == kts[-1]:
                pT, b_pT = self.pb[0], self.pbb[0]
                for j in range(nq // 128):
                    po, b_po = self.pf[base + j // 2], self.pfb[base + j // 2]
                    (on, b_on), (rs, b_rs) = ons[j], rss[j]
                    c = (j % 2) * 129
                    S.op("dve", lambda e, po=po, c=c, rs=rs: e.reciprocal(out=rs[:], in_=po[:, c + 128:c + 129]), reads=[b_po], writes=[b_rs])
                    S.op("act", lambda e, po=po, c=c, rs=rs, on=on: e.activation(out=on[:], in_=po[:, c:c + 128], func=AF.Copy, scale=rs[:, 0:1]), reads=[b_po, b_rs], writes=[b_on])
                    S.op("pe", lambda e, j=j, on=on: e.transpose(out=pT[:, j * 128:(j + 1) * 128], in_=on[:], identity=self.ident[:]), reads=[b_on], writes=[b_pT], signal=True)
                S.op("dve", lambda e: e.tensor_copy(out=d["oT"][:, q0:q0 + nq], in_=pT[:, 0:nq]), reads=[b_pT], writes=[d["b_oT"]])

        emit_S(0)
        for it in range(n_it):
            if it + 1 < n_it:
                emit_S(it + 1)
            emit_PV(it)
            if pend and (it % every == every - 1):
                pend.pop(0)()
        while pend:
            pend.pop(0)()
        c0 = CTX if last else 0
        S.dma("sp", lambda e, h=h, d=d, c0=c0: e.dma_start(out=self.oT_scr[h, :, c0:NTOK], in_=d["oT"][:, c0:NTOK]), reads=[d["b_oT"]], writes=[])

    if self.cfg.get("mla_stop") == "p2":
        return
    self.phase_begin()
    wo = [self.sb("wo%d" % j, [128, MLA_H, 512], BF16) for j in range(4)]
    b_wo = [Buf("wo") for j in range(4)]
    for nb in range(4):
        S.dma("pool", lambda e, nb=nb: e.dma_start(out=wo[nb][:], in_=self.W("mla_w_o", i)[:, nb * 512:(nb + 1) * 512].rearrange("(k p) n -> p k n", p=128)), writes=[b_wo[nb]])
    g1 = self.sb("g1", [128, D], F32)
    b_g1 = Buf("g1")
    ob = [(self.sb("ob%d" % j, [128, MLA_H, 512], BF16), Buf("ob")) for j in range(2)]
    xts = [(self.sb("xr%d" % j, [128, D], F32), Buf("xr")) for j in range(3)]
    tps = [(self.sb("tq%d" % j, [128, 512], F32), Buf("tq")) for j in range(2)]
    blocks = ([] if last else [(0, CTX, 1)]) + [(CTX + 512 * b, 512, 0) for b in range(8)]
    cur_who = None
    pit = 0
    for bi, (c0, n, who) in enumerate(blocks):
        if who != cur_who:
            cur_who = who
            S.dma("sp", lambda e, who=who: e.dma_start(out=g1[:], in_=bcast_ap(self.modrows[l, who:who + 1, 2 * D:3 * D], 128)), writes=[b_g1])
        o, b_o = ob[bi % 2]
        S.dma("sp", lambda e, o=o, c0=c0, n=n: e.dma_start(out=o[:, :, 0:n], in_=self.oT_scr[:, :, c0:c0 + n].rearrange("h p t -> p h t")), writes=[b_o])
        for tj in range(n // 128):
            xt, b_xt = xts[pit % 3]
            r0 = c0 + tj * 128
            S.dma("sp", lambda e, xt=xt, r0=r0: e.dma_start(out=xt[:], in_=self.res[r0:r0 + 128, :]), writes=[b_xt])
            for nb in range(4):
                pp, b_pp = self.pf[pit % 4], self.pfb[pit % 4]
                tq, b_tq = tps[pit % 2]
                pit += 1
                for hh in range(MLA_H):
                    S.op("pe", lambda e, hh=hh, o=o, tj=tj, nb=nb, pp=pp: e.matmul(pp[:, :], lhsT=o[:, hh, tj * 128:(tj + 1) * 128], rhs=wo[nb][:, hh, :], start=(hh == 0), stop=(hh == MLA_H - 1)),
                         reads=[b_o, b_wo[nb]], writes=[b_pp], signal=(hh == MLA_H - 1))
                S.op("dve", lambda e, pp=pp, tq=tq, nb=nb: e.tensor_tensor(out=tq[:], in0=pp[:, :], in1=g1[:, nb * 512:(nb + 1) * 512], op=ALU.mult), reads=[b_pp, b_g1], writes=[b_tq])
                S.op("pool", lambda e, tq=tq, xt=xt, nb=nb: e.tensor_tensor(out=xt[:, nb * 512:(nb + 1) * 512], in0=tq[:], in1=xt[:, nb * 512:(nb + 1) * 512], op=ALU.add), reads=[b_tq, b_xt], writes=[b_xt])
            S.dma("sp", lambda e, xt=xt, r0=r0: e.dma_start(out=self.res[r0:r0 + 128, :], in_=xt[:]), reads=[b_xt], writes=[])


K.mla = _mla


def _ret_tables(self):
    S = self.S
    cosR = self.sb("cosR", [128, NTOK], BF16)
    sinR = self.sb("sinR", [128, NTOK], BF16)
    b_tab = Buf("tabR")
    m = self.mem.mark()
    pidx = self.sb("pidx", [128, 1], I32)
    inv = self.sb("inv", [128, 2], F32)
    posi = self.sb("posi", [128, SEQ], I32)
    ang = self.sb("ang", [128, SEQ], F32)
    tA = self.sb("tA", [128, SEQ], F32)
    tB = self.sb("tB", [128, SEQ], F32)
    tI = self.sb("tI", [128, SEQ], I32)
    b_i, b_ang = Buf("inv"), Buf("ang")
    S.op("pool", lambda e: e.iota(pidx[:], pattern=[[0, 1]], base=0, channel_multiplier=1), writes=[b_i])
    S.op("dve", lambda e: e.tensor_copy(out=inv[:, 0:1], in_=pidx[:]), reads=[b_i], writes=[b_i])
    S.op("act", lambda e: e.activation(out=inv[:, 1:2], in_=inv[:, 0:1], func=AF.Exp, scale=-math.log(10000.0) / 128.0), reads=[b_i], writes=[b_i])
    S.op("pool", lambda e: e.iota(posi[:, :], pattern=[[1, SEQ]], base=0, channel_multiplier=0), writes=[b_ang])
    S.op("dve", lambda e: e.tensor_copy(out=ang[:], in_=posi[:]), reads=[b_ang], writes=[b_ang])
    S.op("dve", lambda e: e.tensor_scalar(out=ang[:], in0=ang[:], scalar1=inv[:, 1:2], scalar2=None, op0=ALU.mult), reads=[b_ang, b_i], writes=[b_ang])
    self.gen_sincos(ang, b_ang, 128, SEQ, cosR[:, CTX:NTOK], sinR[:, CTX:NTOK], b_tab, tA, tB, tI, b_ang)
    S.op("dve", lambda e: e.memset(cosR[:, 0:CTX], 1.0), writes=[b_tab])
    S.op("dve", lambda e: e.memset(sinR[:, 0:CTX], 0.0), writes=[b_tab])
    self.mem.release(m)
    return cosR, sinR


K.ret_tables = _ret_tables


def _ret(self, l, last):
    S, nc = self.S, self.nc
    i = l // 2
    NBK = [(0, CTX)] + [(CTX + 512 * b, 512) for b in range(8)]
    if not hasattr(self, "v_scr"):
        self.v_scr = nc.dram_tensor("v_scr", [NTOK, 2 * D], BF16).ap()
        self.gf_scr = nc.dram_tensor("gf_scr", [NTOK, 2 * D], BF16).ap()
        self.gb_scr = nc.dram_tensor("gb_scr", [NTOK, 2 * D], BF16).ap()
        self.gated_scr = nc.dram_tensor("gated_scr", [NTOK, 2 * D], BF16).ap()
        self.qT_scr = nc.dram_tensor("qT_scr", [KD, 128, NTOK], BF16).ap()
        self.kT_scr = nc.dram_tensor("kT_scr", [KD, 128, NTOK], BF16).ap()
    self.make_hT(l, 1, 0, skip_ctx=False)
    if self.cfg.get("ret_stop") == "r0":
        return

    self.phase_begin()
    hT = self.sb("hTall", [128, KD, NTOK], BF16)
    b_hT = Buf("hTall")
    for k4 in range(4):
        S.dma("sp", lambda e, k4=k4: e.dma_start(out=hT[:, k4 * 4:(k4 + 1) * 4, :], in_=self.h2T[:, k4 * 4:(k4 + 1) * 4, :]), writes=[b_hT])
    wbuf = [(self.sb("wb%d" % j, [128, KD, 512], BF16), Buf("wb")) for j in range(2)]
    stg = [(self.sb("stg%d" % j, [128, 512], BF16), Buf("stg")) for j in range(4)]
    it = 0
    pit = 0
    for (wname, dst, silu) in (("ret_w_v", self.v_scr, False), ("ret_w_gf", self.gf_scr, True), ("ret_w_gb", self.gb_scr, True)):
        for nb in range(8):
            w, b_w = wbuf[it % 2]
            it += 1
            S.dma("pool", lambda e, w=w, wname=wname, nb=nb: e.dma_start(out=w[:], in_=self.W(wname, i)[:, nb * 512:(nb + 1) * 512].rearrange("(k p) n -> p k n", p=128)), writes=[b_w])
            for t in range(NT):
                pp, b_pp = self.pf[pit % 4], self.pfb[pit % 4]
                st, b_st = stg[pit % 4]
                pit += 1
                for k in range(KD):
                    S.op("pe", lambda e, k=k, t=t, w=w, pp=pp: e.matmul(pp[:, :], lhsT=hT[:, k, t * 128:(t + 1) * 128], rhs=w[:, k, :], start=(k == 0), stop=(k == KD - 1)),
                         reads=[b_hT, b_w], writes=[b_pp], signal=(k == KD - 1))
                if silu:
                    S.op("act", lambda e, pp=pp, st=st: e.activation(out=st[:], in_=pp[:, :], func=AF.Silu), reads=[b_pp], writes=[b_st])
                else:
                    S.op("dve", lambda e, pp=pp, st=st: e.tensor_copy(out=st[:], in_=pp[:, :]), reads=[b_pp], writes=[b_st])
                S.dma("sp", lambda e, st=st, dst=dst, t=t, nb=nb: e.dma_start(out=dst[t * 128:(t + 1) * 128, nb * 512:(nb + 1) * 512], in_=st[:]), reads=[b_st], writes=[])
    if self.cfg.get("ret_stop") == "r1b":
        return

    self.phase_begin()
    cosR, sinR = self.ret_tables()
    S.barrier()
    w = self.sb("wqk", [128, KD, D], BF16)
    b_wqk = Buf("wqk")
    hbs = [(self.sb("hb%d" % j, [128, KD, 512], BF16), Buf("hb")) for j in range(2)]
    sgs = [(self.sb("sg%d" % j, [128, KD, 512], BF16), Buf("sg")) for j in range(2)]
    rts = [[(self.sb("rt%d_%d" % (j, q), [128, 512], F32), Buf("rt")) for q in range(4)] for j in range(2)]
    bi = 0
    hi = 0
    for (wname, dst) in (("ret_w_q", self.qT_scr), ("ret_w_k", self.kT_scr)):
        for c4 in range(4):
            S.dma("pool", lambda e, wname=wname, c4=c4: e.dma_start(out=w[:, :, c4 * 512:(c4 + 1) * 512], in_=self.W(wname, i)[:, c4 * 512:(c4 + 1) * 512].rearrange("(k p) n -> p k n", p=128)), writes=[b_wqk])
        for (c0, n) in NBK:
            hb, b_hb = hbs[bi % 2]
            sg, b_sg = sgs[bi % 2]
            bi += 1
            S.dma("sp", lambda e, hb=hb, c0=c0, n=n: e.dma_start(out=hb[:, :, 0:n], in_=self.h2T[:, :, c0:c0 + n]), writes=[b_hb])
            for h in range(RET_H):
                pA, b_pA = self.pf[(hi % 2) * 2], self.pfb[(hi % 2) * 2]
                pB, b_pB = self.pf[(hi % 2) * 2 + 1], self.pfb[(hi % 2) * 2 + 1]
                r = rts[hi % 2]
                hi += 1
                for (pp, b_pp, off) in ((pA, b_pA, 0), (pB, b_pB, 128)):
                    for k in range(KD):
                        S.op("pe", lambda e, k=k, pp=pp, hb=hb, h=h, off=off, n=n: e.matmul(pp[:, 0:n], lhsT=w[:, k, h * 256 + off:h * 256 + off + 128], rhs=hb[:, k, 0:n],
                                                                                              start=(k == 0), stop=(k == KD - 1)),
                             reads=[b_wqk, b_hb], writes=[b_pp], signal=(k == KD - 1))
                (t1, b1), (t2, b2), (t3, b3), (t4, b4) = r
                S.op("dve", lambda e, t1=t1, pA=pA, c0=c0, n=n: e.tensor_tensor(out=t1[:, 0:n], in0=pA[:, 0:n], in1=cosR[:, c0:c0 + n], op=ALU.mult), reads=[b_pA], writes=[b1])
                S.op("dve", lambda e, t2=t2, pB=pB, c0=c0, n=n: e.tensor_tensor(out=t2[:, 0:n], in0=pB[:, 0:n], in1=sinR[:, c0:c0 + n], op=ALU.mult), reads=[b_pB], writes=[b2])
                S.op("dve", lambda e, t3=t3, pB=pB, c0=c0, n=n: e.tensor_tensor(out=t3[:, 0:n], in0=pB[:, 0:n], in1=cosR[:, c0:c0 + n], op=ALU.mult), reads=[b_pB], writes=[b3])
                S.op("dve", lambda e, t4=t4, pA=pA, c0=c0, n=n: e.tensor_tensor(out=t4[:, 0:n], in0=pA[:, 0:n], in1=sinR[:, c0:c0 + n], op=ALU.mult), reads=[b_pA], writes=[b4])
                S.op("pool", lambda e, t1=t1, t2=t2, sg=sg, h=h, n=n: e.tensor_tensor(out=sg[:, 2 * h, 0:n], in0=t1[:, 0:n], in1=t2[:, 0:n], op=ALU.subtract), reads=[b1, b2], writes=[b_sg])
                S.op("pool", lambda e, t3=t3, t4=t4, sg=sg, h=h, n=n: e.tensor_tensor(out=sg[:, 2 * h + 1, 0:n], in0=t3[:, 0:n], in1=t4[:, 0:n], op=ALU.add), reads=[b3, b4], writes=[b_sg])
            S.dma("sp", lambda e, sg=sg, dst=dst, c0=c0, n=n: e.dma_start(out=dst[:, :, c0:c0 + n].rearrange("c p t -> p c t"), in_=sg[:, :, 0:n]), reads=[b_sg], writes=[])
    if self.cfg.get("ret_stop") == "r1a":
        return

    self.phase_begin()
    NCH = NT
    lg = self.sb("lg", [128, 16], F32)
    cd = self.sb("cd", [128, 16], F32)
    b_c = Buf("rc")
    S.dma("sp", lambda e: e.dma_start(out=lg[:, 0:8], in_=bcast_ap(self.W("ret_decay_f", i), 128)), writes=[b_c])
    S.dma("sp", lambda e: e.dma_start(out=lg[:, 8:16], in_=bcast_ap(self.W("ret_decay_b", i), 128)), writes=[b_c])
    S.op("act", lambda e: e.activation(out=lg[:], in_=lg[:], func=AF.Exp), reads=[b_c], writes=[b_c])
    S.op("dve", lambda e: e.tensor_scalar(out=lg[:], in0=lg[:], scalar1=-1.0, scalar2=None, op0=ALU.mult), reads=[b_c], writes=[b_c])
    S.op("act", lambda e: e.activation(out=cd[:], in_=lg[:], func=AF.Exp, scale=128.0), reads=[b_c], writes=[b_c])
    di = self.sb("di", [128, 128], I32)
    d1 = self.sb("d1", [128, 128], F32)
    ef = self.sb("ef", [128, 2, 128], F32)
    ind = self.sb("ind", [128, 2, 128], F32)
    xr = self.sb("xr", [128, 2, 128], F32)
    zc = self.sb("zc", [128, 2], F32)
    zi = self.sb("zi", [128, 2], I32)
    S.op("pool", lambda e: e.iota(di[:], pattern=[[1, 128]], base=0, channel_multiplier=-1), writes=[b_c])
    S.op("dve", lambda e: e.tensor_copy(out=d1[:], in_=di[:]), reads=[b_c], writes=[b_c])
    S.op("dve", lambda e: e.tensor_scalar(out=ef[:, 0, :], in0=d1[:], scalar1=0.0, scalar2=None, op0=ALU.max), reads=[b_c], writes=[b_c])
    S.op("dve", lambda e: e.tensor_scalar(out=ef[:, 1, :], in0=d1[:], scalar1=-1.0, scalar2=0.0, op0=ALU.mult, op1=ALU.max), reads=[b_c], writes=[b_c])
    S.op("dve", lambda e: e.tensor_scalar(out=ind[:, 0, :], in0=d1[:], scalar1=0.0, scalar2=1.0 / 16.0, op0=ALU.is_ge, op1=ALU.mult), reads=[b_c], writes=[b_c])
    S.op("dve", lambda e: e.tensor_scalar(out=ind[:, 1, :], in0=d1[:], scalar1=0.0, scalar2=1.0 / 16.0, op0=ALU.is_le, op1=ALU.mult), reads=[b_c], writes=[b_c])
    S.op("pool", lambda e: e.iota(di[:], pattern=[[1, 128]], base=1, channel_multiplier=0), reads=[b_c], writes=[b_c])
    S.op("dve", lambda e: e.tensor_copy(out=xr[:, 0, :], in_=di[:]), reads=[b_c], writes=[b_c])
    S.op("dve", lambda e: e.tensor_scalar(out=xr[:, 1, :], in0=xr[:, 0, :], scalar1=-1.0, scalar2=129.0, op0=ALU.mult, op1=ALU.add), reads=[b_c], writes=[b_c])
    S.op("pool", lambda e: e.iota(zi[:, 0:1], pattern=[[0, 1]], base=127, channel_multiplier=-1), reads=[b_c], writes=[b_c])
    S.op("pool", lambda e: e.iota(zi[:, 1:2], pattern=[[0, 1]], base=0, channel_multiplier=1), reads=[b_c], writes=[b_c])
    S.op("dve", lambda e: e.tensor_copy(out=zc[:], in_=zi[:]), reads=[b_c], writes=[b_c])
    mask = self.sb("mask", [128, 16, 128], F32)
    xi = self.sb("xi", [128, 16, 128], F32)
    zeta = self.sb("zeta", [128, 16], F32)
    tmpm = self.sb("tmpm", [128, 128], F32)
    for h in range(RET_H):
        for dr in range(2):
            col = dr * 8 + h
            hd = h * 2 + dr
            S.op("act", lambda e, dr=dr, col=col: e.activation(out=tmpm[:], in_=ef[:, dr, :], func=AF.Exp, scale=lg[:, col:col + 1]), reads=[b_c], writes=[b_c])
            S.op("dve", lambda e, dr=dr, hd=hd: e.tensor_tensor(out=mask[:, hd, :], in0=tmpm[:], in1=ind[:, dr, :], op=ALU.mult), reads=[b_c], writes=[b_c])
            S.op("act", lambda e, dr=dr, col=col, hd=hd: e.activation(out=xi[:, hd, :], in_=xr[:, dr, :], func=AF.Exp, scale=lg[:, col:col + 1]), reads=[b_c], writes=[b_c])
            S.op("act", lambda e, dr=dr, col=col, hd=hd: e.activation(out=zeta[:, hd:hd + 1], in_=zc[:, dr:dr + 1], func=AF.Exp, scale=lg[:, col:col + 1]), reads=[b_c], writes=[b_c])
    S.op("dve", lambda e: e.tensor_scalar(out=zeta[:], in0=zeta[:], scalar1=1.0 / 16.0, scalar2=None, op0=ALU.mult), reads=[b_c], writes=[b_c])
    epsg = self.sb("epsg", [128, 1], F32)
    S.op("dve", lambda e: e.memset(epsg[:], GN_EPS), reads=[b_c], writes=[b_c])
    S.barrier()
    qk = [{"q": self.sb("q%d" % j, [128, 2, NTOK], BF16), "k": self.sb("k%d" % j, [128, 2, NTOK], BF16), "b": Buf("qk")} for j in range(2)]
    vh = self.sb("vh", [128, NCH, 512], BF16)
    b_vh = Buf("vh")
    gated = self.sb("gated", [128, NCH, 512], BF16)
    b_gt = [Buf("gt%d" % c) for c in range(NCH)]
    S32 = [self.sb("S32_%d" % dr, [128, 2, 512], F32) for dr in range(2)]
    S16 = [self.sb("S16_%d" % dr, [128, 2, 512], BF16) for dr in range(2)]
    b_S = [Buf("S0"), Buf("S1")]
    sTm = [[(self.sb("sTm%d_%d" % (dr, j), [128, 128], BF16), Buf("sTm")) for j in range(2)] for dr in range(2)]
    kz = [[(self.sb("kz%d_%d" % (dr, j), [128, 256], BF16), Buf("kz")) for j in range(2)] for dr in range(2)]
    qx = [[(self.sb("qx%d_%d" % (dr, j), [128, 2, 128], BF16), Buf("qx")) for j in range(2)] for dr in range(2)]
    gt_in = [[(self.sb("gi%d_%d" % (dr, j), [128, 512], BF16), Buf("gi")) for j in range(2)] for dr in range(2)]
    nrm = [[(self.sb("nr%d_%d" % (dr, j), [128, 512], BF16), Buf("nr")) for j in range(2)] for dr in range(2)]
    gtmp = [(self.sb("gtmp%d" % j, [128, 512], BF16), Buf("gtmp")) for j in range(2)]
    st6 = [[(self.sb("st%d_%d" % (dr, j), [128, 16], F32), Buf("st")) for j in range(2)] for dr in range(2)]

    def load_qk(h):
        d = qk[h % 2]
        S.dma("sp", lambda e: e.dma_start(out=d["q"][:], in_=self.qT_scr[2 * h:2 * h + 2, :, :].rearrange("c p t -> p c t")), writes=[d["b"]])
        S.dma("sp", lambda e: e.dma_start(out=d["k"][:], in_=self.kT_scr[2 * h:2 * h + 2, :, :].rearrange("c p t -> p c t")), writes=[d["b"]])

    order = [list(range(NCH)), [1, 0] + list(range(NCH - 1, 1, -1))]
    load_qk(0)
    cnt = [0, 0]
    for h in range(RET_H):
        d = qk[h % 2]
        if h + 1 < RET_H:
            load_qk(h + 1)
        S.dma("sp", lambda e, h=h: e.dma_start(out=vh[:], in_=self.v_scr[:, h * 512:(h + 1) * 512].rearrange("(c p) n -> p c n", p=128)), writes=[b_vh])
        arrived = [0] * NCH
        for step in range(NCH):
            for dr in range(2):
                c = order[dr][step]
                first = (step == 0)
                lastst = (step == NCH - 1)
                is_ctx = c < 2
                need_y = not (last and is_ctx)
                hd = h * 2 + dr
                j2 = cnt[dr] % 2
                cnt[dr] += 1
                cs = slice(c * 128, (c + 1) * 128)
                bA, bY, bU0, bU1 = dr * 4, dr * 4 + 1, dr * 4 + 2, dr * 4 + 3
                if need_y:
                    (sm, b_sm) = sTm[dr][j2]
                    S.op("pe", lambda e, d=d, cs=cs, bA=bA: e.matmul(self.pf[bA][:, 0:128], lhsT=d["k"][:, 0, cs], rhs=d["q"][:, 0, cs], start=True, stop=False), reads=[d["b"]], writes=[self.pfb[bA]], signal=False)
                    S.op("pe", lambda e, d=d, cs=cs, bA=bA: e.matmul(self.pf[bA][:, 0:128], lhsT=d["k"][:, 1, cs], rhs=d["q"][:, 1, cs], start=False, stop=True), reads=[d["b"]], writes=[self.pfb[bA]], signal=True)
                    S.op("dve", lambda e, sm=sm, bA=bA, hd=hd: e.tensor_tensor(out=sm[:], in0=self.pf[bA][:, 0:128], in1=mask[:, hd, :], op=ALU.mult), reads=[self.pfb[bA]], writes=[b_sm])
                    if not first:
                        (qxt, b_qx) = qx[dr][j2]
                        for a in range(2):
                            S.op("pool", lambda e, d=d, qxt=qxt, a=a, cs=cs, hd=hd: e.tensor_tensor(out=qxt[:, a, :], in0=d["q"][:, a, cs], in1=xi[:, hd, :], op=ALU.mult), reads=[d["b"]], writes=[b_qx])
                    S.op("pe", lambda e, sm=sm, c=c, bY=bY, first=first: e.matmul(self.pf[bY][:, :], lhsT=sm[:], rhs=vh[:, c, :], start=True, stop=first), reads=[b_sm, b_vh], writes=[self.pfb[bY]], signal=first)
                    if not first:
                        for a in range(2):
                            S.op("pe", lambda e, qxt=qxt, a=a, bY=bY, dr=dr: e.matmul(self.pf[bY][:, :], lhsT=qxt[:, a, :], rhs=S16[dr][:, a, :], start=False, stop=(a == 1)),
                                 reads=[b_qx, b_S[dr]], writes=[self.pfb[bY]], signal=(a == 1))
                    (s6, b_s6) = st6[dr][j2]
                    S.op("dve", lambda e, s6=s6, bY=bY: e.bn_stats(out=s6[:, 0:6], in_=self.pf[bY][:, :]), reads=[self.pfb[bY]], writes=[b_s6])
                    S.op("dve", lambda e, s6=s6: e.bn_aggr(out=s6[:, 6:8], in_=s6[:, 0:6]), reads=[b_s6], writes=[b_s6])
                    S.op("act", lambda e, s6=s6: e.activation(out=s6[:, 8:9], in_=s6[:, 7:8], func=AF.Sqrt, bias=epsg[:, 0:1]), reads=[b_s6], writes=[b_s6])
                    S.op("dve", lambda e, s6=s6: e.reciprocal(out=s6[:, 9:10], in_=s6[:, 8:9]), reads=[b_s6], writes=[b_s6])
                    S.op("dve", lambda e, s6=s6: e.scalar_tensor_tensor(out=s6[:, 10:11], in0=s6[:, 6:7], scalar=-1.0, in1=s6[:, 9:10], op0=ALU.mult, op1=ALU.mult), reads=[b_s6], writes=[b_s6])
                    (nr, b_nr) = nrm[dr][j2]
                    S.op("act", lambda e, s6=s6, nr=nr, bY=bY: e.activation(out=nr[:], in_=self.pf[bY][:, :], func=AF.Identity, scale=s6[:, 9:10], bias=s6[:, 10:11]), reads=[self.pfb[bY], b_s6], writes=[b_nr])
                    (gi, b_gi) = gt_in[dr][j2]
                    gsrc = self.gf_scr if dr == 0 else self.gb_scr
                    S.dma("sp", lambda e, gi=gi, gsrc=gsrc, c=c, h=h: e.dma_start(out=gi[:], in_=gsrc[c * 128:(c + 1) * 128, h * 512:(h + 1) * 512]), writes=[b_gi])
                    if arrived[c] == 0:
                        S.op("dve", lambda e, gi=gi, nr=nr, c=c: e.tensor_tensor(out=gated[:, c, :], in0=gi[:], in1=nr[:], op=ALU.mult), reads=[b_gi, b_nr], writes=[b_gt[c]])
                    else:
                        (gm, b_gm) = gtmp[cnt[dr] % 2]
                        S.op("pool", lambda e, gi=gi, nr=nr, gm=gm: e.tensor_tensor(out=gm[:], in0=gi[:], in1=nr[:], op=ALU.mult), reads=[b_gi, b_nr], writes=[b_gm])
                        S.op("pool", lambda e, gm=gm, c=c: e.tensor_tensor(out=gated[:, c, :], in0=gated[:, c, :], in1=gm[:], op=ALU.add), reads=[b_gm], writes=[b_gt[c]])
                    arrived[c] += 1
                if not lastst:
                    (kzt, b_kz) = kz[dr][j2]
                    for a in range(2):
                        S.op("pe", lambda e, d=d, a=a, cs=cs, bA=bA: e.transpose(out=self.pf[bA][:].bitcast(BF16)[:, a * 128:(a + 1) * 128], in_=d["k"][:, a, cs], identity=self.ident[:]),
                             reads=[d["b"]], writes=[self.pfb[bA]], signal=(a == 1))
                    S.op("act", lambda e, kzt=kzt, bA=bA, hd=hd: e.activation(out=kzt[:], in_=self.pf[bA][:].bitcast(BF16)[:, 0:256], func=AF.Copy, scale=zeta[:, hd:hd + 1]), reads=[self.pfb[bA]], writes=[b_kz])
                    for a, bU in ((0, bU0), (1, bU1)):
                        S.op("pe", lambda e, kzt=kzt, a=a, bU=bU, c=c: e.matmul(self.pf[bU][:, :], lhsT=kzt[:, a * 128:(a + 1) * 128], rhs=vh[:, c, :], start=True, stop=True),
                             reads=[b_kz, b_vh], writes=[self.pfb[bU]], signal=True)
                        if first:
                            S.op("dve", lambda e, a=a, bU=bU, dr=dr: e.tensor_copy(out=S32[dr][:, a, :], in_=self.pf[bU][:, :]), reads=[self.pfb[bU]], writes=[b_S[dr]])
                        else:
                            S.op("dve", lambda e, a=a, bU=bU, dr=dr, hd=hd, h=h: e.scalar_tensor_tensor(out=S32[dr][:, a, :], in0=S32[dr][:, a, :], scalar=cd[:, dr * 8 + h:dr * 8 + h + 1], in1=self.pf[bU][:, :],
                                                                                                op0=ALU.mult, op1=ALU.add), reads=[self.pfb[bU]], writes=[b_S[dr]])
                        S.op("act", lambda e, a=a, dr=dr: e.copy(out=S16[dr][:, a, :], in_=S32[dr][:, a, :]), reads=[], writes=[b_S[dr]])
        c_lo = 2 if last else 0
        S.dma("sp", lambda e, h=h, c_lo=c_lo: e.dma_start(out=self.gated_scr[c_lo * 128:NTOK, h * 512:(h + 1) * 512].rearrange("(c p) n -> p c n", p=128), in_=gated[:, c_lo:NCH, :]),
              reads=b_gt[c_lo:], writes=[])
    if self.cfg.get("ret_stop") == "r2":
        return

    self.phase_begin()
    NK = 2 * D // 128
    wo = self.sb("wo", [128, NK, D], BF16)
    b_wo = [Buf("wo%d" % j) for j in range(4)]
    for nb in range(4):
        S.dma("pool", lambda e, nb=nb: e.dma_start(out=wo[:, :, nb * 512:(nb + 1) * 512], in_=self.W("ret_w_o", i)[:, nb * 512:(nb + 1) * 512].rearrange("(k p) n -> p k n", p=128)), writes=[b_wo[nb]])
    g1 = self.sb("g1", [128, D], F32)
    b_g1 = Buf("g1")
    gts = [(self.sb("gt%d" % j, [128, 2 * D], BF16), Buf("gt")) for j in range(2)]
    gTs = [(self.sb("gT%d" % j, [128, NK, 128], BF16), Buf("gT")) for j in range(2)]
    xts = [(self.sb("xr%d" % j, [128, D], F32), Buf("xr")) for j in range(2)]
    tps = [(self.sb("tq%d" % j, [128, 512], F32), Buf("tq")) for j in range(2)]
    cur_who = None
    pit = 0
    for ti, t in enumerate(range(2 if last else 0, NT)):
        who = 1 if t < 2 else 0
        if who != cur_who:
            cur_who = who
            S.dma("sp", lambda e, who=who: e.dma_start(out=g1[:], in_=bcast_ap(self.modrows[l, who:who + 1, 2 * D:3 * D], 128)), writes=[b_g1])
        (gt, b_gtl), (gT, b_gT), (xt, b_xt) = gts[ti % 2], gTs[ti % 2], xts[ti % 2]
        S.dma("sp", lambda e, gt=gt, t=t: e.dma_start(out=gt[:], in_=self.gated_scr[t * 128:(t + 1) * 128, :]), writes=[b_gtl])
        S.dma("sp", lambda e, xt=xt, t=t: e.dma_start(out=xt[:], in_=self.res[t * 128:(t + 1) * 128, :]), writes=[b_xt])
        for q8 in range(4):
            pb, b_pb = (self.pf[4 + q8 % 2][:].bitcast(BF16), self.pfb[4 + q8 % 2])
            for j in range(8):
                c = q8 * 8 + j
                S.op("pe", lambda e, gt=gt, c=c, j=j, pb=pb: e.transpose(out=pb[:, j * 128:(j + 1) * 128], in_=gt[:, c * 128:(c + 1) * 128], identity=self.ident[:]),
                     reads=[b_gtl], writes=[b_pb], signal=(j == 7))
            if q8 % 2 == 0:
                S.op("act", lambda e, gT=gT, q8=q8, pb=pb: e.copy(out=gT[:, q8 * 8:(q8 + 1) * 8, :], in_=pb.rearrange("p (k t) -> p k t", k=8)), reads=[b_pb], writes=[b_gT])
            else:
                S.op("dve", lambda e, gT=gT, q8=q8, pb=pb: e.tensor_copy(out=gT[:, q8 * 8:(q8 + 1) * 8, :], in_=pb.rearrange("p (k t) -> p k t", k=8)), reads=[b_pb], writes=[b_gT])
        for nb in range(4):
            pp, b_pp = self.pf[pit % 4], self.pfb[pit % 4]
            tq, b_tq = tps[pit % 2]
            pit += 1
            for c in range(NK):
                S.op("pe", lambda e, c=c, gT=gT, nb=nb, pp=pp: e.matmul(pp[:, :], lhsT=gT[:, c, :], rhs=wo[:, c, nb * 512:(nb + 1) * 512], start=(c == 0), stop=(c == NK - 1)),
                     reads=[b_gT, b_wo[nb]], writes=[b_pp], signal=(c == NK - 1))
            S.op("dve", lambda e, pp=pp, tq=tq, nb=nb: e.tensor_tensor(out=tq[:], in0=pp[:, :], in1=g1[:, nb * 512:(nb + 1) * 512], op=ALU.mult), reads=[b_pp, b_g1], writes=[b_tq])
            S.op("pool", lambda e, tq=tq, xt=xt, nb=nb: e.tensor_tensor(out=xt[:, nb * 512:(nb + 1) * 512], in0=tq[:], in1=xt[:, nb * 512:(nb + 1) * 512], op=ALU.add), reads=[b_tq, b_xt], writes=[b_xt])
        S.dma("sp", lambda e, xt=xt, t=t: e.dma_start(out=self.res[t * 128:(t + 1) * 128, :], in_=xt[:]), reads=[b_xt], writes=[])


K.ret = _ret
```

```python
import contextlib
import math
import numpy as np
import concourse.bass as bass
import concourse.mybir as mybir
from concourse.bass_utils import run_bass_kernel_spmd

F32 = mybir.dt.float32
BF16 = mybir.dt.bfloat16
I32 = mybir.dt.int32
AF = mybir.ActivationFunctionType
ALU = mybir.AluOpType
AX = mybir.AxisListType

D = 2048
SEQ = 4096
CTX = 256
NTOK = SEQ + CTX
NT = NTOK // 128
DEPTH = 4
FF = 5632
NFC = FF // 128
KD = D // 128
NMOD = 6
EPS = 1e-6
GN_EPS = 1e-5
MLA_H = 16
QR = 512
NOPE = 128
ROPE = 64
RET_H = 8
RDK = 256
RDV = 512

ENGS = ("pe", "act", "dve", "pool", "sp")
DMA_K = {"sp": 8, "pool": 16, "act": 4, "pe": 4, "dve": 4}
SB_BASE = 16640
SB_END = 229376


class Buf:
    __slots__ = ("name", "w", "r", "excl")

    def __init__(self, name="", excl=False):
        self.name = name
        self.w = None
        self.r = {}
        self.excl = excl


class Sched:
    def __init__(self):
        self.streams = {e: [] for e in ENGS}
        self.cnt = {e: 0 for e in ENGS}
        self.seen = {e: {} for e in ENGS}
        self.dma_i = {e: 0 for e in ENGS}
        self.dma_val = {}

    def _wait(self, eng, tok):
        if tok is None:
            return
        k, v = tok
        if k == eng and eng == "pe":
            return
        s = self.seen[eng]
        if s.get(k, 0) >= v:
            return
        s[k] = v
        self.streams[eng].append(("w", k, v))

    def _deps(self, eng, reads, writes):
        for b in reads:
            self._wait(eng, b.w)
        for b in writes:
            self._wait(eng, b.w)
            for k, v in b.r.items():
                self._wait(eng, (k, v))

    def _mark(self, tok, reads, writes):
        k, v = tok
        for b in reads:
            if b.r.get(k, 0) < v:
                b.r[k] = v
        for b in writes:
            b.w = tok
            b.r = {}

    def op(self, eng, fn, reads=(), writes=(), signal=True):
        if any(b.excl for b in reads):
            writes = list(writes) + [b for b in reads if b.excl and b not in writes]
            reads = [b for b in reads if not b.excl]
        self._deps(eng, reads, writes)
        if signal:
            self.cnt[eng] += 1
            tok = (eng, self.cnt[eng])
        else:
            tok = (eng, self.cnt[eng] + 1)
        self.streams[eng].append(("o", fn, signal))
        self._mark(tok, reads, writes)

    def dma(self, q, fn, reads=(), writes=()):
        i = self.dma_i[q]
        self.dma_i[q] += 1
        key = ("d", q, i % DMA_K[q])
        prev = self.dma_val.get(key, 0)
        if prev:
            self._wait(q, (key, prev))
        self._deps(q, reads, writes)
        val = prev + 16
        self.dma_val[key] = val
        self.streams[q].append(("d", fn, key))
        self._mark((key, val), reads, writes)

    def barrier(self):
        for e in ENGS:
            for key, v in self.dma_val.items():
                self._wait(e, (key, v))
            for e2 in ENGS:
                if e2 != e and self.cnt[e2]:
                    self._wait(e, (e2, self.cnt[e2]))
            if e != "pe" and self.cnt[e]:
                self._wait(e, (e, self.cnt[e]))

    def emit(self, nc):
        keys = [e for e in ENGS if self.cnt[e]] + list(self.dma_val.keys())
        with contextlib.ExitStack() as st:
            sems = {}
            for k in keys:
                nm = k if isinstance(k, str) else "d_%s_%d" % (k[1], k[2])
                sems[k] = st.enter_context(nc.semaphore("s_" + nm))
            block = st.enter_context(nc.Block())
            streams = self.streams

            def run(engname, eng):
                for it in streams[engname]:
                    t = it[0]
                    if t == "w":
                        eng.wait_ge(sems[it[1]], it[2])
                    elif t == "o":
                        ins = it[1](eng)
                        if it[2]:
                            ins.then_inc(sems[engname], 1)
                    else:
                        it[1](eng).then_inc(sems[it[2]], 16)

            block.tensor(lambda e: run("pe", e))
            block.scalar(lambda e: run("act", e))
            block.vector(lambda e: run("dve", e))
            block.gpsimd(lambda e: run("pool", e))
            block.sync(lambda e: run("sp", e))


class Mem:
    def __init__(self, nc):
        self.nc = nc
        self.base = SB_BASE
        self.off = SB_BASE
        self.n = 0

    def alloc(self, name, shape, dt):
        nbytes = int(np.prod(shape[1:])) * mybir.dt.size(dt)
        nbytes = (nbytes + 63) // 64 * 64
        assert self.off + nbytes <= SB_END, (name, self.off, nbytes)
        self.n += 1
        t = self.nc.alloc_sbuf_tensor_at("%s_%d" % (name, self.n), list(shape), dt, offset=self.off)
        self.off += nbytes
        return t

    def mark(self):
        return self.off

    def release(self, m):
        self.off = m


def bcast_ap(ap, nparts):
    inner = [list(x) for x in ap.ap]
    while len(inner) > 1 and inner[0][1] == 1:
        inner = inner[1:]
    return bass.AP(tensor=ap.tensor, offset=ap.offset, ap=[[0, nparts]] + inner)


class K:
    def __init__(self, cfg):
        self.cfg = cfg
        self.nc = nc = bass.Bass("TRN2", target_bir_lowering=False)
        self.S = Sched()
        self.mem = Mem(nc)
        self.din = {}

    def dram_in(self, name, shape, dt=F32):
        t = self.nc.dram_tensor(name, list(shape), dt, kind="ExternalInput").ap()
        self.din[name] = t
        return t

    SHAPES = {
        "x": [SEQ, D], "c": [1, D], "ctx": [CTX, D], "c_ctx": [1, D], "final_norm_g": [1, D],
        "mod_w": [D, NMOD * D], "mod_b": [1, NMOD * D], "norm_mix_g": [1, D], "norm_ffn_g": [1, D],
        "mla_w_dq": [D, QR], "mla_q_norm_g": [1, QR], "mla_w_uq": [QR, MLA_H * 192],
        "mla_w_dkv": [D, QR + ROPE], "mla_kv_norm_g": [1, QR], "mla_w_ukv": [QR, MLA_H * 256], "mla_w_o": [D, D],
        "ret_w_q": [D, D], "ret_w_k": [D, D], "ret_w_v": [D, 2 * D], "ret_w_gf": [D, 2 * D], "ret_w_gb": [D, 2 * D],
        "ret_w_o": [2 * D, D], "ret_decay_f": [1, RET_H], "ret_decay_b": [1, RET_H],
        "ffn_w_gate": [D, FF], "ffn_w_up": [D, FF], "ffn_conv_w": [3, FF], "ffn_conv_b": [1, FF], "ffn_w_down": [FF, D],
    }
    GLOBAL = ("x", "c", "ctx", "c_ctx", "final_norm_g")

    def W(self, name, idx=None):
        key = name if name in self.GLOBAL else "%s_%d" % (name, idx)
        if key not in self.din:
            self.din[key] = self.nc.dram_tensor(key, list(self.SHAPES[name]), F32, kind="ExternalInput").ap()
        return self.din[key]

    def declare(self):
        nc = self.nc
        self.out = nc.dram_tensor("out", [SEQ, D], F32, kind="ExternalOutput").ap()
        self.res = nc.dram_tensor("res", [NTOK, D], F32).ap()
        self.modrows = nc.dram_tensor("modrows", [DEPTH, 2, NMOD * D], F32).ap()
        self.h2T = nc.dram_tensor("h2T_scr", [128, KD, NTOK], BF16).ap()
        if self.cfg.get("dbg_ctx"):
            self.dbg_ctx = nc.dram_tensor("dbg_ctx", [CTX, D], F32, kind="ExternalOutput").ap()

    def sb(self, name, shape, dt):
        return self.mem.alloc(name, shape, dt)

    def setup_consts(self):
        S = self.S
        self.ident = self.sb("ident", [128, 128], BF16)
        self.identf = self.sb("identf", [128, 128], F32)
        idb = Buf("ident")
        for t in (self.ident, self.identf):
            S.op("pool", lambda e, t=t: e.memset(t[:], 1.0), writes=[idb])
            S.op("pool", lambda e, t=t: e.affine_select(out=t[:], in_=t[:], pattern=[[-1, 128]], compare_op=ALU.is_equal,
                                                        fill=0.0, base=0, channel_multiplier=1), reads=[idb], writes=[idb])
        self.persist = self.mem.mark()

    def psum_setup(self):
        nc = self.nc
        self.pf = [nc.alloc_psum_tensor("pf%d" % i, [128, 512], F32) for i in range(8)]
        self.pfb = [Buf("pf%d" % i, excl=True) for i in range(8)]
        self.pb = [self.pf[6][:].bitcast(BF16), self.pf[7][:].bitcast(BF16)]
        self.pbb = [self.pfb[6], self.pfb[7]]

    def ffn_convert_start(self, l):
        nc = self.nc
        if not hasattr(self, "wbf"):
            self.wbf = {
                "ffn_w_gate": nc.dram_tensor("gate_bf", [D, FF], BF16).ap(),
                "ffn_w_up": nc.dram_tensor("up_bf", [D, FF], BF16).ap(),
                "ffn_w_down": nc.dram_tensor("down_bf", [FF, D], BF16).ap(),
            }
            self.wbf_buf = {k: Buf(k) for k in self.wbf}
            self.bg = []
        for name in ("ffn_w_gate", "ffn_w_up", "ffn_w_down"):
            src = self.W(name, l)
            dst = self.wbf[name]
            rows = src.shape[0]
            step = rows // 4
            for p in range(4):
                def piece(src=src, dst=dst, name=name, r0=p * step, r1=(p + 1) * step):
                    self.S.dma("pool", lambda e: e.dma_start(out=dst[r0:r1, :].rearrange("r (a b) -> (r a) b", b=512),
                                                             in_=src[r0:r1, :].rearrange("r (a b) -> (r a) b", b=512)),
                               reads=[], writes=[self.wbf_buf[name]])
                self.bg.append(piece)

    def bg_step(self, n=1):
        for _ in range(n):
            if getattr(self, "bg", None):
                self.bg.pop(0)()

    def bg_flush(self):
        while getattr(self, "bg", None):
            self.bg.pop(0)()

    def phase_begin(self):
        self.S.barrier()
        self.mem.release(self.persist)

    def init_res(self):
        S = self.S
        b = Buf("res_init")
        S.dma("sp", lambda e: e.dma_start(out=self.res[0:CTX, :], in_=self.W("ctx")), writes=[b])
        for i in range(4):
            S.dma("sp", lambda e, i=i: e.dma_start(out=self.res[CTX + i * 1024:CTX + (i + 1) * 1024, :],
                                                   in_=self.W("x")[i * 1024:(i + 1) * 1024, :]), writes=[b])

    def mod_setup(self, layers):
        S, nc = self.S, self.nc
        self.phase_begin()
        craw = self.sb("craw", [128, KD, 2], F32)
        condT = self.sb("condT", [128, KD, 2], BF16)
        b_c = Buf("craw")
        with nc.allow_non_contiguous_dma(reason="tiny cond vectors"):
            S.dma("sp", lambda e: e.dma_start(out=craw[:, :, 0:1], in_=self.W("c").rearrange("o (k p) -> p k o", p=128), allow_slow_non_contiguous=True), writes=[b_c])
            S.dma("sp", lambda e: e.dma_start(out=craw[:, :, 1:2], in_=self.W("c_ctx").rearrange("o (k p) -> p k o", p=128), allow_slow_non_contiguous=True), writes=[b_c])
        b_ct = Buf("condT")
        S.op("act", lambda e: e.activation(out=condT[:], in_=craw[:], func=AF.Silu), reads=[b_c], writes=[b_ct])
        NB = NMOD * D // 512
        wts = [self.sb("modw%d" % i, [128, KD, 512], BF16) for i in range(3)]
        wbs = [Buf("modw%d" % i) for i in range(3)]
        rows = self.sb("modrow", [2, NMOD * D], F32)
        brow = self.sb("modbrow", [2, NMOD * D], F32)
        grow = self.sb("modgrow", [2, 2, D], F32)
        b_rows, b_brow, b_grow = Buf("rows"), Buf("brow"), Buf("grow")
        it = 0
        for l in layers:
            S.dma("sp", lambda e, l=l: e.dma_start(out=brow[:], in_=bcast_ap(self.W("mod_b", l), 2)), writes=[b_brow])
            S.dma("sp", lambda e, l=l: e.dma_start(out=grow[:, 0, :], in_=bcast_ap(self.W("norm_mix_g", l), 2)), writes=[b_grow])
            S.dma("sp", lambda e, l=l: e.dma_start(out=grow[:, 1, :], in_=bcast_ap(self.W("norm_ffn_g", l), 2)), writes=[b_grow])
            for nb in range(NB):
                w, wb = wts[it % 3], wbs[it % 3]
                pf, pfb = self.pf[it % 2], self.pfb[it % 2]
                it += 1
                S.dma("pool", lambda e, l=l, nb=nb, w=w: e.dma_start(
                    out=w[:], in_=self.W("mod_w", l)[:, nb * 512:(nb + 1) * 512].rearrange("(k p) n -> p k n", p=128)), writes=[wb])
                for k in range(KD):
                    S.op("pe", lambda e, k=k, w=w, pf=pf: e.matmul(pf[0:2, :], lhsT=condT[:, k, :], rhs=w[:, k, :], start=(k == 0), stop=(k == KD - 1)),
                         reads=[b_ct, wb], writes=[pfb], signal=(k == KD - 1))
                S.op("dve", lambda e, nb=nb, pf=pf: e.tensor_tensor(out=rows[:, nb * 512:(nb + 1) * 512], in0=pf[0:2, :],
                                                                      in1=brow[:, nb * 512:(nb + 1) * 512], op=ALU.add),
                     reads=[pfb, b_brow], writes=[b_rows])
            S.op("dve", lambda e: e.scalar_tensor_tensor(out=rows[:, D:2 * D], in0=rows[:, D:2 * D], scalar=1.0, in1=grow[:, 0, :],
                                                         op0=ALU.add, op1=ALU.mult), reads=[b_rows, b_grow], writes=[b_rows])
            S.op("dve", lambda e: e.scalar_tensor_tensor(out=rows[:, 4 * D:5 * D], in0=rows[:, 4 * D:5 * D], scalar=1.0, in1=grow[:, 1, :],
                                                         op0=ALU.add, op1=ALU.mult), reads=[b_rows, b_grow], writes=[b_rows])
            S.dma("sp", lambda e, l=l: e.dma_start(out=self.modrows[l], in_=rows[:]), reads=[b_rows], writes=[])

    def load_bc(self, name, l, who, seg, n=D, eng="sp"):
        t = self.sb(name, [128, n], F32)
        b = Buf(name)
        src = self.modrows[l, who:who + 1, seg * D:seg * D + n]
        self.S.dma(eng, lambda e: e.dma_start(out=t[:], in_=bcast_ap(src, 128)), writes=[b])
        return t, b

    def norm_mod_tile(self, xt, b_xt, gs, b_gs, sh, b_sh, hT_out, b_hT, tmp, pbi):
        S = self.S
        junk, b_junk = tmp["junk"]
        ss, b_ss = tmp["ss"]
        t1, b_t1 = tmp["t1"]
        h, b_h = tmp["h"]
        S.op("act", lambda e: e.activation(out=junk[:], in_=xt, func=AF.Square, accum_out=ss[:, 0:1]), reads=[b_xt], writes=[b_junk, b_ss])
        eps, b_eps = tmp["eps"]
        S.op("act", lambda e: e.activation(out=ss[:, 1:2], in_=ss[:, 0:1], func=AF.Sqrt, scale=1.0 / D, bias=eps[:, 0:1]), reads=[b_ss, b_eps], writes=[b_ss])
        S.op("dve", lambda e: e.reciprocal(out=ss[:, 2:3], in_=ss[:, 1:2]), reads=[b_ss], writes=[b_ss])
        S.op("dve", lambda e: e.scalar_tensor_tensor(out=t1[:], in0=xt, scalar=ss[:, 2:3], in1=gs[:], op0=ALU.mult, op1=ALU.mult),
             reads=[b_xt, b_ss, b_gs], writes=[b_t1])
        S.op("pool", lambda e: e.tensor_tensor(out=h[:], in0=t1[:], in1=sh[:], op=ALU.add), reads=[b_t1, b_sh], writes=[b_h])
        for half in range(2):
            pb, pbb = self.pb[(pbi + half) % 2], self.pbb[(pbi + half) % 2]
            for j in range(8):
                k = half * 8 + j
                S.op("pe", lambda e, k=k, j=j, pb=pb: e.transpose(out=pb[:, j * 128:(j + 1) * 128], in_=h[:, k * 128:(k + 1) * 128], identity=self.ident[:]),
                     reads=[b_h], writes=[pbb], signal=(j == 7))
            eng = "act" if half == 0 else "dve"
            if eng == "act":
                S.op("act", lambda e, half=half, pb=pb: e.copy(out=hT_out[:, half * 8:(half + 1) * 8, :], in_=pb[:].rearrange("p (k t) -> p k t", k=8)),
                     reads=[pbb], writes=[b_hT])
            else:
                S.op("dve", lambda e, half=half, pb=pb: e.tensor_copy(out=hT_out[:, half * 8:(half + 1) * 8, :], in_=pb[:].rearrange("p (k t) -> p k t", k=8)),
                     reads=[pbb], writes=[b_hT])

    def norm_tmp(self, nsets=2):
        eps = self.sb("eps", [128, 1], F32)
        b_eps = Buf("eps")
        self.S.op("pool", lambda e: e.memset(eps[:], EPS), writes=[b_eps])
        sets = []
        for i in range(nsets):
            sets.append({
                "junk": (self.sb("junk%d" % i, [128, D], BF16), Buf("junk")),
                "ss": (self.sb("ss%d" % i, [128, 4], F32), Buf("ss")),
                "t1": (self.sb("t1_%d" % i, [128, D], F32), Buf("t1")),
                "h": (self.sb("h%d" % i, [128, D], BF16), Buf("h")),
                "eps": (eps, b_eps),
            })
        if nsets == 1:
            sets.append(sets[0])
        return sets

    def ffn_f1(self, l, skip_ctx=False):
        self.make_hT(l, 4, 3, skip_ctx)

    def make_hT(self, l, seg_gs, seg_sh, skip_ctx=False):
        S = self.S
        self.phase_begin()
        bcs = {}
        for who in (0, 1):
            bcs[who] = (self.load_bc("gs2_%d" % who, l, who, seg_gs), self.load_bc("sh2_%d" % who, l, who, seg_sh))
        tmps = self.norm_tmp()
        xts = [(self.sb("xt%d" % i, [128, D], F32), Buf("xt")) for i in range(2)]
        stg = [(self.sb("stg%d" % i, [128, KD, 512], BF16), Buf("stg")) for i in range(2)]
        t0 = 2 if skip_ctx else 0
        groups = ([] if skip_ctx else [[0, 1]]) + [list(range(2 + 4 * g, 6 + 4 * g)) for g in range(8)]
        gi = 0
        for grp in groups:
            st, b_st = stg[gi % 2]
            for j, t in enumerate(grp):
                xt, b_xt = xts[t % 2]
                who = 1 if t < 2 else 0
                S.dma("sp", lambda e, t=t, xt=xt: e.dma_start(out=xt[:], in_=self.res[t * 128:(t + 1) * 128, :]), writes=[b_xt])
                (gs, b_gs), (sh, b_sh) = bcs[who]
                self.norm_mod_tile(xt[:], b_xt, gs, b_gs, sh, b_sh, st[:, :, j * 128:(j + 1) * 128], b_st, tmps[t % 2], 0)
            n = len(grp) * 128
            c0 = grp[0] * 128
            S.dma("sp", lambda e, st=st, n=n, c0=c0: e.dma_start(out=self.h2T[:, :, c0:c0 + n], in_=st[:, :, 0:n]), reads=[b_st], writes=[])
            gi += 1

    def ffn_f2(self, l, skip_ctx=False):
        S, nc = self.S, self.nc
        if not hasattr(self, "wbf") or self.cfg.get("convert_in_f2"):
            self.ffn_convert_start(l)
        self.bg_flush()
        self.phase_begin()
        cw = self.sb("cw", [128, NFC, 3], F32)
        cb = self.sb("cb", [128, NFC], F32)
        b_cw = Buf("cw")
        crow = self.sb("crow", [NFC, 4, 128], F32)
        b_crow = Buf("crow")
        for j in range(3):
            S.dma("sp", lambda e, j=j: e.dma_start(out=crow[:, j, :], in_=self.W("ffn_conv_w", l)[j:j + 1, :].rearrange("o (c p) -> (o c) p", p=128)), writes=[b_crow])
        S.dma("sp", lambda e: e.dma_start(out=crow[:, 3, :], in_=self.W("ffn_conv_b", l).rearrange("o (c p) -> (o c) p", p=128)), writes=[b_crow])
        for j in range(4):
            S.op("pe", lambda e, j=j: e.transpose(out=self.pf[7][:, j * NFC:(j + 1) * NFC], in_=crow[:, j, :], identity=self.identf[0:NFC, 0:NFC]),
                 reads=[b_crow], writes=[self.pfb[7]])
        S.op("dve", lambda e: e.tensor_copy(out=cw[:].rearrange("p c j -> p j c"), in_=self.pf[7][:, 0:3 * NFC].rearrange("p (j c) -> p j c", j=3)), reads=[self.pfb[7]], writes=[b_cw])
        S.op("dve", lambda e: e.tensor_copy(out=cb[:], in_=self.pf[7][:, 3 * NFC:4 * NFC]), reads=[self.pfb[7]], writes=[b_cw])
        g2t = self.sb("g2t", [128, D], F32)
        b_g2 = Buf("g2t")
        g2_who = [None]
        hblk = self.sb("hblk", [128, KD, 514], BF16)
        b_hblk = Buf("hblk")
        aT = self.sb("aT", [128, NFC, 512], BF16)
        b_aT = [Buf("aT%d" % i) for i in range(NFC)]
        wg = [(self.sb("wg%d" % i, [128, KD, 512], BF16), Buf("wg")) for i in range(2)]
        wu = [(self.sb("wu%d" % i, [128, KD, 512], BF16), Buf("wu")) for i in range(2)]
        wd = [(self.sb("wd%d" % i, [128, NFC, 256], BF16), Buf("wd")) for i in range(2)]
        gsb = [(self.sb("gsb%d" % i, [128, 514], F32), Buf("gsb")) for i in range(2)]
        c1 = [(self.sb("c1_%d" % i, [128, 512], F32), Buf("c1")) for i in range(2)]
        c2 = [(self.sb("c2_%d" % i, [128, 512], F32), Buf("c2")) for i in range(2)]
        sg = [(self.sb("sg%d" % i, [128, 512], F32), Buf("sg")) for i in range(2)]
        xp = [(self.sb("xp%d" % i, [128, 256], F32), Buf("xp")) for i in range(3)]
        yp = [(self.sb("yp%d" % i, [128, 256], F32), Buf("yp")) for i in range(3)]
        tp = [(self.sb("tp%d" % i, [128, 256], F32), Buf("tp")) for i in range(2)]
        pg, pu, ph, pd = (self.pf[0], self.pf[1]), (self.pf[2], self.pf[3]), self.pf[4], (self.pf[5], self.pf[6])
        b_pg, b_pu, b_ph, b_pd = (self.pfb[0], self.pfb[1]), (self.pfb[2], self.pfb[3]), self.pfb[4], (self.pfb[5], self.pfb[6])
        blocks = [] if skip_ctx else [(0, 256, True, True, 1)]
        for i in range(8):
            blocks.append((CTX + i * 512, 512, i == 0, i == 7, 0))
        it = 0
        pit = 0
        for (s0, n, lpad, rpad, who) in blocks:
            lo = s0 - (0 if lpad else 1)
            hi = s0 + n + (0 if rpad else 1)
            dst0 = 1 - (s0 - lo)
            S.dma("sp", lambda e, lo=lo, hi=hi, dst0=dst0: e.dma_start(out=hblk[:, :, dst0:dst0 + hi - lo], in_=self.h2T[:, :, lo:hi]), writes=[b_hblk])
            if lpad:
                S.op("dve", lambda e: e.memset(hblk[:, :, 0:1], 0.0), writes=[b_hblk])
            if rpad:
                S.op("dve", lambda e, n=n: e.memset(hblk[:, :, n + 1:n + 2], 0.0), writes=[b_hblk])
            for fb in range(NFC // 4):
                (wgt, b_wg), (wut, b_wu) = wg[fb % 2], wu[fb % 2]
                S.dma("pool", lambda e, fb=fb, wgt=wgt: e.dma_start(out=wgt[:], in_=self.wbf["ffn_w_gate"][:, fb * 512:(fb + 1) * 512].rearrange("(k p) n -> p k n", p=128)), reads=[self.wbf_buf["ffn_w_gate"]], writes=[b_wg])
                S.dma("pool", lambda e, fb=fb, wut=wut: e.dma_start(out=wut[:], in_=self.wbf["ffn_w_up"][:, fb * 512:(fb + 1) * 512].rearrange("(k p) n -> p k n", p=128)), reads=[self.wbf_buf["ffn_w_up"]], writes=[b_wu])
                for f4 in range(4):
                    fc = fb * 4 + f4
                    i2 = it % 2
                    it += 1
                    for k in range(KD):
                        S.op("pe", lambda e, k=k, f4=f4, wgt=wgt, i2=i2, n=n: e.matmul(pg[i2][:, 0:n], lhsT=wgt[:, k, f4 * 128:(f4 + 1) * 128], rhs=hblk[:, k, 1:n + 1],
                                                                                         start=(k == 0), stop=(k == KD - 1)),
                             reads=[b_wg, b_hblk], writes=[b_pg[i2]], signal=(k == KD - 1))
                    for k in range(KD):
                        S.op("pe", lambda e, k=k, f4=f4, wgt=wgt, n=n: e.matmul(ph[:, 0:2], lhsT=wgt[:, k, f4 * 128:(f4 + 1) * 128], rhs=hblk[:, k, 0:n + 2:n + 1],
                                                                                  start=(k == 0), stop=(k == KD - 1)),
                             reads=[b_wg, b_hblk], writes=[b_ph], signal=(k == KD - 1))
                    for k in range(KD):
                        S.op("pe", lambda e, k=k, f4=f4, wut=wut, i2=i2, n=n: e.matmul(pu[i2][:, 0:n], lhsT=wut[:, k, f4 * 128:(f4 + 1) * 128], rhs=hblk[:, k, 1:n + 1],
                                                                                         start=(k == 0), stop=(k == KD - 1)),
                             reads=[b_wu, b_hblk], writes=[b_pu[i2]], signal=(k == KD - 1))
                    (t1, b_t1), (t2, b_t2), (s, b_s) = c1[i2], c2[i2], sg[i2]
                    S.op("act", lambda e, t1=t1, i2=i2, fc=fc, n=n: e.activation(out=t1[:, 1:n], in_=pg[i2][:, 0:n - 1], func=AF.Copy, scale=cw[:, fc, 0:1]), reads=[b_pg[i2], b_cw], writes=[b_t1])
                    S.op("act", lambda e, t1=t1, fc=fc: e.activation(out=t1[:, 0:1], in_=ph[:, 0:1], func=AF.Copy, scale=cw[:, fc, 0:1]), reads=[b_ph, b_cw], writes=[b_t1])
                    S.op("dve", lambda e, t1=t1, t2=t2, i2=i2, fc=fc, n=n: e.scalar_tensor_tensor(out=t2[:, 0:n], in0=pg[i2][:, 0:n], scalar=cw[:, fc, 1:2], in1=t1[:, 0:n],
                                                                                           op0=ALU.mult, op1=ALU.add), reads=[b_pg[i2], b_t1, b_cw], writes=[b_t2])
                    S.op("dve", lambda e, t2=t2, i2=i2, fc=fc, n=n: e.scalar_tensor_tensor(out=t2[:, 0:n - 1], in0=pg[i2][:, 1:n], scalar=cw[:, fc, 2:3], in1=t2[:, 0:n - 1],
                                                                                    op0=ALU.mult, op1=ALU.add), reads=[b_pg[i2], b_cw], writes=[b_t2])
                    S.op("dve", lambda e, t2=t2, fc=fc, n=n: e.scalar_tensor_tensor(out=t2[:, n - 1:n], in0=ph[:, 1:2], scalar=cw[:, fc, 2:3], in1=t2[:, n - 1:n],
                                                                             op0=ALU.mult, op1=ALU.add), reads=[b_ph, b_cw], writes=[b_t2])
                    S.op("act", lambda e, t2=t2, s=s, fc=fc, n=n: e.activation(out=s[:, 0:n], in_=t2[:, 0:n], func=AF.Silu, bias=cb[:, fc:fc + 1]), reads=[b_t2, b_cw], writes=[b_s])
                    S.op("dve", lambda e, s=s, fc=fc, i2=i2, n=n: e.tensor_tensor(out=aT[:, fc, 0:n], in0=pu[i2][:, 0:n], in1=s[:, 0:n], op=ALU.mult),
                         reads=[b_pu[i2], b_s], writes=[b_aT[fc]])
            if g2_who[0] != who:
                g2_who[0] = who
                S.dma("sp", lambda e, who=who: e.dma_start(out=g2t[:], in_=bcast_ap(self.modrows[l, who:who + 1, 5 * D:6 * D], 128)), writes=[b_g2])
            for nb in range(D // 256):
                wdt, b_wd = wd[nb % 2]
                S.dma("pool", lambda e, nb=nb, wdt=wdt: e.dma_start(out=wdt[:], in_=self.wbf["ffn_w_down"][:, nb * 256:(nb + 1) * 256].rearrange("(c p) n -> p c n", p=128)), reads=[self.wbf_buf["ffn_w_down"]], writes=[b_wd])
                for tj in range(n // 128):
                    r0 = s0 + tj * 128
                    (xpt, b_xp), (ypt, b_yp), (tpt, b_tp) = xp[pit % 3], yp[pit % 3], tp[pit % 2]
                    pit += 1
                    S.dma("sp", lambda e, r0=r0, nb=nb, xpt=xpt: e.dma_start(out=xpt[:], in_=self.res[r0:r0 + 128, nb * 256:(nb + 1) * 256]), writes=[b_xp])
                    hp = pit % 2
                    for fc in range(NFC):
                        S.op("pe", lambda e, fc=fc, tj=tj, wdt=wdt, hp=hp: e.matmul(pd[hp][:, 0:256], lhsT=aT[:, fc, tj * 128:(tj + 1) * 128], rhs=wdt[:, fc, :],
                                                                                     start=(fc == 0), stop=(fc == NFC - 1)),
                             reads=[b_aT[fc], b_wd], writes=[b_pd[hp]], signal=(fc == NFC - 1))
                    S.op("dve", lambda e, tpt=tpt, nb=nb, hp=hp: e.tensor_tensor(out=tpt[:], in0=pd[hp][:, 0:256], in1=g2t[:, nb * 256:(nb + 1) * 256], op=ALU.mult),
                         reads=[b_pd[hp], b_g2], writes=[b_tp])
                    S.op("pool", lambda e, tpt=tpt, xpt=xpt, ypt=ypt: e.tensor_tensor(out=ypt[:], in0=tpt[:], in1=xpt[:], op=ALU.add), reads=[b_tp, b_xp], writes=[b_yp])
                    S.dma("sp", lambda e, r0=r0, nb=nb, ypt=ypt: e.dma_start(out=self.res[r0:r0 + 128, nb * 256:(nb + 1) * 256], in_=ypt[:]), reads=[b_yp], writes=[])

    def final_norm(self, plain_copy=False):
        S = self.S
        self.phase_begin()
        if self.cfg.get("dbg_ctx"):
            S.dma("sp", lambda e: e.dma_start(out=self.dbg_ctx, in_=self.res[0:CTX, :]), writes=[Buf("dbgc")])
        if plain_copy:
            b = Buf("o")
            for i in range(4):
                S.dma("sp", lambda e, i=i: e.dma_start(out=self.out[i * 1024:(i + 1) * 1024, :], in_=self.res[CTX + i * 1024:CTX + (i + 1) * 1024, :]), writes=[b])
            return
        g = self.sb("fng", [128, D], F32)
        b_g = Buf("fng")
        S.dma("sp", lambda e: e.dma_start(out=g[:], in_=bcast_ap(self.W("final_norm_g"), 128)), writes=[b_g])
        eps = self.sb("eps", [128, 1], F32)
        b_eps = Buf("eps")
        S.op("pool", lambda e: e.memset(eps[:], EPS), writes=[b_eps])
        xts = [(self.sb("xt%d" % i, [128, D], F32), Buf("xt")) for i in range(3)]
        ys = [(self.sb("y%d" % i, [128, D], F32), Buf("y")) for i in range(3)]
        junk = [(self.sb("junk%d" % i, [128, D], BF16), Buf("junk")) for i in range(2)]
        sss = [(self.sb("ss%d" % i, [128, 4], F32), Buf("ss")) for i in range(2)]
        for t in range(SEQ // 128):
            (xt, b_xt), (y, b_y), (jk, b_jk), (ss, b_ss) = xts[t % 3], ys[t % 3], junk[t % 2], sss[t % 2]
            S.dma("sp", lambda e, t=t, xt=xt: e.dma_start(out=xt[:], in_=self.res[CTX + t * 128:CTX + (t + 1) * 128, :]), writes=[b_xt])
            S.op("act", lambda e, xt=xt, jk=jk, ss=ss: e.activation(out=jk[:], in_=xt[:], func=AF.Square, accum_out=ss[:, 0:1]), reads=[b_xt], writes=[b_jk, b_ss])
            S.op("act", lambda e, ss=ss: e.activation(out=ss[:, 1:2], in_=ss[:, 0:1], func=AF.Sqrt, scale=1.0 / D, bias=eps[:, 0:1]), reads=[b_ss, b_eps], writes=[b_ss])
            S.op("dve", lambda e, ss=ss: e.reciprocal(out=ss[:, 2:3], in_=ss[:, 1:2]), reads=[b_ss], writes=[b_ss])
            S.op("dve", lambda e, xt=xt, y=y, ss=ss: e.scalar_tensor_tensor(out=y[:], in0=xt[:], scalar=ss[:, 2:3], in1=g[:], op0=ALU.mult, op1=ALU.mult),
                 reads=[b_xt, b_ss, b_g], writes=[b_y])
            S.dma("sp", lambda e, t=t, y=y: e.dma_start(out=self.out[t * 128:(t + 1) * 128, :], in_=y[:]), reads=[b_y], writes=[])

    def build(self):
        cfg = self.cfg
        self.declare()
        self.psum_setup()
        self.setup_consts()
        self.init_res()
        layers = cfg.get("layers", list(range(DEPTH)))
        if not cfg.get("no_mod"):
            self.mod_setup(sorted(set(l for l, _ in layers)) or cfg.get("mod_layers", []))
        for (l, what) in layers:
            last = (l == DEPTH - 1)
            if what in ("mix", "all"):
                if l % 2 == 0:
                    self.mla(l, last)
                else:
                    self.ret(l, last)
            if what in ("ffn", "all", "f1"):
                self.ffn_f1(l, skip_ctx=last)
            if what in ("ffn", "all", "f2"):
                self.ffn_f2(l, skip_ctx=last)
        self.final_norm(plain_copy=cfg.get("plain_out", False))
        self.S.barrier()
        self.S.emit(self.nc)
        return self.nc


_CACHE = {}


def get_program(cfg_key="full"):
    if cfg_key not in _CACHE:
        cfg = {"layers": [(l, "all") for l in range(DEPTH)]}
        k = K(cfg)
        nc = k.build()
        _CACHE[cfg_key] = (nc, list(k.din.keys()))
    return _CACHE[cfg_key]


def make_in_maps(inputs, names, ncores=8):
    f = lambda a: np.ascontiguousarray(np.asarray(a, dtype=np.float32))
    shared = {}
    for key in names:
        if key in ("x", "c", "ctx"):
            continue
        if key in ("c_ctx", "final_norm_g"):
            shared[key] = f(inputs[key]).reshape(1, D)
            continue
        name, idx = key.rsplit("_", 1)
        a = f(inputs[name][int(idx)])
        shared[key] = a.reshape(1, -1) if a.ndim == 1 else a
    maps = []
    for b in range(ncores):
        m = dict(shared)
        if "x" in names:
            m["x"] = f(inputs["x"][b])
        if "c" in names:
            m["c"] = f(inputs["c"][b]).reshape(1, D)
        if "ctx" in names:
            m["ctx"] = f(inputs["ctx"][b])
        maps.append(m)
    return maps


def kernel(**inputs):
    nc, names = get_program()
    maps = make_in_maps(inputs, names, 8)
    res = run_bass_kernel_spmd(nc, maps, core_ids=list(range(8)))
    return np.stack([np.asarray(r["out"], dtype=np.float32) for r in res.results], axis=0)


TWO_PI = 2.0 * math.pi
PI_LO = 3.1415925


def _gen_sincos(self, ang, b_ang, P, N, cos_out, sin_out, b_out, tA, tB, tI, b_t):
    S = self.S
    a, A, B, I = ang[0:P, 0:N], tA[0:P, 0:N], tB[0:P, 0:N], tI[0:P, 0:N]
    S.op("dve", lambda e: e.tensor_scalar(out=A, in0=a, scalar1=1.0 / TWO_PI, scalar2=None, op0=ALU.mult), reads=[b_ang], writes=[b_t])
    S.op("dve", lambda e: e.tensor_copy(out=I, in_=A), reads=[b_t], writes=[b_t])
    S.op("dve", lambda e: e.tensor_copy(out=A, in_=I), reads=[b_t], writes=[b_t])
    S.op("dve", lambda e: e.scalar_tensor_tensor(out=B, in0=A, scalar=-TWO_PI, in1=a, op0=ALU.mult, op1=ALU.add), reads=[b_t, b_ang], writes=[b_t])

    def wrap(X):
        S.op("dve", lambda e: e.tensor_scalar(out=A, in0=X, scalar1=math.pi, scalar2=None, op0=ALU.is_gt), reads=[b_t], writes=[b_t])
        S.op("dve", lambda e: e.scalar_tensor_tensor(out=X, in0=A, scalar=-TWO_PI, in1=X, op0=ALU.mult, op1=ALU.add), reads=[b_t], writes=[b_t])
        S.op("dve", lambda e: e.tensor_scalar(out=A, in0=X, scalar1=-math.pi, scalar2=None, op0=ALU.is_lt), reads=[b_t], writes=[b_t])
        S.op("dve", lambda e: e.scalar_tensor_tensor(out=X, in0=A, scalar=TWO_PI, in1=X, op0=ALU.mult, op1=ALU.add), reads=[b_t], writes=[b_t])
        S.op("dve", lambda e: e.tensor_scalar(out=X, in0=X, scalar1=PI_LO, scalar2=-PI_LO, op0=ALU.min, op1=ALU.max), reads=[b_t], writes=[b_t])

    wrap(B)
    S.op("act", lambda e: e.activation(out=sin_out, in_=B, func=AF.Sin), reads=[b_t], writes=[b_out])
    S.op("dve", lambda e: e.tensor_scalar(out=B, in0=B, scalar1=math.pi / 2, scalar2=None, op0=ALU.add), reads=[b_t, b_out], writes=[b_t])
    wrap(B)
    S.op("act", lambda e: e.activation(out=cos_out, in_=B, func=AF.Sin), reads=[b_t], writes=[b_out])


K.gen_sincos = _gen_sincos


def _mla_tables(self):
    S = self.S
    cosT = self.sb("cosT", [64, NTOK], BF16)
    sinT = self.sb("sinT", [64, NTOK], BF16)
    b_tab = Buf("tab")
    self.mla_keep = self.mem.mark()
    m = self.mem.mark()
    pidx = self.sb("pidx", [64, 1], I32)
    inv = self.sb("inv", [64, 2], F32)
    posi = self.sb("posi", [64, SEQ], I32)
    ang = self.sb("ang", [64, SEQ], F32)
    tA = self.sb("tA", [64, SEQ], F32)
    tB = self.sb("tB", [64, SEQ], F32)
    tI = self.sb("tI", [64, SEQ], I32)
    b_i, b_ang = Buf("inv"), Buf("ang")
    b_t = b_ang
    S.op("pool", lambda e: e.iota(pidx[:], pattern=[[0, 1]], base=0, channel_multiplier=1), writes=[b_i])
    pfl = self.sb("pfl", [64, 2], F32)
    S.op("dve", lambda e: e.tensor_copy(out=pfl[:, 0:1], in_=pidx[:]), reads=[b_i], writes=[b_i])
    S.op("dve", lambda e: e.tensor_copy(out=inv[:, 0:1], in_=pidx[:]), reads=[b_i], writes=[b_i])
    for thr in (16.0, 32.0, 48.0):
        S.op("dve", lambda e, thr=thr: e.tensor_scalar(out=pfl[:, 1:2], in0=pfl[:, 0:1], scalar1=thr, scalar2=None, op0=ALU.is_ge), reads=[b_i], writes=[b_i])
        S.op("dve", lambda e: e.scalar_tensor_tensor(out=inv[:, 0:1], in0=pfl[:, 1:2], scalar=-16.0, in1=inv[:, 0:1], op0=ALU.mult, op1=ALU.add), reads=[b_i], writes=[b_i])
    S.op("act", lambda e: e.activation(out=inv[:, 1:2], in_=inv[:, 0:1], func=AF.Exp, scale=-math.log(10000.0) / 16.0), reads=[b_i], writes=[b_i])
    S.op("pool", lambda e: e.iota(posi[:, :], pattern=[[1, SEQ]], base=0, channel_multiplier=0), writes=[b_ang])
    S.op("dve", lambda e: e.tensor_copy(out=tA[:], in_=posi[:]), reads=[b_ang], writes=[b_ang])
    S.op("dve", lambda e: e.tensor_scalar(out=tB[:], in0=tA[:], scalar1=-31.5, scalar2=1.0 / 64.0, op0=ALU.add, op1=ALU.mult), reads=[b_ang], writes=[b_ang])
    S.op("dve", lambda e: e.tensor_copy(out=tI[:], in_=tB[:]), reads=[b_ang], writes=[b_ang])
    S.op("dve", lambda e: e.tensor_copy(out=tB[:], in_=tI[:]), reads=[b_ang], writes=[b_ang])
    S.op("dve", lambda e: e.tensor_scalar(out=ang[0:32, :], in0=tB[0:32, :], scalar1=inv[0:32, 1:2], scalar2=None, op0=ALU.mult), reads=[b_ang, b_i], writes=[b_ang])
    S.op("dve", lambda e: e.scalar_tensor_tensor(out=tA[32:64, :], in0=tB[32:64, :], scalar=-64.0, in1=tA[32:64, :], op0=ALU.mult, op1=ALU.add), reads=[b_ang], writes=[b_ang])
    S.op("dve", lambda e: e.tensor_scalar(out=ang[32:64, :], in0=tA[32:64, :], scalar1=inv[32:64, 1:2], scalar2=None, op0=ALU.mult), reads=[b_ang, b_i], writes=[b_ang])
    self.gen_sincos(ang, b_ang, 64, SEQ, cosT[:, CTX:NTOK], sinT[:, CTX:NTOK], b_tab, tA, tB, tI, b_t)
    S.op("dve", lambda e: e.memset(cosT[:, 0:CTX], 1.0), writes=[b_tab])
    S.op("dve", lambda e: e.memset(sinT[:, 0:CTX], 0.0), writes=[b_tab])
    self.mem.release(m)
    return cosT, sinT


K.mla_tables = _mla_tables


def _rot_weights(self, eng_ops, wsrc4, wdst4, reads, writes):
    S = self.S
    for (dq, sq, sgn) in ((0, 1, -1.0), (1, 0, 1.0), (2, 3, -1.0), (3, 2, 1.0)):
        S.op("dve", lambda e, dq=dq, sq=sq, sgn=sgn: e.tensor_scalar(out=wdst4[:, :, dq, :], in0=wsrc4[:, :, sq, :], scalar1=sgn, scalar2=None, op0=ALU.mult),
             reads=reads, writes=writes)


K.rot_weights = _rot_weights


def _mla(self, l, last):
    S, nc = self.S, self.nc
    i = l // 2
    SCALE = 192.0 ** -0.5
    if not self.cfg.get("no_bg"):
        self.ffn_convert_start(l)
    self.phase_begin()
    cosT, sinT = self.mla_tables()
    cqnT = self.sb("cqnT", [128, 4, NTOK], BF16)
    ckvT = self.sb("ckvT", [128, 4, NTOK], BF16)
    krT = self.sb("krT", [64, NTOK], BF16)
    keep = self.mem.mark()
    if not hasattr(self, "oT_scr"):
        self.oT_scr = nc.dram_tensor("oT_scr", [MLA_H, 128, NTOK], BF16).ap()

    if self.cfg.get("mla_stop") == "tab":
        return
    S.barrier()
    wdq = self.sb("wdq", [128, KD, QR], BF16)
    wdkv = self.sb("wdkv", [128, KD, QR + ROPE], BF16)
    wrot = self.sb("wrot", [128, KD, ROPE], BF16)
    b_wdq, b_wdkv, b_wrot = Buf("wdq"), Buf("wdkv"), Buf("wrot")
    for c4 in range(4):
        S.dma("pool", lambda e, c4=c4: e.dma_start(out=wdq[:, c4 * 4:(c4 + 1) * 4, :], in_=self.W("mla_w_dq", i)[c4 * 512:(c4 + 1) * 512, :].rearrange("(k p) n -> p k n", p=128)), writes=[b_wdq])
        S.dma("pool", lambda e, c4=c4: e.dma_start(out=wdkv[:, c4 * 4:(c4 + 1) * 4, :], in_=self.W("mla_w_dkv", i)[c4 * 512:(c4 + 1) * 512, :].rearrange("(k p) n -> p k n", p=128)), writes=[b_wdkv])
    SK = self.cfg.get("p1_skip", "")
    if "r" not in SK:
        self.rot_weights(None, wdkv[:, :, QR:QR + ROPE].rearrange("p k (q f) -> p k q f", q=4), wrot[:].rearrange("p k (q f) -> p k q f", q=4), [b_wdkv], [b_wrot])
    qg = self.sb("qg", [128, QR], F32)
    kvg = self.sb("kvg", [128, QR], F32)
    b_g = Buf("qkvg")
    S.dma("sp", lambda e: e.dma_start(out=qg[:], in_=bcast_ap(self.W("mla_q_norm_g", i), 128)), writes=[b_g])
    S.dma("sp", lambda e: e.dma_start(out=kvg[:], in_=bcast_ap(self.W("mla_kv_norm_g", i), 128)), writes=[b_g])
    gs = self.sb("gs1", [128, D], F32)
    sh = self.sb("sh1", [128, D], F32)
    b_gs, b_sh = Buf("gs1"), Buf("sh1")
    tmps = self.norm_tmp(1)
    eps, b_eps = tmps[0]["eps"]
    xts = [(self.sb("xt%d" % j, [128, D], F32), Buf("xt")) for j in range(2)]
    hTs = [(self.sb("hT%d" % j, [128, KD, 128], BF16), Buf("hT")) for j in range(2)]
    cqn = [(self.sb("cqn%d" % j, [128, 2, QR], BF16), Buf("cqn")) for j in range(2)]
    sq = [(self.sb("sq%d" % j, [128, 8], F32), Buf("sq")) for j in range(2)]
    jk2 = self.sb("jk2", [128, QR], BF16)
    b_jk2 = Buf("jk2")
    rt = [(self.sb("rt%d" % j, [64, 2, 128], F32), Buf("rt")) for j in range(2)]
    b_cq, b_kv, b_kr = Buf("cqnT"), Buf("ckvT"), Buf("krT")
    for t in range(NT):
        who = 1 if t < 2 else 0
        if t in (0, 2):
            S.dma("sp", lambda e, who=who: e.dma_start(out=gs[:], in_=bcast_ap(self.modrows[l, who:who + 1, 1 * D:2 * D], 128)), writes=[b_gs])
            S.dma("sp", lambda e, who=who: e.dma_start(out=sh[:], in_=bcast_ap(self.modrows[l, who:who + 1, 0:D], 128)), writes=[b_sh])
        xt, b_xt = xts[t % 2]
        hT, b_hT = hTs[t % 2]
        S.dma("sp", lambda e, t=t, xt=xt: e.dma_start(out=xt[:], in_=self.res[t * 128:(t + 1) * 128, :]), writes=[b_xt])
        self.norm_mod_tile(xt[:], b_xt, gs, b_gs, sh, b_sh, hT[:], b_hT, tmps[t % 2], 0)
        pq, b_pq = self.pf[t % 2], self.pfb[t % 2]
        pk, b_pk = self.pf[2 + t % 2], self.pfb[2 + t % 2]
        for k in range(KD):
            S.op("pe", lambda e, k=k, hT=hT, pq=pq: e.matmul(pq[:, :], lhsT=hT[:, k, :], rhs=wdq[:, k, :], start=(k == 0), stop=(k == KD - 1)),
                 reads=[b_hT, b_wdq], writes=[b_pq], signal=(k == KD - 1))
        for k in range(KD):
            S.op("pe", lambda e, k=k, hT=hT, pk=pk: e.matmul(pk[:, :], lhsT=hT[:, k, :], rhs=wdkv[:, k, 0:QR], start=(k == 0), stop=(k == KD - 1)),
                 reads=[b_hT, b_wdkv], writes=[b_pk], signal=(k == KD - 1))
        for k in range(KD if "k" not in SK else 0):
            S.op("pe", lambda e, k=k, hT=hT: e.matmul(self.pf[4][0:64, 0:128], lhsT=wdkv[:, k, QR:QR + ROPE], rhs=hT[:, k, :], start=(k == 0), stop=(k == KD - 1)),
                 reads=[b_hT, b_wdkv], writes=[self.pfb[4]], signal=(k == KD - 1))
        for k in range(KD if "k" not in SK else 0):
            S.op("pe", lambda e, k=k, hT=hT: e.matmul(self.pf[5][0:64, 0:128], lhsT=wrot[:, k, :], rhs=hT[:, k, :], start=(k == 0), stop=(k == KD - 1)),
                 reads=[b_hT, b_wrot], writes=[self.pfb[5]], signal=(k == KD - 1))
        (cn, b_cn), (s8, b_s8) = cqn[t % 2], sq[t % 2]
        if "n" in SK:
            continue
        for which, (pp, b_pp, gt) in enumerate(((pq, b_pq, qg), (pk, b_pk, kvg))):
            o = which * 4
            S.op("act", lambda e, pp=pp, s8=s8, o=o: e.activation(out=jk2[:], in_=pp[:, :], func=AF.Square, accum_out=s8[:, o:o + 1]), reads=[b_pp], writes=[b_jk2, b_s8])
            S.op("act", lambda e, s8=s8, o=o: e.activation(out=s8[:, o + 1:o + 2], in_=s8[:, o:o + 1], func=AF.Sqrt, scale=1.0 / QR, bias=eps[:, 0:1]), reads=[b_s8, b_eps], writes=[b_s8])
            S.op("dve", lambda e, s8=s8, o=o: e.reciprocal(out=s8[:, o + 2:o + 3], in_=s8[:, o + 1:o + 2]), reads=[b_s8], writes=[b_s8])
            S.op("dve", lambda e, pp=pp, s8=s8, o=o, gt=gt, cn=cn, which=which: e.scalar_tensor_tensor(out=cn[:, which, :], in0=pp[:, :], scalar=s8[:, o + 2:o + 3], in1=gt[:], op0=ALU.mult, op1=ALU.mult),
                 reads=[b_pp, b_s8, b_g], writes=[b_cn])
        pb, b_pb = self.pb[t % 2], self.pbb[t % 2]
        for j in range(8):
            S.op("pe", lambda e, j=j, cn=cn, pb=pb: e.transpose(out=pb[:, j * 128:(j + 1) * 128], in_=cn[:, j // 4, (j % 4) * 128:(j % 4 + 1) * 128], identity=self.ident[:]),
                 reads=[b_cn], writes=[b_pb], signal=(j == 7))
        S.op("act", lambda e, pb=pb, t=t: e.copy(out=cqnT[:, :, t * 128:(t + 1) * 128], in_=pb[:, 0:512].rearrange("p (k t) -> p k t", k=4)), reads=[b_pb], writes=[b_cq])
        S.op("dve", lambda e, pb=pb, t=t: e.tensor_copy(out=ckvT[:, :, t * 128:(t + 1) * 128], in_=pb[:, 512:1024].rearrange("p (k t) -> p k t", k=4)), reads=[b_pb], writes=[b_kv])
        if "k" in SK:
            continue
        r, b_r = rt[t % 2]
        S.op("dve", lambda e, r=r, t=t: e.tensor_tensor(out=r[:, 0, :], in0=self.pf[4][0:64, 0:128], in1=cosT[:, t * 128:(t + 1) * 128], op=ALU.mult), reads=[self.pfb[4]], writes=[b_r])
        S.op("dve", lambda e, r=r, t=t: e.tensor_tensor(out=r[:, 1, :], in0=self.pf[5][0:64, 0:128], in1=sinT[:, t * 128:(t + 1) * 128], op=ALU.mult), reads=[self.pfb[5]], writes=[b_r])
        S.op("dve", lambda e, r=r, t=t: e.tensor_tensor(out=krT[:, t * 128:(t + 1) * 128], in0=r[:, 0, :], in1=r[:, 1, :], op=ALU.add), reads=[b_r], writes=[b_kr])

    if self.cfg.get("mla_stop") == "p1":
        return
    S.barrier()
    self.mem.release(keep)
    NB = [(0, CTX)] + [(CTX + 512 * b, 512) for b in range(8)]
    hb = []
    for j in range(2):
        hb.append({
            "wuq": self.sb("wuq%d" % j, [128, 4, 192], BF16), "wukv": self.sb("wukv%d" % j, [128, 4, 256], BF16),
            "wqr": self.sb("wqr%d" % j, [128, 4, 64], BF16),
            "qn": self.sb("qn%d" % j, [128, NTOK], BF16), "qr": self.sb("qr%d" % j, [64, NTOK], BF16),
            "kn": self.sb("kn%d" % j, [128, NTOK], BF16), "v": self.sb("v%d" % j, [128, NT, 129], BF16),
            "oT": self.sb("oTh%d" % j, [128, NTOK], BF16),
            "b_w": Buf("w"), "b_wqr": Buf("wqr"), "b_qn": Buf("qn"), "b_qr": Buf("qr"), "b_kn": Buf("kn"), "b_v": Buf("v"), "b_oT": Buf("oT"),
        })
        S.op("dve", lambda e, v=hb[j]["v"]: e.memset(v[:, :, 128:129], 1.0), writes=[hb[j]["b_v"]])
    pts = [(self.sb("pt%d" % j, [128, 512], BF16), Buf("pt")) for j in range(4)]
    rq = [(self.sb("rq%d" % j, [64, 2, 512], F32), Buf("rq")) for j in range(2)]
    PREP = 7
    ones_b = self.sb("ones_b", [128, 128], BF16)
    b_z = Buf("z")
    S.op("dve", lambda e: e.memset(ones_b[:], 1.0), writes=[b_z])
    rcs = [(self.sb("rc%d" % j, [128, 512], F32), Buf("rc")) for j in range(2)]

    def load_head_w(h):
        d = hb[h % 2]
        S.dma("pool", lambda e: e.dma_start(out=d["wuq"][:], in_=self.W("mla_w_uq", i)[:, h * 192:(h + 1) * 192].rearrange("(k p) n -> p k n", p=128)), writes=[d["b_w"]])
        S.dma("pool", lambda e: e.dma_start(out=d["wukv"][:], in_=self.W("mla_w_ukv", i)[:, h * 256:(h + 1) * 256].rearrange("(k p) n -> p k n", p=128)), writes=[d["b_w"]])
        self.rot_weights(None, d["wuq"][:, :, 128:192].rearrange("p k (q f) -> p k q f", q=4), d["wqr"][:].rearrange("p k (q f) -> p k q f", q=4), [d["b_w"]], [d["b_wqr"]])

    def prep_groups(h):
        d = hb[h % 2]
        gl = []
        pp, b_pp = self.pf[PREP], self.pfb[PREP]
        rqi = [0]
        for (c0, n) in NB:
            def g_qn(c0=c0, n=n):
                for k in range(4):
                    S.op("pe", lambda e, k=k: e.matmul(pp[:, 0:n], lhsT=d["wuq"][:, k, 0:128], rhs=cqnT[:, k, c0:c0 + n], start=(k == 0), stop=(k == 3)),
                         reads=[d["b_w"]], writes=[b_pp], signal=(k == 3))
                S.op("dve", lambda e: e.tensor_copy(out=d["qn"][:, c0:c0 + n], in_=pp[:, 0:n]), reads=[b_pp], writes=[d["b_qn"]])
            def g_kn(c0=c0, n=n):
                for k in range(4):
                    S.op("pe", lambda e, k=k: e.matmul(pp[:, 0:n], lhsT=d["wukv"][:, k, 0:128], rhs=ckvT[:, k, c0:c0 + n], start=(k == 0), stop=(k == 3)),
                         reads=[d["b_w"]], writes=[b_pp], signal=(k == 3))
                S.op("dve", lambda e: e.tensor_copy(out=d["kn"][:, c0:c0 + n], in_=pp[:, 0:n]), reads=[b_pp], writes=[d["b_kn"]])
            def g_qr(c0=c0, n=n):
                r, b_r = rq[rqi[0] % 2]
                rqi[0] += 1
                for half, wt, bw, tab in ((0, d["wuq"], d["b_w"], cosT), (1, d["wqr"], d["b_wqr"], sinT)):
                    for k in range(4):
                        lhs = wt[:, k, 128:192] if half == 0 else wt[:, k, :]
                        S.op("pe", lambda e, k=k, lhs=lhs: e.matmul(pp[0:64, 0:n], lhsT=lhs, rhs=cqnT[:, k, c0:c0 + n], start=(k == 0), stop=(k == 3)),
                             reads=[bw], writes=[b_pp], signal=(k == 3))
                    S.op("dve", lambda e, half=half, tab=tab: e.tensor_tensor(out=r[:, half, 0:n], in0=pp[0:64, 0:n], in1=tab[:, c0:c0 + n], op=ALU.mult), reads=[b_pp], writes=[b_r])
                S.op("dve", lambda e: e.tensor_tensor(out=d["qr"][:, c0:c0 + n], in0=r[:, 0, 0:n], in1=r[:, 1, 0:n], op=ALU.add), reads=[b_r], writes=[d["b_qr"]])
            gl += [g_qn, g_kn, g_qr]
        for t0 in range(0, NT, 4):
            nt = min(4, NT - t0)
            def g_v(t0=t0, nt=nt):
                for tt in range(nt):
                    t = t0 + tt
                    for k in range(4):
                        S.op("pe", lambda e, k=k, t=t, tt=tt: e.matmul(pp[:, tt * 128:(tt + 1) * 128], lhsT=ckvT[:, k, t * 128:(t + 1) * 128], rhs=d["wukv"][:, k, 128:256],
                                                                    start=(k == 0), stop=(k == 3)),
                             reads=[d["b_w"]], writes=[b_pp], signal=(k == 3 and tt == nt - 1))
                S.op("dve", lambda e: e.tensor_copy(out=d["v"][:, t0:t0 + nt, 0:128], in_=pp[:, 0:nt * 128].rearrange("p (t c) -> p t c", c=128)), reads=[b_pp], writes=[d["b_v"]])
            gl.append(g_v)
        return gl

    load_head_w(0)
    for g in prep_groups(0):
        g()
    qblocks = ([] if last else [(0, CTX, [0, 1])]) + [(CTX + 512 * b, 512, list(range(NT))) for b in range(8)]
    sit = 0
    for h in range(MLA_H):
        d = hb[h % 2]
        pend = []
        if h + 1 < MLA_H:
            load_head_w(h + 1)
            pend = prep_groups(h + 1)
        self.bg_step()
        iters = [(qi, q0, nq, kt, kts) for qi, (q0, nq, kts) in enumerate(qblocks) for kt in kts]
        n_it = len(iters)
        every = max(1, n_it // (len(pend) + 1)) if pend else 0

        def emit_S(it, d=d, iters=iters):
            qi, q0, nq, kt, kts = iters[it]
            psb = (0, 1, 6)[it % 3]
            ps, b_ps = self.pf[psb], self.pfb[psb]
            S.op("pe", lambda e: e.matmul(ps[:, 0:nq], lhsT=d["kn"][:, kt * 128:(kt + 1) * 128], rhs=d["qn"][:, q0:q0 + nq], start=True, stop=False),
                 reads=[d["b_kn"], d["b_qn"]], writes=[b_ps], signal=False)
            S.op("pe", lambda e: e.matmul(ps[:, 0:nq], lhsT=krT[:, kt * 128:(kt + 1) * 128], rhs=d["qr"][:, q0:q0 + nq], start=False, stop=True),
                 reads=[d["b_qr"]], writes=[b_ps], signal=True)
            pt, b_pt = pts[it % 4]
            S.op("act", lambda e: e.activation(out=pt[:, 0:nq], in_=ps[:, 0:nq], func=AF.Exp, scale=SCALE), reads=[b_ps], writes=[b_pt])

        def emit_PV(it, d=d, iters=iters):
            qi, q0, nq, kt, kts = iters[it]
            pt, b_pt = pts[it % 4]
            po, b_po = self.pf[2 + (qi % 2)], self.pfb[2 + (qi % 2)]
            pden, b_pd = self.pf[4 + (qi % 2)], self.pfb[4 + (qi % 2)]
            S.op("pe", lambda e: e.matmul(po[:, 0:nq], lhsT=d["v"][:, kt, 0:128], rhs=pt[:, 0:nq], start=(kt == kts[0]), stop=(kt == kts[-1])),
                 reads=[b_pt, d["b_v"]], writes=[b_po], signal=(kt == kts[-1]))
            S.op("pe", lambda e: e.matmul(pden[:, 0:nq], lhsT=ones_b[:], rhs=pt[:, 0:nq], start=(kt == kts[0]), stop=(kt == kts[-1])),
                 reads=[b_pt, b_z], writes=[b_pd], signal=True)
            if kt == kts[-1]:
                rc, b_rc = rcs[qi % 2]
                S.op("dve", lambda e: e.reciprocal(out=rc[:, 0:nq], in_=pden[:, 0:nq]), reads=[b_pd], writes=[b_rc])
                S.op("dve", lambda e: e.tensor_tensor(out=d["oT"][:, q0:q0 + nq], in0=po[:, 0:nq], in1=rc[:, 0:nq], op=ALU.mult), reads=[b_po, b_rc], writes=[d["b_oT"]])

        emit_S(0)
        if n_it > 1:
            emit_S(1)
        for it in range(n_it):
            if it + 2 < n_it:
                emit_S(it + 2)
            emit_PV(it)
            if pend and (it % every == every - 1):
                pend.pop(0)()
        while pend:
            pend.pop(0)()
        c0 = CTX if last else 0
        S.dma("sp", lambda e, h=h, d=d, c0=c0: e.dma_start(out=self.oT_scr[h, :, c0:NTOK], in_=d["oT"][:, c0:NTOK]), reads=[d["b_oT"]], writes=[])

    if self.cfg.get("mla_stop") == "p2":
        return
    self.phase_begin()
    wo = [self.sb("wo%d" % j, [128, MLA_H, 512], BF16) for j in range(4)]
    b_wo = [Buf("wo") for j in range(4)]
    for nb in range(4):
        S.dma("pool", lambda e, nb=nb: e.dma_start(out=wo[nb][:], in_=self.W("mla_w_o", i)[:, nb * 512:(nb + 1) * 512].rearrange("(k p) n -> p k n", p=128)), writes=[b_wo[nb]])
    g1 = self.sb("g1", [128, D], F32)
    b_g1 = Buf("g1")
    ob = [(self.sb("ob%d" % j, [128, MLA_H, 512], BF16), Buf("ob")) for j in range(2)]
    xts = [(self.sb("xr%d" % j, [128, D], F32), Buf("xr")) for j in range(3)]
    tps = [(self.sb("tq%d" % j, [128, 512], F32), Buf("tq")) for j in range(2)]
    blocks = ([] if last else [(0, CTX, 1)]) + [(CTX + 512 * b, 512, 0) for b in range(8)]
    cur_who = None
    pit = 0
    for bi, (c0, n, who) in enumerate(blocks):
        if who != cur_who:
            cur_who = who
            S.dma("sp", lambda e, who=who: e.dma_start(out=g1[:], in_=bcast_ap(self.modrows[l, who:who + 1, 2 * D:3 * D], 128)), writes=[b_g1])
        o, b_o = ob[bi % 2]
        S.dma("sp", lambda e, o=o, c0=c0, n=n: e.dma_start(out=o[:, :, 0:n], in_=self.oT_scr[:, :, c0:c0 + n].rearrange("h p t -> p h t")), writes=[b_o])
        for tj in range(n // 128):
            xt, b_xt = xts[pit % 3]
            r0 = c0 + tj * 128
            S.dma("sp", lambda e, xt=xt, r0=r0: e.dma_start(out=xt[:], in_=self.res[r0:r0 + 128, :]), writes=[b_xt])
            for nb in range(4):
                pp, b_pp = self.pf[pit % 4], self.pfb[pit % 4]
                tq, b_tq = tps[pit % 2]
                pit += 1
                for hh in range(MLA_H):
                    S.op("pe", lambda e, hh=hh, o=o, tj=tj, nb=nb, pp=pp: e.matmul(pp[:, :], lhsT=o[:, hh, tj * 128:(tj + 1) * 128], rhs=wo[nb][:, hh, :], start=(hh == 0), stop=(hh == MLA_H - 1)),
                         reads=[b_o, b_wo[nb]], writes=[b_pp], signal=(hh == MLA_H - 1))
                S.op("dve", lambda e, pp=pp, tq=tq, nb=nb: e.tensor_tensor(out=tq[:], in0=pp[:, :], in1=g1[:, nb * 512:(nb + 1) * 512], op=ALU.mult), reads=[b_pp, b_g1], writes=[b_tq])
                S.op("pool", lambda e, tq=tq, xt=xt, nb=nb: e.tensor_tensor(out=xt[:, nb * 512:(nb + 1) * 512], in0=tq[:], in1=xt[:, nb * 512:(nb + 1) * 512], op=ALU.add), reads=[b_tq, b_xt], writes=[b_xt])
            S.dma("sp", lambda e, xt=xt, r0=r0: e.dma_start(out=self.res[r0:r0 + 128, :], in_=xt[:]), reads=[b_xt], writes=[])


K.mla = _mla


def _ret_tables(self):
    S = self.S
    cosR = self.sb("cosR", [128, NTOK], BF16)
    sinR = self.sb("sinR", [128, NTOK], BF16)
    b_tab = Buf("tabR")
    m = self.mem.mark()
    pidx = self.sb("pidx", [128, 1], I32)
    inv = self.sb("inv", [128, 2], F32)
    posi = self.sb("posi", [128, SEQ], I32)
    ang = self.sb("ang", [128, SEQ], F32)
    tA = self.sb("tA", [128, SEQ], F32)
    tB = self.sb("tB", [128, SEQ], F32)
    tI = self.sb("tI", [128, SEQ], I32)
    b_i, b_ang = Buf("inv"), Buf("ang")
    S.op("pool", lambda e: e.iota(pidx[:], pattern=[[0, 1]], base=0, channel_multiplier=1), writes=[b_i])
    S.op("dve", lambda e: e.tensor_copy(out=inv[:, 0:1], in_=pidx[:]), reads=[b_i], writes=[b_i])
    S.op("act", lambda e: e.activation(out=inv[:, 1:2], in_=inv[:, 0:1], func=AF.Exp, scale=-math.log(10000.0) / 128.0), reads=[b_i], writes=[b_i])
    S.op("pool", lambda e: e.iota(posi[:, :], pattern=[[1, SEQ]], base=0, channel_multiplier=0), writes=[b_ang])
    S.op("dve", lambda e: e.tensor_copy(out=ang[:], in_=posi[:]), reads=[b_ang], writes=[b_ang])
    S.op("dve", lambda e: e.tensor_scalar(out=ang[:], in0=ang[:], scalar1=inv[:, 1:2], scalar2=None, op0=ALU.mult), reads=[b_ang, b_i], writes=[b_ang])
    self.gen_sincos(ang, b_ang, 128, SEQ, cosR[:, CTX:NTOK], sinR[:, CTX:NTOK], b_tab, tA, tB, tI, b_ang)
    S.op("dve", lambda e: e.memset(cosR[:, 0:CTX], 1.0), writes=[b_tab])
    S.op("dve", lambda e: e.memset(sinR[:, 0:CTX], 0.0), writes=[b_tab])
    self.mem.release(m)
    return cosR, sinR


K.ret_tables = _ret_tables


def _ret(self, l, last):
    S, nc = self.S, self.nc
    i = l // 2
    NBK = [(0, CTX)] + [(CTX + 512 * b, 512) for b in range(8)]
    if not hasattr(self, "v_scr"):
        self.v_scr = nc.dram_tensor("v_scr", [NTOK, 2 * D], BF16).ap()
        self.gf_scr = nc.dram_tensor("gf_scr", [NTOK, 2 * D], BF16).ap()
        self.gb_scr = nc.dram_tensor("gb_scr", [NTOK, 2 * D], BF16).ap()
        self.gated_scr = nc.dram_tensor("gated_scr", [NTOK, 2 * D], BF16).ap()
        self.qT_scr = nc.dram_tensor("qT_scr", [KD, 128, NTOK], BF16).ap()
        self.kT_scr = nc.dram_tensor("kT_scr", [KD, 128, NTOK], BF16).ap()
    if not self.cfg.get("no_bg"):
        self.ffn_convert_start(l)
    self.make_hT(l, 1, 0, skip_ctx=False)
    if self.cfg.get("ret_stop") == "r0":
        return

    self.phase_begin()
    hT = self.sb("hTall", [128, KD, NTOK], BF16)
    b_hT = Buf("hTall")
    for k4 in range(4):
        S.dma("sp", lambda e, k4=k4: e.dma_start(out=hT[:, k4 * 4:(k4 + 1) * 4, :], in_=self.h2T[:, k4 * 4:(k4 + 1) * 4, :]), writes=[b_hT])
    wbuf = [(self.sb("wb%d" % j, [128, KD, 512], BF16), Buf("wb")) for j in range(2)]
    stg = [(self.sb("stg%d" % j, [128, 512], BF16), Buf("stg")) for j in range(4)]
    it = 0
    pit = 0
    for (wname, dst, silu) in (("ret_w_v", self.v_scr, False), ("ret_w_gf", self.gf_scr, True), ("ret_w_gb", self.gb_scr, True)):
        for nb in range(8):
            w, b_w = wbuf[it % 2]
            it += 1
            S.dma("pool", lambda e, w=w, wname=wname, nb=nb: e.dma_start(out=w[:], in_=self.W(wname, i)[:, nb * 512:(nb + 1) * 512].rearrange("(k p) n -> p k n", p=128)), writes=[b_w])
            if it >= 2:
                self.bg_step()
            for t in range(NT):
                pp, b_pp = self.pf[pit % 4], self.pfb[pit % 4]
                st, b_st = stg[pit % 4]
                pit += 1
                for k in range(KD):
                    S.op("pe", lambda e, k=k, t=t, w=w, pp=pp: e.matmul(pp[:, :], lhsT=hT[:, k, t * 128:(t + 1) * 128], rhs=w[:, k, :], start=(k == 0), stop=(k == KD - 1)),
                         reads=[b_hT, b_w], writes=[b_pp], signal=(k == KD - 1))
                if silu:
                    S.op("act", lambda e, pp=pp, st=st: e.activation(out=st[:], in_=pp[:, :], func=AF.Silu), reads=[b_pp], writes=[b_st])
                else:
                    S.op("dve", lambda e, pp=pp, st=st: e.tensor_copy(out=st[:], in_=pp[:, :]), reads=[b_pp], writes=[b_st])
                S.dma("sp", lambda e, st=st, dst=dst, t=t, nb=nb: e.dma_start(out=dst[t * 128:(t + 1) * 128, nb * 512:(nb + 1) * 512], in_=st[:]), reads=[b_st], writes=[])
    if self.cfg.get("ret_stop") == "r1b":
        return

    self.phase_begin()
    cosR, sinR = self.ret_tables()
    S.barrier()
    w = self.sb("wqk", [128, KD, D], BF16)
    b_wqk = Buf("wqk")
    hbs = [(self.sb("hb%d" % j, [128, KD, 512], BF16), Buf("hb")) for j in range(2)]
    sgs = [(self.sb("sg%d" % j, [128, KD, 512], BF16), Buf("sg")) for j in range(2)]
    rts = [[(self.sb("rt%d_%d" % (j, q), [128, 512], F32), Buf("rt")) for q in range(4)] for j in range(2)]
    bi = 0
    hi = 0
    for (wname, dst) in (("ret_w_q", self.qT_scr), ("ret_w_k", self.kT_scr)):
        for c4 in range(4):
            S.dma("pool", lambda e, wname=wname, c4=c4: e.dma_start(out=w[:, :, c4 * 512:(c4 + 1) * 512], in_=self.W(wname, i)[:, c4 * 512:(c4 + 1) * 512].rearrange("(k p) n -> p k n", p=128)), writes=[b_wqk])
        for (c0, n) in NBK:
            hb, b_hb = hbs[bi % 2]
            sg, b_sg = sgs[bi % 2]
            bi += 1
            S.dma("sp", lambda e, hb=hb, c0=c0, n=n: e.dma_start(out=hb[:, :, 0:n], in_=self.h2T[:, :, c0:c0 + n]), writes=[b_hb])
            for h in range(RET_H):
                pA, b_pA = self.pf[(hi % 2) * 2], self.pfb[(hi % 2) * 2]
                pB, b_pB = self.pf[(hi % 2) * 2 + 1], self.pfb[(hi % 2) * 2 + 1]
                r = rts[hi % 2]
                hi += 1
                for (pp, b_pp, off) in ((pA, b_pA, 0), (pB, b_pB, 128)):
                    for k in range(KD):
                        S.op("pe", lambda e, k=k, pp=pp, hb=hb, h=h, off=off, n=n: e.matmul(pp[:, 0:n], lhsT=w[:, k, h * 256 + off:h * 256 + off + 128], rhs=hb[:, k, 0:n],
                                                                                              start=(k == 0), stop=(k == KD - 1)),
                             reads=[b_wqk, b_hb], writes=[b_pp], signal=(k == KD - 1))
                (t1, b1), (t2, b2), (t3, b3), (t4, b4) = r
                S.op("dve", lambda e, t1=t1, pA=pA, c0=c0, n=n: e.tensor_tensor(out=t1[:, 0:n], in0=pA[:, 0:n], in1=cosR[:, c0:c0 + n], op=ALU.mult), reads=[b_pA], writes=[b1])
                S.op("dve", lambda e, t2=t2, pB=pB, c0=c0, n=n: e.tensor_tensor(out=t2[:, 0:n], in0=pB[:, 0:n], in1=sinR[:, c0:c0 + n], op=ALU.mult), reads=[b_pB], writes=[b2])
                S.op("dve", lambda e, t3=t3, pB=pB, c0=c0, n=n: e.tensor_tensor(out=t3[:, 0:n], in0=pB[:, 0:n], in1=cosR[:, c0:c0 + n], op=ALU.mult), reads=[b_pB], writes=[b3])
                S.op("dve", lambda e, t4=t4, pA=pA, c0=c0, n=n: e.tensor_tensor(out=t4[:, 0:n], in0=pA[:, 0:n], in1=sinR[:, c0:c0 + n], op=ALU.mult), reads=[b_pA], writes=[b4])
                S.op("pool", lambda e, t1=t1, t2=t2, sg=sg, h=h, n=n: e.tensor_tensor(out=sg[:, 2 * h, 0:n], in0=t1[:, 0:n], in1=t2[:, 0:n], op=ALU.subtract), reads=[b1, b2], writes=[b_sg])
                S.op("pool", lambda e, t3=t3, t4=t4, sg=sg, h=h, n=n: e.tensor_tensor(out=sg[:, 2 * h + 1, 0:n], in0=t3[:, 0:n], in1=t4[:, 0:n], op=ALU.add), reads=[b3, b4], writes=[b_sg])
            S.dma("sp", lambda e, sg=sg, dst=dst, c0=c0, n=n: e.dma_start(out=dst[:, :, c0:c0 + n].rearrange("c p t -> p c t"), in_=sg[:, :, 0:n]), reads=[b_sg], writes=[])
    if self.cfg.get("ret_stop") == "r1a":
        return

    self.phase_begin()
    NCH = NT
    lg = self.sb("lg", [128, 16], F32)
    cd = self.sb("cd", [128, 16], F32)
    b_c = Buf("rc")
    S.dma("sp", lambda e: e.dma_start(out=lg[:, 0:8], in_=bcast_ap(self.W("ret_decay_f", i), 128)), writes=[b_c])
    S.dma("sp", lambda e: e.dma_start(out=lg[:, 8:16], in_=bcast_ap(self.W("ret_decay_b", i), 128)), writes=[b_c])
    S.op("act", lambda e: e.activation(out=lg[:], in_=lg[:], func=AF.Exp), reads=[b_c], writes=[b_c])
    S.op("dve", lambda e: e.tensor_scalar(out=lg[:], in0=lg[:], scalar1=-1.0, scalar2=None, op0=ALU.mult), reads=[b_c], writes=[b_c])
    S.op("act", lambda e: e.activation(out=cd[:], in_=lg[:], func=AF.Exp, scale=128.0), reads=[b_c], writes=[b_c])
    di = self.sb("di", [128, 128], I32)
    d1 = self.sb("d1", [128, 128], F32)
    ef = self.sb("ef", [128, 2, 128], F32)
    ind = self.sb("ind", [128, 2, 128], F32)
    xr = self.sb("xr", [128, 2, 128], F32)
    zc = self.sb("zc", [128, 2], F32)
    zi = self.sb("zi", [128, 2], I32)
    S.op("pool", lambda e: e.iota(di[:], pattern=[[1, 128]], base=0, channel_multiplier=-1), writes=[b_c])
    S.op("dve", lambda e: e.tensor_copy(out=d1[:], in_=di[:]), reads=[b_c], writes=[b_c])
    S.op("dve", lambda e: e.tensor_scalar(out=ef[:, 0, :], in0=d1[:], scalar1=0.0, scalar2=None, op0=ALU.max), reads=[b_c], writes=[b_c])
    S.op("dve", lambda e: e.tensor_scalar(out=ef[:, 1, :], in0=d1[:], scalar1=-1.0, scalar2=0.0, op0=ALU.mult, op1=ALU.max), reads=[b_c], writes=[b_c])
    S.op("dve", lambda e: e.tensor_scalar(out=ind[:, 0, :], in0=d1[:], scalar1=0.0, scalar2=1.0 / 16.0, op0=ALU.is_ge, op1=ALU.mult), reads=[b_c], writes=[b_c])
    S.op("dve", lambda e: e.tensor_scalar(out=ind[:, 1, :], in0=d1[:], scalar1=0.0, scalar2=1.0 / 16.0, op0=ALU.is_le, op1=ALU.mult), reads=[b_c], writes=[b_c])
    S.op("pool", lambda e: e.iota(di[:], pattern=[[1, 128]], base=1, channel_multiplier=0), reads=[b_c], writes=[b_c])
    S.op("dve", lambda e: e.tensor_copy(out=xr[:, 0, :], in_=di[:]), reads=[b_c], writes=[b_c])
    S.op("dve", lambda e: e.tensor_scalar(out=xr[:, 1, :], in0=xr[:, 0, :], scalar1=-1.0, scalar2=129.0, op0=ALU.mult, op1=ALU.add), reads=[b_c], writes=[b_c])
    S.op("pool", lambda e: e.iota(zi[:, 0:1], pattern=[[0, 1]], base=127, channel_multiplier=-1), reads=[b_c], writes=[b_c])
    S.op("pool", lambda e: e.iota(zi[:, 1:2], pattern=[[0, 1]], base=0, channel_multiplier=1), reads=[b_c], writes=[b_c])
    S.op("dve", lambda e: e.tensor_copy(out=zc[:], in_=zi[:]), reads=[b_c], writes=[b_c])
    mask = self.sb("mask", [128, 16, 128], F32)
    xi = self.sb("xi", [128, 16, 128], F32)
    zeta = self.sb("zeta", [128, 16], F32)
    tmpm = self.sb("tmpm", [128, 128], F32)
    for h in range(RET_H):
        for dr in range(2):
            col = dr * 8 + h
            hd = h * 2 + dr
            S.op("act", lambda e, dr=dr, col=col: e.activation(out=tmpm[:], in_=ef[:, dr, :], func=AF.Exp, scale=lg[:, col:col + 1]), reads=[b_c], writes=[b_c])
            S.op("dve", lambda e, dr=dr, hd=hd: e.tensor_tensor(out=mask[:, hd, :], in0=tmpm[:], in1=ind[:, dr, :], op=ALU.mult), reads=[b_c], writes=[b_c])
            S.op("act", lambda e, dr=dr, col=col, hd=hd: e.activation(out=xi[:, hd, :], in_=xr[:, dr, :], func=AF.Exp, scale=lg[:, col:col + 1]), reads=[b_c], writes=[b_c])
            S.op("act", lambda e, dr=dr, col=col, hd=hd: e.activation(out=zeta[:, hd:hd + 1], in_=zc[:, dr:dr + 1], func=AF.Exp, scale=lg[:, col:col + 1]), reads=[b_c], writes=[b_c])
    S.op("dve", lambda e: e.tensor_scalar(out=zeta[:], in0=zeta[:], scalar1=1.0 / 16.0, scalar2=None, op0=ALU.mult), reads=[b_c], writes=[b_c])
    epsg = self.sb("epsg", [128, 1], F32)
    S.op("dve", lambda e: e.memset(epsg[:], GN_EPS), reads=[b_c], writes=[b_c])
    S.barrier()
    qk = [{"q": self.sb("q%d" % j, [128, 2, NTOK], BF16), "k": self.sb("k%d" % j, [128, 2, NTOK], BF16), "b": Buf("qk")} for j in range(2)]
    vh = self.sb("vh", [128, NCH, 512], BF16)
    b_vh = Buf("vh")
    gated = self.sb("gated", [128, NCH, 512], BF16)
    b_gt = [Buf("gt%d" % c) for c in range(NCH)]
    S32 = [self.sb("S32_%d" % dr, [128, 2, 512], F32) for dr in range(2)]
    S16 = [[self.sb("S16_%d_%d" % (dr, p), [128, 2, 512], BF16) for p in range(2)] for dr in range(2)]
    b_S = [Buf("S0"), Buf("S1")]
    b_S16 = [[Buf("S16") for p in range(2)] for dr in range(2)]
    sTm = [[(self.sb("sTm%d_%d" % (dr, j), [128, 128], BF16), Buf("sTm")) for j in range(2)] for dr in range(2)]
    kz = [[(self.sb("kz%d_%d" % (dr, j), [128, 256], BF16), Buf("kz")) for j in range(2)] for dr in range(2)]
    qx = [[(self.sb("qx%d_%d" % (dr, j), [128, 2, 128], BF16), Buf("qx")) for j in range(2)] for dr in range(2)]
    gt_in = [[(self.sb("gi%d_%d" % (dr, j), [128, 512], BF16), Buf("gi")) for j in range(2)] for dr in range(2)]
    nrm = [[(self.sb("nr%d_%d" % (dr, j), [128, 512], BF16), Buf("nr")) for j in range(2)] for dr in range(2)]
    gtmp = [(self.sb("gtmp%d" % j, [128, 512], BF16), Buf("gtmp")) for j in range(2)]
    st6 = [[(self.sb("st%d_%d" % (dr, j), [128, 16], F32), Buf("st")) for j in range(2)] for dr in range(2)]

    def load_qk(h):
        d = qk[h % 2]
        S.dma("sp", lambda e: e.dma_start(out=d["q"][:], in_=self.qT_scr[2 * h:2 * h + 2, :, :].rearrange("c p t -> p c t")), writes=[d["b"]])
        S.dma("sp", lambda e: e.dma_start(out=d["k"][:], in_=self.kT_scr[2 * h:2 * h + 2, :, :].rearrange("c p t -> p c t")), writes=[d["b"]])

    order = [list(range(NCH)), [1, 0] + list(range(NCH - 1, 1, -1))]
    load_qk(0)
    cnt = [0, 0]
    for h in range(RET_H):
        d = qk[h % 2]
        if h + 1 < RET_H:
            load_qk(h + 1)
        S.dma("sp", lambda e, h=h: e.dma_start(out=vh[:], in_=self.v_scr[:, h * 512:(h + 1) * 512].rearrange("(c p) n -> p c n", p=128)), writes=[b_vh])
        arrived = [0] * NCH
        for step in range(NCH):
            for dr in range(2):
                c = order[dr][step]
                first = (step == 0)
                lastst = (step == NCH - 1)
                is_ctx = c < 2
                need_y = not (last and is_ctx)
                hd = h * 2 + dr
                j2 = cnt[dr] % 2
                cnt[dr] += 1
                par = step % 2
                cs = slice(c * 128, (c + 1) * 128)
                bA, bY, bU0, bU1 = dr * 4, dr * 4 + 1, dr * 4 + 2, dr * 4 + 3
                if need_y:
                    (sm, b_sm) = sTm[dr][j2]
                    S.op("pe", lambda e, d=d, cs=cs, bA=bA: e.matmul(self.pf[bA][:, 0:128], lhsT=d["k"][:, 0, cs], rhs=d["q"][:, 0, cs], start=True, stop=False), reads=[d["b"]], writes=[self.pfb[bA]], signal=False)
                    S.op("pe", lambda e, d=d, cs=cs, bA=bA: e.matmul(self.pf[bA][:, 0:128], lhsT=d["k"][:, 1, cs], rhs=d["q"][:, 1, cs], start=False, stop=True), reads=[d["b"]], writes=[self.pfb[bA]], signal=True)
                    S.op("dve", lambda e, sm=sm, bA=bA, hd=hd: e.tensor_tensor(out=sm[:], in0=self.pf[bA][:, 0:128], in1=mask[:, hd, :], op=ALU.mult), reads=[self.pfb[bA]], writes=[b_sm])
                    if not first:
                        (qxt, b_qx) = qx[dr][j2]
                        for a in range(2):
                            S.op("dve", lambda e, d=d, qxt=qxt, a=a, cs=cs, hd=hd: e.tensor_tensor(out=qxt[:, a, :], in0=d["q"][:, a, cs], in1=xi[:, hd, :], op=ALU.mult), reads=[d["b"]], writes=[b_qx])
                if not lastst:
                    (kzt, b_kz) = kz[dr][j2]
                    for a in range(2):
                        S.op("pe", lambda e, d=d, a=a, cs=cs, bA=bA: e.transpose(out=self.pf[bA][:].bitcast(BF16)[:, a * 128:(a + 1) * 128], in_=d["k"][:, a, cs], identity=self.ident[:]),
                             reads=[d["b"]], writes=[self.pfb[bA]], signal=(a == 1))
                    S.op("act", lambda e, kzt=kzt, bA=bA, hd=hd: e.activation(out=kzt[:], in_=self.pf[bA][:].bitcast(BF16)[:, 0:256], func=AF.Copy, scale=zeta[:, hd:hd + 1]), reads=[self.pfb[bA]], writes=[b_kz])
                    for a, bU in ((0, bU0), (1, bU1)):
                        S.op("pe", lambda e, kzt=kzt, a=a, bU=bU, c=c: e.matmul(self.pf[bU][:, :], lhsT=kzt[:, a * 128:(a + 1) * 128], rhs=vh[:, c, :], start=True, stop=True),
                             reads=[b_kz, b_vh], writes=[self.pfb[bU]], signal=True)
                        if first:
                            S.op("dve", lambda e, a=a, bU=bU, dr=dr: e.tensor_copy(out=S32[dr][:, a, :], in_=self.pf[bU][:, :]), reads=[self.pfb[bU]], writes=[b_S[dr]])
                        else:
                            S.op("dve", lambda e, a=a, bU=bU, dr=dr, hd=hd, h=h: e.scalar_tensor_tensor(out=S32[dr][:, a, :], in0=S32[dr][:, a, :], scalar=cd[:, dr * 8 + h:dr * 8 + h + 1], in1=self.pf[bU][:, :],
                                                                                                op0=ALU.mult, op1=ALU.add), reads=[self.pfb[bU]], writes=[b_S[dr]])
                        S.op("act", lambda e, a=a, dr=dr, par=par: e.copy(out=S16[dr][par][:, a, :], in_=S32[dr][:, a, :]), reads=[b_S[dr]], writes=[b_S16[dr][par]])
                if need_y:
                    S.op("pe", lambda e, sm=sm, c=c, bY=bY, first=first: e.matmul(self.pf[bY][:, :], lhsT=sm[:], rhs=vh[:, c, :], start=True, stop=first), reads=[b_sm, b_vh], writes=[self.pfb[bY]], signal=first)
                    if not first:
                        for a in range(2):
                            S.op("pe", lambda e, qxt=qxt, a=a, bY=bY, dr=dr, par=par: e.matmul(self.pf[bY][:, :], lhsT=qxt[:, a, :], rhs=S16[dr][1 - par][:, a, :], start=False, stop=(a == 1)),
                                 reads=[b_qx, b_S16[dr][1 - par]], writes=[self.pfb[bY]], signal=(a == 1))
                    (s6, b_s6) = st6[dr][j2]
                    S.op("dve", lambda e, s6=s6, bY=bY: e.bn_stats(out=s6[:, 0:6], in_=self.pf[bY][:, :]), reads=[self.pfb[bY]], writes=[b_s6])
                    S.op("dve", lambda e, s6=s6: e.bn_aggr(out=s6[:, 6:8], in_=s6[:, 0:6]), reads=[b_s6], writes=[b_s6])
                    S.op("act", lambda e, s6=s6: e.activation(out=s6[:, 8:9], in_=s6[:, 7:8], func=AF.Sqrt, bias=epsg[:, 0:1]), reads=[b_s6], writes=[b_s6])
                    S.op("dve", lambda e, s6=s6: e.reciprocal(out=s6[:, 9:10], in_=s6[:, 8:9]), reads=[b_s6], writes=[b_s6])
                    S.op("dve", lambda e, s6=s6: e.scalar_tensor_tensor(out=s6[:, 10:11], in0=s6[:, 6:7], scalar=-1.0, in1=s6[:, 9:10], op0=ALU.mult, op1=ALU.mult), reads=[b_s6], writes=[b_s6])
                    (nr, b_nr) = nrm[dr][j2]
                    S.op("act", lambda e, s6=s6, nr=nr, bY=bY: e.activation(out=nr[:], in_=self.pf[bY][:, :], func=AF.Identity, scale=s6[:, 9:10], bias=s6[:, 10:11]), reads=[self.pfb[bY], b_s6], writes=[b_nr])
                    (gi, b_gi) = gt_in[dr][j2]
                    gsrc = self.gf_scr if dr == 0 else self.gb_scr
                    S.dma("sp", lambda e, gi=gi, gsrc=gsrc, c=c, h=h: e.dma_start(out=gi[:], in_=gsrc[c * 128:(c + 1) * 128, h * 512:(h + 1) * 512]), writes=[b_gi])
                    if arrived[c] == 0:
                        S.op("pool", lambda e, gi=gi, nr=nr, c=c: e.tensor_tensor(out=gated[:, c, :], in0=gi[:], in1=nr[:], op=ALU.mult), reads=[b_gi, b_nr], writes=[b_gt[c]])
                    else:
                        (gm, b_gm) = gtmp[cnt[dr] % 2]
                        S.op("pool", lambda e, gi=gi, nr=nr, gm=gm: e.tensor_tensor(out=gm[:], in0=gi[:], in1=nr[:], op=ALU.mult), reads=[b_gi, b_nr], writes=[b_gm])
                        S.op("pool", lambda e, gm=gm, c=c: e.tensor_tensor(out=gated[:, c, :], in0=gated[:, c, :], in1=gm[:], op=ALU.add), reads=[b_gm], writes=[b_gt[c]])
                    arrived[c] += 1
        c_lo = 2 if last else 0
        S.dma("sp", lambda e, h=h, c_lo=c_lo: e.dma_start(out=self.gated_scr[c_lo * 128:NTOK, h * 512:(h + 1) * 512].rearrange("(c p) n -> p c n", p=128), in_=gated[:, c_lo:NCH, :]),
              reads=b_gt[c_lo:], writes=[])
    if self.cfg.get("ret_stop") == "r2":
        return

    self.phase_begin()
    NK = 2 * D // 128
    wo = self.sb("wo", [128, NK, D], BF16)
    b_wo = [Buf("wo%d" % j) for j in range(4)]
    for nb in range(4):
        S.dma("pool", lambda e, nb=nb: e.dma_start(out=wo[:, :, nb * 512:(nb + 1) * 512], in_=self.W("ret_w_o", i)[:, nb * 512:(nb + 1) * 512].rearrange("(k p) n -> p k n", p=128)), writes=[b_wo[nb]])
    g1 = self.sb("g1", [128, D], F32)
    b_g1 = Buf("g1")
    gts = [(self.sb("gt%d" % j, [128, 2 * D], BF16), Buf("gt")) for j in range(2)]
    gTs = [(self.sb("gT%d" % j, [128, NK, 128], BF16), Buf("gT")) for j in range(2)]
    xts = [(self.sb("xr%d" % j, [128, D], F32), Buf("xr")) for j in range(2)]
    tps = [(self.sb("tq%d" % j, [128, 512], F32), Buf("tq")) for j in range(2)]
    cur_who = None
    pit = 0
    for ti, t in enumerate(range(2 if last else 0, NT)):
        who = 1 if t < 2 else 0
        if who != cur_who:
            cur_who = who
            S.dma("sp", lambda e, who=who: e.dma_start(out=g1[:], in_=bcast_ap(self.modrows[l, who:who + 1, 2 * D:3 * D], 128)), writes=[b_g1])
        (gt, b_gtl), (gT, b_gT), (xt, b_xt) = gts[ti % 2], gTs[ti % 2], xts[ti % 2]
        S.dma("sp", lambda e, gt=gt, t=t: e.dma_start(out=gt[:], in_=self.gated_scr[t * 128:(t + 1) * 128, :]), writes=[b_gtl])
        S.dma("sp", lambda e, xt=xt, t=t: e.dma_start(out=xt[:], in_=self.res[t * 128:(t + 1) * 128, :]), writes=[b_xt])
        for q8 in range(4):
            pb, b_pb = (self.pf[4 + q8 % 2][:].bitcast(BF16), self.pfb[4 + q8 % 2])
            for j in range(8):
                c = q8 * 8 + j
                S.op("pe", lambda e, gt=gt, c=c, j=j, pb=pb: e.transpose(out=pb[:, j * 128:(j + 1) * 128], in_=gt[:, c * 128:(c + 1) * 128], identity=self.ident[:]),
                     reads=[b_gtl], writes=[b_pb], signal=(j == 7))
            if q8 % 2 == 0:
                S.op("act", lambda e, gT=gT, q8=q8, pb=pb: e.copy(out=gT[:, q8 * 8:(q8 + 1) * 8, :], in_=pb.rearrange("p (k t) -> p k t", k=8)), reads=[b_pb], writes=[b_gT])
            else:
                S.op("dve", lambda e, gT=gT, q8=q8, pb=pb: e.tensor_copy(out=gT[:, q8 * 8:(q8 + 1) * 8, :], in_=pb.rearrange("p (k t) -> p k t", k=8)), reads=[b_pb], writes=[b_gT])
        for nb in range(4):
            pp, b_pp = self.pf[pit % 4], self.pfb[pit % 4]
            tq, b_tq = tps[pit % 2]
            pit += 1
            for c in range(NK):
                S.op("pe", lambda e, c=c, gT=gT, nb=nb, pp=pp: e.matmul(pp[:, :], lhsT=gT[:, c, :], rhs=wo[:, c, nb * 512:(nb + 1) * 512], start=(c == 0), stop=(c == NK - 1)),
                     reads=[b_gT, b_wo[nb]], writes=[b_pp], signal=(c == NK - 1))
            S.op("dve", lambda e, pp=pp, tq=tq, nb=nb: e.tensor_tensor(out=tq[:], in0=pp[:, :], in1=g1[:, nb * 512:(nb + 1) * 512], op=ALU.mult), reads=[b_pp, b_g1], writes=[b_tq])
            S.op("pool", lambda e, tq=tq, xt=xt, nb=nb: e.tensor_tensor(out=xt[:, nb * 512:(nb + 1) * 512], in0=tq[:], in1=xt[:, nb * 512:(nb + 1) * 512], op=ALU.add), reads=[b_tq, b_xt], writes=[b_xt])
        S.dma("sp", lambda e, xt=xt, t=t: e.dma_start(out=self.res[t * 128:(t + 1) * 128, :], in_=xt[:]), reads=[b_xt], writes=[])


K.ret = _ret
```
